# Optimizing a Trainium2 kernel written in Bass

```python
import jax, jax.numpy as jnp
from jax import lax
import numpy as np

D_MODEL = 1024
BATCH = 4
SEQ = 4096
DEPTH = 4

CHUNK = 64
N_MEM = 256
D_FF = 2816
NORM_EPS = 1e-6
RWKV_HEADS = 8
RWKV_HEAD_DIM = 64
RWKV_WIDTH = RWKV_HEADS * RWKV_HEAD_DIM
DECAY_LORA = 64
AAA_LORA = 64
MV_LORA = 32
GATE_LORA = 128
RWKV_LN_EPS = 64e-5
GLA_HEADS = 4
GLA_DK = 64
GLA_DV = 128
GLA_KW = GLA_HEADS * GLA_DK
GLA_VW = GLA_HEADS * GLA_DV
GLA_GATE_LORA = 16
GLA_TAU = 16.0
CONV_WIDTH = 4
XA_HEADS = 4
XA_HEAD_DIM = 128
XA_WIDTH = XA_HEADS * XA_HEAD_DIM
N_BRANCH = 3
BRANCH_WIDTH = 512
RWKV_COLS = 3 * RWKV_WIDTH + DECAY_LORA + AAA_LORA + GATE_LORA
GLA_QKV_COLS = 2 * GLA_KW + GLA_VW
GLA_COLS = GLA_QKV_COLS + GLA_GATE_LORA + GLA_VW
GATE_COLS = N_BRANCH * D_MODEL
W_IN_COLS = RWKV_COLS + GLA_COLS + XA_WIDTH + GATE_COLS

kernel_name = 'hybrid_rwkv7_gla_memattn_macaron'


def split_cols(t, sizes):
    idx = np.cumsum(sizes)[:-1].tolist()
    return jnp.split(t, idx, axis=-1)


def rmsnorm(x, g):
    xf = x.astype(jnp.float32)
    y = xf * lax.rsqrt(jnp.mean(xf * xf, axis=-1, keepdims=True) + NORM_EPS)
    return (y * g.astype(jnp.float32)).astype(x.dtype)


def shift_right(t):
    return jnp.pad(t, ((0, 0), (1, 0), (0, 0)))[:, :-1]


def causal_depthwise_conv(t, w):
    s = t.shape[1]
    tp = jnp.pad(t, ((0, 0), (CONV_WIDTH - 1, 0), (0, 0)))
    out = tp[:, 0:s] * w[0]
    for j in range(1, CONV_WIDTH):
        out = out + tp[:, j:j + s] * w[j]
    return out


def swiglu_ffn(h, w_in, w_out):
    gate, up = jnp.split(h @ w_in, 2, axis=-1)
    return (jax.nn.silu(gate) * up) @ w_out


def rwkv7_mixer(u, v_first, mu, w0, w2, a0, a2, g2, k_k, k_a, r_k, ln_w, ln_b, v_mix):
    b, s, _ = u.shape
    u = u + mu * (shift_right(u) - u)
    r, k, v, wl, al, gl = split_cols(u, (RWKV_WIDTH, RWKV_WIDTH, RWKV_WIDTH, DECAY_LORA, AAA_LORA, GATE_LORA))
    if v_mix is None:
        v_first = v
    else:
        v0, v1, v2 = v_mix
        v = v + (v_first - v) * jax.nn.sigmoid(v0 + (v @ v1) @ v2)
    w_log = -jax.nn.softplus(-(w0 + jnp.tanh(wl) @ w2)) - 0.5
    a = jax.nn.sigmoid(a0 + al @ a2)
    g = jax.nn.sigmoid(gl) @ g2
    hshape = (RWKV_HEADS, RWKV_HEAD_DIM)
    heads = lambda t: t.astype(jnp.float32).reshape(b, s, *hshape)
    r_h, v_h, a_h = heads(r), heads(v), heads(a)
    kk = heads(k) * k_k.astype(jnp.float32).reshape(hshape)
    kk = kk / jnp.maximum(jnp.linalg.norm(kk, axis=-1, keepdims=True), 1e-12)
    k_h = heads(k) * (1.0 + (a_h - 1.0) * k_a.astype(jnp.float32).reshape(hshape))
    decay = jnp.exp(-jnp.exp(heads(w_log)))

    def step(state, inp):
        r_t, w_t, k_t, v_t, kk_t, a_t = inp
        sa = jnp.einsum('bhvk,bhk->bhv', state, -kk_t)
        state = (state * w_t[:, :, None, :] + sa[..., None] * (kk_t * a_t)[:, :, None, :]
                 + v_t[..., None] * k_t[:, :, None, :])
        return state, jnp.einsum('bhvk,bhk->bhv', state, r_t)

    seq_first = lambda t: jnp.moveaxis(t, 1, 0)
    s0 = jnp.zeros((b, RWKV_HEADS, RWKV_HEAD_DIM, RWKV_HEAD_DIM), jnp.float32)
    _, y = lax.scan(step, s0, tuple(seq_first(t) for t in (r_h, decay, k_h, v_h, kk, a_h)))
    y = jnp.moveaxis(y, 0, 1)
    mean = jnp.mean(y, axis=-1, keepdims=True)
    var = jnp.mean(jnp.square(y - mean), axis=-1, keepdims=True)
    yn = (y - mean) * lax.rsqrt(var + RWKV_LN_EPS) * ln_w.astype(jnp.float32).reshape(hshape) \
        + ln_b.astype(jnp.float32).reshape(hshape)
    bonus = jnp.sum(r_h * k_h * r_k.astype(jnp.float32), axis=-1, keepdims=True) * v_h
    out = (yn + bonus).reshape(b, s, RWKV_WIDTH) * g.astype(jnp.float32)
    return out.astype(u.dtype), v_first


def gla_mixer(u, conv_w, a_up, a_bias, norm_w):
    b, s, _ = u.shape
    n_chunks = s // CHUNK
    qkv, al, go = split_cols(u, (GLA_QKV_COLS, GLA_GATE_LORA, GLA_VW))
    qkv = jax.nn.silu(causal_depthwise_conv(qkv, conv_w))
    q, k, v = split_cols(qkv, (GLA_KW, GLA_KW, GLA_VW))
    log_a = jax.nn.log_sigmoid((al @ a_up + a_bias).astype(jnp.float32)) / GLA_TAU
    chunks = lambda t, d: t.astype(jnp.float32).reshape(b, n_chunks, CHUNK, GLA_HEADS, d).transpose(0, 3, 1, 2, 4)
    q = chunks(q, GLA_DK) * (GLA_DK ** -0.5)
    k = chunks(k, GLA_DK)
    v = chunks(v, GLA_DV)
    gcum = jnp.cumsum(chunks(log_a, GLA_DK), axis=3)
    qg, kg = q * jnp.exp(gcum), k * jnp.exp(-gcum)
    qr, kr = q * jnp.exp(-gcum), k * jnp.exp(gcum)
    a_past = jnp.einsum('bhntd,bhnsd->bhnts', qg, kg)
    a_future = jnp.einsum('bhntd,bhnsd->bhnts', qr, kr)
    lower = jnp.tril(jnp.ones((CHUNK, CHUNK), dtype=bool))
    intra = jnp.einsum('bhnts,bhnse->bhnte', jnp.where(lower, a_past, a_future), v)
    g_last = gcum[:, :, :, -1:, :]
    kv_chunk = jnp.einsum('bhnsd,bhnse->bhnde', k * jnp.exp(g_last - gcum), v)
    decay_chunk = jnp.exp(g_last[:, :, :, 0, :])

    def step(state, inp):
        kv_c, dec_c = inp
        return state * dec_c[..., None] + kv_c, state

    s0 = jnp.zeros((b, GLA_HEADS, GLA_DK, GLA_DV), jnp.float32)
    _, s_prev = lax.scan(step, s0, (jnp.moveaxis(kv_chunk, 2, 0), jnp.moveaxis(decay_chunk, 2, 0)))
    inter = jnp.einsum('bhntd,bhnde->bhnte', qg, jnp.moveaxis(s_prev, 0, 2))
    o = (intra + inter).transpose(0, 2, 3, 1, 4).reshape(b, s, GLA_HEADS, GLA_DV)
    o = o * lax.rsqrt(jnp.mean(o * o, axis=-1, keepdims=True) + NORM_EPS) \
        * norm_w.astype(jnp.float32).reshape(GLA_HEADS, GLA_DV)
    out = o.reshape(b, s, GLA_VW) * jax.nn.silu(go.astype(jnp.float32))
    return out.astype(u.dtype)


def memory_attention(q, mem_n, w_kv):
    b, s, _ = q.shape
    m = mem_n.shape[1]
    k, v = jnp.split(mem_n @ w_kv, 2, axis=-1)
    q = q.reshape(b, s, XA_HEADS, XA_HEAD_DIM)
    k = k.reshape(b, m, XA_HEADS, XA_HEAD_DIM)
    v = v.reshape(b, m, XA_HEADS, XA_HEAD_DIM)
    scores = jnp.einsum('bshd,bmhd->bhsm', q, k).astype(jnp.float32) * (XA_HEAD_DIM ** -0.5)
    p = jax.nn.softmax(scores, axis=-1).astype(v.dtype)
    return jnp.einsum('bhsm,bmhd->bshd', p, v).reshape(b, s, XA_WIDTH)


def setup_inputs(seed: int = 0) -> dict:
    key = jax.random.key(seed)
    ks = iter(jax.random.split(key, 40))
    nrm = lambda shape, scale: scale * jax.random.normal(next(ks), shape, jnp.float32)
    gain = lambda shape: 1.0 + nrm(shape, 0.05)
    L = DEPTH
    return {
        'x': nrm((BATCH, SEQ, D_MODEL), 1.0),
        'mem': nrm((BATCH, N_MEM, D_MODEL), 1.0),
        'ffn1_norm': gain((L, D_MODEL)),
        'ffn1_w_in': nrm((L, D_MODEL, 2 * D_FF), D_MODEL ** -0.5),
        'ffn1_w_out': nrm((L, D_FF, D_MODEL), D_FF ** -0.5),
        'mix_norm': gain((L, D_MODEL)),
        'mem_norm': gain((L, D_MODEL)),
        'w_in': nrm((L, D_MODEL, W_IN_COLS), D_MODEL ** -0.5),
        'rwkv_mu': 0.5 + nrm((L, RWKV_COLS), 0.1),
        'rwkv_w0': -1.0 + nrm((L, RWKV_WIDTH), 0.5),
        'rwkv_w2': nrm((L, DECAY_LORA, RWKV_WIDTH), 0.5 * DECAY_LORA ** -0.5),
        'rwkv_a0': nrm((L, RWKV_WIDTH), 0.1),
        'rwkv_a2': nrm((L, AAA_LORA, RWKV_WIDTH), 0.5 * AAA_LORA ** -0.5),
        'rwkv_g2': nrm((L, GATE_LORA, RWKV_WIDTH), GATE_LORA ** -0.5),
        'rwkv_k_k': 0.85 + nrm((L, RWKV_WIDTH), 0.05),
        'rwkv_k_a': 1.0 + nrm((L, RWKV_WIDTH), 0.05),
        'rwkv_r_k': nrm((L, RWKV_HEADS, RWKV_HEAD_DIM), 0.1),
        'rwkv_ln_w': gain((L, RWKV_WIDTH)),
        'rwkv_ln_b': nrm((L, RWKV_WIDTH), 0.02),
        'rwkv_v0': nrm((L - 1, RWKV_WIDTH), 0.1),
        'rwkv_v1': nrm((L - 1, RWKV_WIDTH, MV_LORA), RWKV_WIDTH ** -0.5),
        'rwkv_v2': nrm((L - 1, MV_LORA, RWKV_WIDTH), 0.5 * MV_LORA ** -0.5),
        'gla_conv': nrm((L, CONV_WIDTH, GLA_QKV_COLS), CONV_WIDTH ** -0.5),
        'gla_a_up': nrm((L, GLA_GATE_LORA, GLA_KW), GLA_GATE_LORA ** -0.5),
        'gla_a_bias': 1.0 + nrm((L, GLA_KW), 0.5),
        'gla_norm': gain((L, GLA_VW)),
        'xa_w_kv': nrm((L, D_MODEL, 2 * XA_WIDTH), D_MODEL ** -0.5),
        'w_branch': nrm((L, N_BRANCH, BRANCH_WIDTH, D_MODEL), BRANCH_WIDTH ** -0.5),
        'w_out': nrm((L, D_MODEL, D_MODEL), D_MODEL ** -0.5),
        'ffn2_norm': gain((L, D_MODEL)),
        'ffn2_w_in': nrm((L, D_MODEL, 2 * D_FF), D_MODEL ** -0.5),
        'ffn2_w_out': nrm((L, D_FF, D_MODEL), D_FF ** -0.5),
        'final_norm': gain((D_MODEL,)),
    }


def reference(x, mem, ffn1_norm, ffn1_w_in, ffn1_w_out, mix_norm, mem_norm, w_in, rwkv_mu, rwkv_w0,
              rwkv_w2, rwkv_a0, rwkv_a2, rwkv_g2, rwkv_k_k, rwkv_k_a, rwkv_r_k, rwkv_ln_w, rwkv_ln_b,
              rwkv_v0, rwkv_v1, rwkv_v2, gla_conv, gla_a_up, gla_a_bias, gla_norm, xa_w_kv, w_branch,
              w_out, ffn2_norm, ffn2_w_in, ffn2_w_out, final_norm):
    b, s, d = x.shape
    v_first = None
    for l in range(DEPTH):
        x = x + 0.5 * swiglu_ffn(rmsnorm(x, ffn1_norm[l]), ffn1_w_in[l], ffn1_w_out[l])
        h = rmsnorm(x, mix_norm[l])
        u_rwkv, u_gla, u_xa, u_gate = split_cols(h @ w_in[l], (RWKV_COLS, GLA_COLS, XA_WIDTH, GATE_COLS))
        v_mix = None if l == 0 else (rwkv_v0[l - 1], rwkv_v1[l - 1], rwkv_v2[l - 1])
        y_rwkv, v_first = rwkv7_mixer(u_rwkv, v_first, rwkv_mu[l], rwkv_w0[l], rwkv_w2[l], rwkv_a0[l],
                                      rwkv_a2[l], rwkv_g2[l], rwkv_k_k[l], rwkv_k_a[l], rwkv_r_k[l],
                                      rwkv_ln_w[l], rwkv_ln_b[l], v_mix)
        y_gla = gla_mixer(u_gla, gla_conv[l], gla_a_up[l], gla_a_bias[l], gla_norm[l])
        y_xa = memory_attention(u_xa, rmsnorm(mem, mem_norm[l]), xa_w_kv[l])
        branches = jnp.stack([y_rwkv, y_gla, y_xa], axis=2)
        gates = jax.nn.sigmoid(u_gate.reshape(b, s, N_BRANCH, d))
        merged = jnp.einsum('bsjc,jcd,bsjd->bsd', branches, w_branch[l], gates)
        x = x + merged @ w_out[l]
        x = x + 0.5 * swiglu_ffn(rmsnorm(x, ffn2_norm[l]), ffn2_w_in[l], ffn2_w_out[l])
    return rmsnorm(x, final_norm)
```

```python
import contextlib
import numpy as np
import concourse.bass as bass
import concourse.mybir as mybir
from concourse.bass_utils import run_bass_kernel_spmd

F32 = mybir.dt.float32
BF16 = mybir.dt.bfloat16
AF = mybir.ActivationFunctionType
ALU = mybir.AluOpType

D = 1024
DFF = 2816
NFF = DFF // 128
NMEM = 256
EPS = 1e-6
LN_EPS = 64e-5
WCOLS = 6928

ENGS = ("pe", "act", "dve", "pool", "sp")
EPOCH = 8000
NDMASEM = 44
DMASEM_RANGE = {"sp": (0, 16), "pool": (16, 16), "act": (32, 8), "cc": (40, 4)}


class Sched:
    def __init__(self, nc):
        self.nc = nc
        self.ops = {e: [] for e in ENGS}
        self.last_write = {}
        self.rd_e = {}
        self.rd_d = {}
        self.seen = {e: {} for e in ENGS}
        self.dma_seen = {e: {} for e in ENGS}
        self.dma_rr_e = {}
        self.dma_val = [0] * NDMASEM
        self.dma_last = [None] * NDMASEM
        self.phase_tok = None

    def add(self, eng, fn, reads=(), writes=(), dma=False, cc=False):
        ops = self.ops[eng]
        idx = len(ops)
        deps = set()
        pbr = [k for k in reads if k.startswith("pb") and k not in writes]
        if pbr:
            writes = list(writes) + pbr
        if self.phase_tok is not None:
            deps.add(self.phase_tok)
        for b in reads:
            lw = self.last_write.get(b)
            if lw is not None:
                deps.add(lw)
        for b in writes:
            lw = self.last_write.get(b)
            if lw is not None:
                deps.add(lw)
            for e2, i2 in self.rd_e.get(b, {}).items():
                deps.add(("e", e2, i2))
            for tok in self.rd_d.get(b, ()):
                deps.add(tok)
        op = dict(fn=fn, waits=[], inc=False, dma=None)
        if dma or cc:
            rk = "cc" if cc else eng
            base, cnt = DMASEM_RANGE[rk]
            s = base + self.dma_rr_e.get(rk, 0)
            self.dma_rr_e[rk] = (self.dma_rr_e.get(rk, 0) + 1) % cnt
            if self.dma_last[s] is not None:
                deps.add(self.dma_last[s])
            amt = 1 if cc else 16
            self.dma_val[s] += amt
            tok = ("d", s, self.dma_val[s])
            self.dma_last[s] = tok
            op["dma"] = (s, self.dma_val[s], amt)
        else:
            tok = ("e", eng, idx)
        best = {}
        for d in deps:
            if d[0] == "e":
                _, se, si = d
                if se == eng and (eng == "pe" or si >= idx):
                    continue
                if self.seen[eng].get(se, -1) >= si:
                    continue
                if best.get(("e", se), -1) < si:
                    best[("e", se)] = si
            else:
                _, s, v = d
                if self.dma_seen[eng].get(s, 0) >= v:
                    continue
                if best.get(("d", s), 0) < v:
                    best[("d", s)] = v
        for k, v in best.items():
            if k[0] == "e":
                self.seen[eng][k[1]] = v
                self.ops[k[1]][v]["inc"] = True
                op["waits"].append(("e", k[1], v))
            else:
                self.dma_seen[eng][k[1]] = v
                op["waits"].append(("d", k[1], v))
        ops.append(op)
        for b in reads:
            if tok[0] == "e":
                m = self.rd_e.setdefault(b, {})
                m[eng] = idx
            else:
                self.rd_d.setdefault(b, set()).add(tok)
        for b in writes:
            self.last_write[b] = tok
            self.rd_e[b] = {}
            self.rd_d[b] = set()
        return tok

    def barrier(self, scratch_ap):
        deps = []
        for e in ENGS:
            if self.ops[e]:
                deps.append(e)
        keys_w = ["__barrier__"]
        idx = len(self.ops["dve"])
        op = dict(fn=lambda e: e.memset(scratch_ap, 0.0), waits=[], inc=True, dma=None)
        for e in ENGS:
            if not self.ops[e]:
                continue
            li = len(self.ops[e]) - 1
            while li >= 0 and self.ops[e][li]["dma"] is not None:
                li -= 1
            if li >= 0 and self.seen["dve"].get(e, -1) < li:
                self.seen["dve"][e] = li
                self.ops[e][li]["inc"] = True
                op["waits"].append(("e", e, li))
        for s in range(NDMASEM):
            if self.dma_last[s] is not None and self.dma_seen["dve"].get(s, 0) < self.dma_val[s]:
                self.dma_seen["dve"][s] = self.dma_val[s]
                op["waits"].append(("d", s, self.dma_val[s]))
        self.ops["dve"].append(op)
        self.phase_tok = ("e", "dve", idx)
        self.last_write = {}
        self.rd_e = {}
        self.rd_d = {}

    def emit(self, final_waits=()):
        nc = self.nc
        for t in final_waits:
            if t[0] == "e":
                self.ops[t[1]][t[2]]["inc"] = True
        with contextlib.ExitStack() as st:
            esems = {}
            counts = {}
            for e in ENGS:
                c = 0
                cl = []
                for op in self.ops[e]:
                    if op["inc"] and op["dma"] is None:
                        c += 1
                    cl.append(c)
                counts[e] = cl
                nep = max(0, c - 1) // EPOCH + 1
                esems[e] = [st.enter_context(nc.semaphore(f"s_{e}_{k}")) for k in range(nep)]
            dsems = [st.enter_context(nc.semaphore(f"s_dma_{k}")) for k in range(NDMASEM)]
            block = st.enter_context(nc.Block())

            def split(c):
                ep = (c - 1) // EPOCH
                return ep, c - ep * EPOCH

            def do_wait(engine, w):
                if w[0] == "e":
                    ep, r = split(counts[w[1]][w[2]])
                    engine.wait_ge(esems[w[1]][ep], r)
                else:
                    engine.wait_ge(dsems[w[1]], w[2])

            def run(e):
                def body(engine):
                    for i, op in enumerate(self.ops[e]):
                        for w in op["waits"]:
                            do_wait(engine, w)
                        ins = op["fn"](engine)
                        if op["dma"] is not None:
                            ins.then_inc(dsems[op["dma"][0]], op["dma"][2])
                        elif op["inc"]:
                            ep, r = split(counts[e][i])
                            ins.then_inc(esems[e][ep], 1)
                    if e == "sp":
                        for w in final_waits:
                            do_wait(engine, w)
                return body

            block.tensor(run("pe"))
            block.scalar(run("act"))
            block.vector(run("dve"))
            block.gpsimd(run("pool"))
            block.sync(run("sp"))


VEC_SPEC = [("ffn1_norm", 8), ("mix_norm", 8), ("mem_norm", 8), ("ffn2_norm", 8), ("final_norm", 8),
            ("rwkv_mu", 14), ("rwkv_w0", 4), ("rwkv_a0", 4), ("rwkv_k_k", 4), ("rwkv_k_a", 4),
            ("rwkv_r_k", 4), ("rwkv_ln_w", 4), ("rwkv_ln_b", 4), ("rwkv_v0", 4),
            ("conv0", 8), ("conv1", 8), ("conv2", 8), ("conv3", 8), ("gla_a_bias", 2), ("gla_norm", 4)]
VOFF = {}
_o = 0
for _n, _c in VEC_SPEC:
    VOFF[_n] = _o
    _o += _c
NV = _o
NST = 512 + 256 + 14 + 24


class B:
    def __init__(self, n_slots, n_sub, TS, norm_after, stages=("ffn1", "mix", "ffn2"), dbg=(), n_cores=8):
        self.n_cores = n_cores
        self.nc = nc = bass.Bass("TRN2", target_bir_lowering=False)
        self.S = Sched(nc)
        self.NSL, self.NSUB, self.TS = n_slots, n_sub, TS
        self.NT = TS // 512
        self.stages = stages
        self.norm_after = list(norm_after)
        self.dbg = dbg
        self.dbg_out = {}
        self.mix_parts = ("xa", "gla", "rwkv", "merge")
        NSL = n_slots
        din = lambda n, sh: nc.dram_tensor(n, sh, F32, kind="ExternalInput").ap()
        dout = lambda n, sh: nc.dram_tensor(n, sh, F32, kind="ExternalOutput").ap()
        self.x_in = din("x_in", [n_sub, 128, 8, TS])
        self.x_out = dout("x_out", [n_sub, 128, 8, TS])
        self.y_out = dout("y_out", [max(1, len(self.norm_after)), n_sub, 128, 8, TS])
        self.memT = din("memT", [128, 8, NMEM])
        self.vecs = din("vecs", [NSL, 128, NV])
        self.flags = din("flags", [128, 2 * NSL])
        self.st_in = din("st_in", [128, NST])
        self.st_out = dout("st_out", [128, NST])
        self.vf_in = din("vf_in", [n_sub, 128, 4, TS])
        self.vf_out = dout("vf_out", [n_sub, 128, 4, TS])
        self.w_f1i = din("w_f1i", [NSL, NFF, 128, 8 * 256])
        self.w_f1o = din("w_f1o", [NSL, 8, 128, NFF * 128])
        self.w_f2i = din("w_f2i", [NSL, NFF, 128, 8 * 256])
        self.w_f2o = din("w_f2o", [NSL, 8, 128, NFF * 128])
        self.w_in = din("w_in", [NSL, 54, 128, 8 * 128])
        self.w_al = din("w_al", [NSL, 128, 8 * 16])
        self.w_kv = din("w_kv", [NSL, 128, 8 * 1024])
        self.w_br = din("w_br", [NSL, 8, 128, 3 * 4 * 128])
        self.w_o = din("w_o", [NSL, 8, 128, 8 * 128])
        self.w_w2 = din("w_w2", [NSL, 64, 512])
        self.w_a2 = din("w_a2", [NSL, 64, 512])
        self.w_g2 = din("w_g2", [NSL, 128, 512])
        self.w_v1 = din("w_v1", [NSL, 128, 4 * 32])
        self.w_v2 = din("w_v2", [NSL, 32, 512])
        self.w_aup = din("w_aup", [NSL, 16, 256])
        self.uid = 0
        self.st_snd = nc.dram_tensor("st_snd", [128, NST], F32).ap()
        self.st_pair = nc.dram_tensor("st_pair", [256, NST], F32).ap()

    def mm(self, out, lhsT, rhs, start, stop, r, w, skip=False):
        if skip:
            self.S.add("pe", lambda e: e.matmul(out, lhsT, rhs, start=start, stop=stop, skip_group_check=True), r, w)
        else:
            self.S.add("pe", lambda e: e.matmul(out, lhsT, rhs, start=start, stop=stop), r, w)

    def act(self, out, in_, func, r, w, bias=None, scale=None, eng="act"):
        kw = {}
        if bias is not None:
            kw["bias"] = bias
        if scale is not None:
            kw["scale"] = scale
        self.S.add("act", lambda e: e.activation(out=out, in_=in_, func=func, **kw), r, w)

    def tt(self, out, in0, in1, op, r, w, eng="dve"):
        self.S.add(eng, lambda e: e.tensor_tensor(out=out, in0=in0, in1=in1, op=op), r, w)

    def ts(self, out, in0, s1, s2, op0, op1, r, w, eng="dve"):
        if op1 is None:
            self.S.add(eng, lambda e: e.tensor_scalar(out=out, in0=in0, scalar1=s1, scalar2=None, op0=op0), r, w)
        else:
            self.S.add(eng, lambda e: e.tensor_scalar(out=out, in0=in0, scalar1=s1, scalar2=s2, op0=op0, op1=op1), r, w)

    def stt(self, out, in0, scalar, in1, op0, op1, r, w):
        self.S.add("dve", lambda e: e.scalar_tensor_tensor(out=out, in0=in0, scalar=scalar, in1=in1, op0=op0, op1=op1), r, w)

    def cp(self, out, in_, r, w, eng="dve"):
        if eng == "act_copy":
            self.S.add("act", lambda e: e.activation(out=out, in_=in_, func=AF.Copy), r, w)
        else:
            self.S.add(eng, lambda e: e.tensor_copy(out=out, in_=in_), r, w)

    def recip(self, out, in_, r, w):
        self.S.add("dve", lambda e: e.reciprocal(out=out, in_=in_), r, w)

    def dma(self, out, in_, r, w, eng="sp"):
        return self.S.add(eng, lambda e: e.dma_start(out=out, in_=in_), r, w, dma=True)

    def memset(self, ap, val, w, eng="pool"):
        self.S.add(eng, lambda e: e.memset(ap, val), (), w)

    def dump(self, name, ap, shape, key):
        if name not in self.dbg:
            return
        t = self.nc.dram_tensor("dbg_" + name, list(shape), ap.dtype if hasattr(ap, "dtype") else F32, kind="ExternalOutput").ap()
        self.dbg_out[name] = self.dma(t, ap, [key], ["dbg_" + name])

    def sb(self, st, name, shape, dt):
        self.uid += 1
        return st.enter_context(self.nc.sbuf_tensor("%s_%d" % (name, self.uid), list(shape), dt))

    def build(self):
        nc, S, TS, NT = self.nc, self.S, self.TS, self.NT
        with contextlib.ExitStack() as st:
            self.PB = [st.enter_context(nc.psum_tensor(f"pb{i}", [128, 512], F32)) for i in range(8)]
            self.c_scr = self.sb(st, "c_scr", [128, 8], F32)
            self.alloc_consts(st)
            self.vec = self.sb(st, "vec", [128, NV], F32)
            self.flg = self.sb(st, "flg", [128, 2 * self.NSL], F32)
            self.rwS = self.sb(st, "rwS", [128, 4, 128], F32)
            self.glS = self.sb(st, "glS", [128, 2, 128], F32)
            self.glSb = self.sb(st, "glSb", [128, 2, 128], BF16)
            self.uprev = self.sb(st, "uprev", [128, 14], F32)
            self.convh = self.sb(st, "convh", [128, 8, 3], F32)
            self.xT = self.sb(st, "xT", [128, 8, TS], F32)
            self.hT = self.sb(st, "hT", [128, 8, TS], BF16)
            self.init_consts()
            self.dma(self.flg[:], self.flags, [], ["flg"])
            finals = []
            for slot in range(self.NSL):
                self.dma(self.vec[:], self.vecs[slot], [], ["vec"])
                self.load_state(slot)
                for sub in range(self.NSUB):
                    src = self.x_in if slot == 0 else self.x_out
                    self.dma(self.xT[:], src[sub], ["xdram%d" % sub], ["xT"])
                    if "ffn1" in self.stages:
                        self.ffn(st, slot, self.w_f1i, self.w_f1o, VOFF["ffn1_norm"])
                    if "mix" in self.stages:
                        self.mixer(st, slot, sub)
                    if sub == self.NSUB - 1:
                        finals.append(self.store_state(slot))
                    if "ffn2" in self.stages:
                        self.ffn(st, slot, self.w_f2i, self.w_f2o, VOFF["ffn2_norm"])
                    t = self.dma(self.x_out[sub], self.xT[:], ["xT"], ["xdram%d" % sub])
                    if slot == self.NSL - 1:
                        finals.append(t)
                    if slot in self.norm_after:
                        self.rmsnorm(VOFF["final_norm"], out_f32=True)
                        t = self.dma(self.y_out[self.norm_after.index(slot), sub], self.xT[:], ["xT"], ["ydram"])
                        finals.append(t)
            for k, t in self.dbg_out.items():
                finals.append(t)
            S.emit(final_waits=finals)
        return nc

    def alloc_consts(self, st):
        self.ident = self.sb(st, "ident", [128, 128], F32)
        self.on_d = self.sb(st, "on_d", [128, 128], BF16)
        self.on_1 = self.sb(st, "on_1", [128, 128], BF16)
        self.on_128 = self.sb(st, "on_128", [128, 128], BF16)
        self.blk = self.sb(st, "blk", [128, 128], F32)
        self.blk64 = self.sb(st, "blk64", [128, 128], F32)
        self.triu = self.sb(st, "triu", [128, 128], F32)
        self.mpast = self.sb(st, "mpast", [128, 4, 128], F32)
        self.mfut = self.sb(st, "mfut", [128, 4, 128], F32)
        self.mstr = self.sb(st, "mstr", [128, 4, 128], F32)
        self.smask = self.sb(st, "smask", [128, 512], F32)
        self.c_eps = self.sb(st, "c_eps", [128, 4], F32)

    def init_consts(self):
        S = self.S
        ms = self.memset
        ms(self.ident[:], 1.0, ["ident"])
        S.add("pool", lambda e: e.affine_select(out=self.ident[:], in_=self.ident[:], pattern=[[-1, 128]],
                                                compare_op=ALU.is_equal, fill=0.0, base=0, channel_multiplier=1),
              ["ident"], ["ident"])
        ms(self.on_d[:], 1.0 / D, ["on_d"])
        ms(self.on_1[:], 1.0, ["on_1"])
        ms(self.on_128[:], 1.0 / 128, ["on_128"])
        ms(self.blk[:], 0.0, ["blk"])
        ms(self.blk[0:64, 0:64], 1.0, ["blk"])
        ms(self.blk[64:128, 64:128], 1.0, ["blk"])
        self.ts(self.blk64[:], self.blk[:], 1.0 / 64, None, ALU.mult, None, ["blk"], ["blk64"], eng="pool")
        ms(self.triu[:], 1.0, ["triu"])
        S.add("pool", lambda e: e.affine_select(out=self.triu[:], in_=self.triu[:], pattern=[[1, 128]],
                                                compare_op=ALU.is_ge, fill=0.0, base=0, channel_multiplier=-1),
              ["triu"], ["triu"])
        for h in range(4):
            self.tt(self.mpast[:, h, :], self.triu[:], self.blk[:], ALU.mult, ["triu", "blk"], ["mpast"], eng="pool")
        for h in range(4):
            self.tt(self.mfut[:, h, :], self.blk[:], self.mpast[:, h, :], ALU.subtract, ["blk", "mpast"], ["mfut"], eng="pool")
            self.tt(self.mstr[:, h, :], self.mpast[:, h, :], self.ident[:], ALU.subtract, ["ident", "mpast"], ["mstr"], eng="pool")
        ms(self.c_eps[:, 0:1], EPS, ["c_eps"])
        ms(self.c_eps[:, 1:2], LN_EPS, ["c_eps"])
        ms(self.c_eps[:, 2:3], 1.0, ["c_eps"])
        ms(self.c_eps[:, 3:4], 0.0, ["c_eps"])
        ms(self.smask[:], 1.0, ["smask"])
        ms(self.smask[:].rearrange("p (c s) -> p c s", s=64)[:, :, 0:1], 0.0, ["smask"])

    def load_state(self, slot):
        f = self.flg[:, 2 * slot + 1:2 * slot + 2]
        tmp = [(self.rwS[:].rearrange("p a b -> p (a b)"), 0, 512, "rwS"),
               (self.glS[:].rearrange("p a b -> p (a b)"), 512, 256, "glS"),
               (self.uprev[:], 768, 14, "uprev"),
               (self.convh[:].rearrange("p a b -> p (a b)"), 782, 24, "convh")]
        for ap, o, n, key in tmp:
            if slot == 0:
                self.dma(ap, self.st_in[:, o:o + n], [], [key])
            else:
                self.dma(ap, self.st_pair[0:128, o:o + n], ["st_pair"], [key])
            self.ts(ap, ap, f, None, ALU.mult, None, [key, "flg"], [key])

    def store_state(self, slot):
        t = None
        tmp = [(self.rwS[:].rearrange("p a b -> p (a b)"), 0, 512, "rwS"),
               (self.glS[:].rearrange("p a b -> p (a b)"), 512, 256, "glS"),
               (self.uprev[:], 768, 14, "uprev"),
               (self.convh[:].rearrange("p a b -> p (a b)"), 782, 24, "convh")]
        last = slot == self.NSL - 1
        for ap, o, n, key in tmp:
            if last:
                t = self.dma(self.st_out[:, o:o + n], ap, [key], ["st_out%d" % o])
            else:
                t = self.dma(self.st_snd[:, o:o + n], ap, [key], ["st_snd"])
        if not last:
            groups = [[2 * i, 2 * i + 1] for i in range(self.n_cores // 2)]
            snd, pair = self.st_snd, self.st_pair
            t = self.S.add("pool", lambda e: e.collective_compute("AllGather", ALU.bypass, replica_groups=groups,
                                                                  ins=[snd.opt()], outs=[pair.opt()]),
                           ["st_snd"], ["st_pair"], cc=True)
        return t

    def rmsnorm(self, goff, out_f32=False):
        with contextlib.ExitStack() as st:
            sq = self.sb(st, "n_sq", [128, 8, 512], BF16)
            rs = self.sb(st, "n_rs", [128, 512], F32)
            for nt in range(self.NT):
                tsl = slice(nt * 512, (nt + 1) * 512)
                ps = self.PB[7]
                for c in range(8):
                    self.act(sq[:, c, :], self.xT[:, c, tsl], AF.Square, ["xT"], ["n_sq%d" % c])
                    self.mm(ps[:], self.on_d[:], sq[:, c, :], c == 0, c == 7, ["on_d", "n_sq%d" % c], ["pb7"])
                self.act(rs[:], ps[:], AF.Sqrt, ["pb7", "c_eps"], ["n_rs"], bias=self.c_eps[:, 0:1])
                self.recip(rs[:], rs[:], ["n_rs"], ["n_rs"])
                for c in range(8):
                    g = self.vec[:, goff + c:goff + c + 1]
                    dst = self.xT[:, c, tsl] if out_f32 else self.hT[:, c, tsl]
                    self.stt(dst, self.xT[:, c, tsl], g, rs[:], ALU.mult, ALU.mult, ["xT", "vec", "n_rs"],
                             ["xT" if out_f32 else "hT"])
            self.S.barrier(self.c_scr[:, 0:1])

    def ffn(self, st0, slot, w_i, w_o, goff):
        self.rmsnorm(goff)
        NT = self.NT
        with contextlib.ExitStack() as st:
            actT = self.sb(st, "f_act", [128, NFF, self.TS], BF16)
            wb = [self.sb(st, "f_wi%d" % i, [128, 8, 256], BF16) for i in range(3)]
            wo = [self.sb(st, "f_wo%d" % i, [128, NFF, 128], BF16) for i in range(2)]
            sg = [self.sb(st, "f_sg%d" % i, [128, 512], F32) for i in range(2)]
            it = 0
            for j in range(NFF):
                w = wb[j % 3]
                wk = "f_wi%d" % (j % 3)
                self.dma(w[:].rearrange("p a b -> p (a b)"), w_i[slot, j], [], [wk], eng="pool")
                for nt in range(NT):
                    tsl = slice(nt * 512, (nt + 1) * 512)
                    k = it % 2
                    it += 1
                    pg, pu = self.PB[k], self.PB[2 + k]
                    for kc in range(8):
                        self.mm(pg[:], w[:, kc, 0:128], self.hT[:, kc, tsl], kc == 0, kc == 7, [wk, "hT"], ["pb%d" % k])
                    for kc in range(8):
                        self.mm(pu[:], w[:, kc, 128:256], self.hT[:, kc, tsl], kc == 0, kc == 7, [wk, "hT"], ["pb%d" % (2 + k)])
                    self.act(sg[k][:], pg[:], AF.Silu, ["pb%d" % k], ["f_sg%d" % k])
                    self.tt(actT[:, j, tsl], sg[k][:], pu[:], ALU.mult, ["f_sg%d" % k, "pb%d" % (2 + k)], ["f_act%d" % j])
            it = 0
            for dc in range(8):
                w = wo[dc % 2]
                wk = "f_wo%d" % (dc % 2)
                self.dma(w[:].rearrange("p a b -> p (a b)"), w_o[slot, dc], [], [wk], eng="pool")
                for nt in range(NT):
                    tsl = slice(nt * 512, (nt + 1) * 512)
                    k = 4 + it % 2
                    it += 1
                    po = self.PB[k]
                    for j in range(NFF):
                        self.mm(po[:], w[:, j, :], actT[:, j, tsl], j == 0, j == NFF - 1, [wk, "f_act%d" % j], ["pb%d" % k])
                    self.stt(self.xT[:, dc, tsl], po[:], 0.5, self.xT[:, dc, tsl], ALU.mult, ALU.add,
                             ["pb%d" % k, "xT"], ["xT"])
            self.S.barrier(self.c_scr[:, 0:1])

    def win_tile(self, slot, ci, bufs, cnt):
        k = cnt[0] % len(bufs)
        cnt[0] += 1
        w = bufs[k]
        key = "m_wi%d" % k
        self.dma(w[:].rearrange("p a b -> p (a b)"), self.w_in[slot, ci], [], [key], eng="pool")
        return w, key

    def proj(self, ps, pkey, w, wkey, tsl, M=128):
        for kc in range(8):
            self.mm(ps, w[:, kc, 0:M], self.hT[:, kc, tsl], kc == 0, kc == 7, [wkey, "hT"], [pkey])

    def mixer(self, st0, slot, sub):
        self.rmsnorm(VOFF["mix_norm"])
        TS, NT = self.TS, self.NT
        with contextlib.ExitStack() as st:
            self.yT = self.sb(st, "yT", [128, 12, TS], BF16)
            self.wib = [self.sb(st, "m_wi%d" % i, [128, 8, 128], BF16) for i in range(4)]
            self.wcnt = [0]
            if len(self.mix_parts) < 4:
                self.memset(self.yT[:].rearrange("p a b -> p (a b)"), 0.0, ["yT"])
            if "xa" in self.mix_parts:
                self.mix_xa(slot)
                self.S.barrier(self.c_scr[:, 0:1])
            if "gla" in self.mix_parts:
                self.mix_gla(slot, sub)
                self.S.barrier(self.c_scr[:, 0:1])
            if "rwkv" in self.mix_parts:
                self.mix_rwkv(slot, sub)
                self.S.barrier(self.c_scr[:, 0:1])
            if "merge" in self.mix_parts:
                self.mix_merge(slot)
                self.S.barrier(self.c_scr[:, 0:1])
            self.dump("yT", self.yT[:], [128, 12, TS], "yT")
            self.S.barrier(self.c_scr[:, 0:1])

    def mix_xa(self, slot):
        TS, NT = self.TS, self.NT
        with contextlib.ExitStack() as st:
            mem = self.sb(st, "xa_mem", [128, 8, NMEM], F32)
            msq = self.sb(st, "xa_msq", [128, 8, NMEM], BF16)
            mrs = self.sb(st, "xa_mrs", [128, NMEM], F32)
            memn = self.sb(st, "xa_memn", [128, 8, NMEM], BF16)
            wkv = self.sb(st, "xa_wkv", [128, 8, 1024], BF16)
            KT = self.sb(st, "xa_KT", [128, 4, NMEM], BF16)
            V = self.sb(st, "xa_V", [128, 2, 512], BF16)
            qT = self.sb(st, "xa_q", [128, 512], BF16)
            E = self.sb(st, "xa_E", [128, 2, 512], BF16)
            rden = self.sb(st, "xa_rden", [128, 512], F32)
            self.dma(mem[:], self.memT, [], ["xa_mem"])
            self.dma(wkv[:].rearrange("p a b -> p (a b)"), self.w_kv[slot], [], ["xa_wkv"], eng="pool")
            ps = self.PB[0]
            for c in range(8):
                self.act(msq[:, c, :], mem[:, c, :], AF.Square, ["xa_mem"], ["xa_msq"])
                self.mm(ps[:, 0:NMEM], self.on_d[:], msq[:, c, :], c == 0, c == 7, ["on_d", "xa_msq"], ["pb0"])
            self.act(mrs[:], ps[:, 0:NMEM], AF.Sqrt, ["pb0", "c_eps"], ["xa_mrs"], bias=self.c_eps[:, 0:1])
            self.recip(mrs[:], mrs[:], ["xa_mrs"], ["xa_mrs"])
            go = VOFF["mem_norm"]
            for c in range(8):
                self.stt(memn[:, c, :], mem[:, c, :], self.vec[:, go + c:go + c + 1], mrs[:], ALU.mult, ALU.mult,
                         ["xa_mem", "vec", "xa_mrs"], ["xa_memn"])
            for h in range(4):
                ps = self.PB[h % 2]
                pk = "pb%d" % (h % 2)
                for kc in range(8):
                    self.mm(ps[:, 0:NMEM], wkv[:, kc, h * 128:(h + 1) * 128], memn[:, kc, :], kc == 0, kc == 7,
                            ["xa_wkv", "xa_memn"], [pk])
                self.act(KT[:, h, :], ps[:, 0:NMEM], AF.Copy, [pk], ["xa_KT"])
            for mb in range(2):
                ps = self.PB[2 + mb]
                pk = "pb%d" % (2 + mb)
                for kc in range(8):
                    self.mm(ps[:], memn[:, kc, mb * 128:(mb + 1) * 128], wkv[:, kc, 512:1024], kc == 0, kc == 7,
                            ["xa_wkv", "xa_memn"], [pk])
                self.act(V[:, mb, :], ps[:], AF.Copy, [pk], ["xa_V"])
            for h in range(4):
                w, wk = self.win_tile(slot, 26 + h, self.wib, self.wcnt)
                for nt in range(NT):
                    tsl = slice(nt * 512, (nt + 1) * 512)
                    self.proj(self.PB[4][:], "pb4", w, wk, tsl)
                    self.act(qT[:], self.PB[4][:], AF.Copy, ["pb4"], ["xa_q"], scale=float(128 ** -0.5))
                    for mb in range(2):
                        self.mm(self.PB[5][:], KT[:, h, mb * 128:(mb + 1) * 128], qT[:], True, True, ["xa_KT", "xa_q"], ["pb5"])
                        self.act(E[:, mb, :], self.PB[5][:], AF.Exp, ["pb5"], ["xa_E%d" % mb])
                    for mb in range(2):
                        self.mm(self.PB[6][:], V[:, mb, h * 128:(h + 1) * 128], E[:, mb, :], mb == 0, mb == 1,
                                ["xa_V", "xa_E%d" % mb], ["pb6"])
                    for mb in range(2):
                        self.mm(self.PB[7][:], self.on_1[:], E[:, mb, :], mb == 0, mb == 1, ["on_1", "xa_E%d" % mb], ["pb7"])
                    self.recip(rden[:], self.PB[7][:], ["pb7"], ["xa_rden"])
                    self.tt(self.yT[:, 8 + h, tsl], self.PB[6][:], rden[:], ALU.mult, ["pb6", "xa_rden"], ["yT"])

    def mix_gla(self, slot, sub):
        TS, NT = self.TS, self.NT
        NBLK = TS // 128
        S = self.S
        import os
        GS = int(os.environ.get("GLA_STOP", "99"))
        with contextlib.ExitStack() as st:
            sgo = self.sb(st, "g_sgo", [128, 4, TS], BF16)
            v = self.sb(st, "g_v", [128, 4, TS], F32)
            Eg = self.sb(st, "g_Eg", [128, 2, TS], F32)
            qg = self.sb(st, "g_qg", [128, 2, TS], BF16)
            qr = self.sb(st, "g_qr", [128, 2, TS], BF16)
            kg = self.sb(st, "g_kg", [128, 2, TS], BF16)
            kr = self.sb(st, "g_kr", [128, 2, TS], BF16)
            kd = self.sb(st, "g_kd", [128, 2, TS], F32)
            it = 0
            with contextlib.ExitStack() as stB:
                qk = self.sb(stB, "g_qk", [128, 4, TS], F32)
                Eng = self.sb(stB, "g_Eng", [128, 2, TS], F32)
                alT = self.sb(stB, "g_alT", [16, TS], F32)
                aup = self.sb(stB, "g_aup", [16, 256], F32)
                wal = self.sb(stB, "g_wal", [128, 8, 16], BF16)
                self.dma(aup[:], self.w_aup[slot], [], ["g_aup"])
                self.dma(wal[:].rearrange("p a b -> p (a b)"), self.w_al[slot], [], ["g_wal"], eng="pool")
                for c in range(4):
                    w, wk = self.win_tile(slot, 22 + c, self.wib, self.wcnt)
                    for nt in range(NT):
                        tsl = slice(nt * 512, (nt + 1) * 512)
                        k = it % 2
                        it += 1
                        self.proj(self.PB[k][:], "pb%d" % k, w, wk, tsl)
                        self.act(sgo[:, c, tsl], self.PB[k][:], AF.Silu, ["pb%d" % k], ["g_sgo"])
                for nt in range(NT):
                    tsl = slice(nt * 512, (nt + 1) * 512)
                    k = it % 2
                    it += 1
                    self.proj(self.PB[k][0:16, :], "pb%d" % k, wal, "g_wal", tsl, M=16)
                    self.act(alT[:, tsl], self.PB[k][0:16, :], AF.Copy, ["pb%d" % k], ["g_alT"])
                for half in range(2):
                    with contextlib.ExitStack() as stC:
                        ug = self.sb(stC, "g_ug", [128, 4, 3 + TS], F32)
                        tmp = self.sb(stC, "g_ctmp", [128, TS], F32)
                        self.cp(ug[:, :, 0:3], self.convh[:, half * 4:half * 4 + 4, :], ["convh"], ["g_ug"])
                        for c4 in range(4):
                            c = half * 4 + c4
                            w, wk = self.win_tile(slot, 14 + c, self.wib, self.wcnt)
                            for nt in range(NT):
                                tsl = slice(nt * 512, (nt + 1) * 512)
                                k = it % 2
                                it += 1
                                self.proj(self.PB[k][:], "pb%d" % k, w, wk, tsl)
                                self.act(ug[:, c4, 3 + nt * 512:3 + (nt + 1) * 512], self.PB[k][:], AF.Copy, ["pb%d" % k], ["g_ug"])
                        self.cp(self.convh[:, half * 4:half * 4 + 4, :], ug[:, :, TS:TS + 3], ["g_ug"], ["convh"])
                        for c4 in range(4):
                            c = half * 4 + c4
                            cw = [self.vec[:, VOFF["conv%d" % jj] + c:VOFF["conv%d" % jj] + c + 1] for jj in range(4)]
                            self.ts(tmp[:], ug[:, c4, 3:3 + TS], cw[3], None, ALU.mult, None, ["g_ug", "vec"], ["g_ctmp"])
                            for jj in (2, 1, 0):
                                self.stt(tmp[:], ug[:, c4, jj:jj + TS], cw[jj], tmp[:], ALU.mult, ALU.add, ["g_ug", "vec", "g_ctmp"], ["g_ctmp"])
                            dst = qk[:, c, :] if c < 4 else v[:, c - 4, :]
                            self.act(dst, tmp[:], AF.Silu, ["g_ctmp"], ["g_qk" if c < 4 else "g_v"])
                        self.S.barrier(self.c_scr[:, 0:1])
                if GS <= 2:
                    return
                with contextlib.ExitStack() as stC:
                    la = self.sb(stC, "g_la", [128, 2, TS], F32)
                    sgm = self.sb(stC, "g_sgm", [128, 512], F32)
                    ab = VOFF["gla_a_bias"]
                    for c2 in range(2):
                        for nt in range(NT):
                            tsl = slice(nt * 512, (nt + 1) * 512)
                            k = it % 2
                            it += 1
                            self.mm(self.PB[k][:], aup[0:16, c2 * 128:(c2 + 1) * 128], alT[0:16, tsl], True, True, ["g_aup", "g_alT"], ["pb%d" % k])
                            self.act(sgm[:], self.PB[k][:], AF.Sigmoid, ["pb%d" % k, "vec"], ["g_sgm"], bias=self.vec[:, ab + c2:ab + c2 + 1])
                            self.act(sgm[:], sgm[:], AF.Ln, ["g_sgm"], ["g_sgm"])
                            S.add("dve", (lambda o, d1: (lambda e: e.tensor_tensor_scan(out=o, data0=self.smask[:], data1=d1, initial=0.0,
                                                                                       op0=ALU.mult, op1=ALU.add)))(la[:, c2, tsl], sgm[:]),
                                  ["smask", "g_sgm"], ["g_la"])
                    self.act(Eg[:].rearrange("p a b -> p (a b)"), la[:].rearrange("p a b -> p (a b)"), AF.Exp, ["g_la"], ["g_Eg"], scale=1.0 / 16)
                    self.act(Eng[:].rearrange("p a b -> p (a b)"), la[:].rearrange("p a b -> p (a b)"), AF.Exp, ["g_la"], ["g_Eng"], scale=-1.0 / 16)
                    self.S.barrier(self.c_scr[:, 0:1])
                for c2 in range(2):
                    self.stt(qg[:, c2, :], qk[:, c2, :], 0.125, Eg[:, c2, :], ALU.mult, ALU.mult, ["g_qk", "g_Eg"], ["g_qg"])
                    self.stt(qr[:, c2, :], qk[:, c2, :], 0.125, Eng[:, c2, :], ALU.mult, ALU.mult, ["g_qk", "g_Eng"], ["g_qr"])
                    self.tt(kg[:, c2, :], qk[:, 2 + c2, :], Eng[:, c2, :], ALU.mult, ["g_qk", "g_Eng"], ["g_kg"])
                    self.tt(kr[:, c2, :], qk[:, 2 + c2, :], Eg[:, c2, :], ALU.mult, ["g_qk", "g_Eg"], ["g_kr"])
                    for ch in range(TS // 64):
                        cs_ = slice(ch * 64, ch * 64 + 64)
                        self.stt(kd[:, c2, cs_], qk[:, 2 + c2, cs_], Eg[:, c2, ch * 64 + 63:ch * 64 + 64], Eng[:, c2, cs_],
                                 ALU.mult, ALU.mult, ["g_qk", "g_Eg", "g_Eng"], ["g_kd"])
                self.S.barrier(self.c_scr[:, 0:1])
            if GS <= 4:
                return
            v_tok = self.sb(st, "g_vtok", [128, 512], F32)
            kd2 = self.sb(st, "g_kd2", [128, 2, 256], F32)
            kgp = self.sb(st, "g_kgp", [128, 2, 2, 128], BF16)
            krp = self.sb(st, "g_krp", [128, 2, 2, 128], BF16)
            qgp = self.sb(st, "g_qgp", [128, 2, 2, 128], F32)
            at = self.sb(st, "g_at", [128, 512], F32)
            at2 = self.sb(st, "g_at2", [128, 512], F32)
            atb = self.sb(st, "g_atb", [128, 512], F32)
            sq = self.sb(st, "g_sq", [128, 512], BF16)
            rs = self.sb(st, "g_rs", [128, 512], F32)
            ot = self.sb(st, "g_ot", [128, 512], F32)
            glS = self.glS
            for t_, k_ in ((kd2, "g_kd2"), (kgp, "g_kgp"), (krp, "g_krp"), (qgp, "g_qgp")):
                self.memset(t_[:].rearrange("p a b -> p (a b)") if len(t_.shape) == 3 else t_[:].rearrange("p a b c -> p (a b c)"), 0.0, [k_])
            mp = self.mpast[:].rearrange("p a b -> p (a b)")
            mf = self.mfut[:].rearrange("p a b -> p (a b)")
            gn = VOFF["gla_norm"]
            for b in range(NBLK):
                bsl = slice(b * 128, (b + 1) * 128)
                for c in range(4):
                    S.add("pe", (lambda o, i_: (lambda e: e.transpose(o, i_, self.ident[:])))(self.PB[0][:, c * 128:(c + 1) * 128], v[:, c, bsl]),
                          ["g_v", "ident"], ["pb0"])
                self.cp(v_tok[:], self.PB[0][:], ["pb0"], ["g_vtok"], eng="act_copy")
                for c2 in range(2):
                    S.add("pe", (lambda o, i_: (lambda e: e.transpose(o, i_, self.ident[:])))(self.PB[1][:, c2 * 128:(c2 + 1) * 128], kd[:, c2, bsl]),
                          ["g_kd", "ident"], ["pb1"])
                for cc in range(2):
                    crow = slice(cc * 64, cc * 64 + 64)
                    self.cp(kd2[crow, cc, :], self.PB[1][crow, 0:256], ["pb1"], ["g_kd2"], eng="act_copy")
                for hh in range(2):
                    r2 = slice(hh * 64, hh * 64 + 64)
                    self.cp(kgp[r2, hh, :, :], kg[r2, :, bsl], ["g_kg"], ["g_kgp"], eng="pool")
                    self.cp(krp[r2, hh, :, :], kr[r2, :, bsl], ["g_kr"], ["g_krp"], eng="pool")
                    self.cp(qgp[r2, hh, :, :], qg[r2, :, bsl], ["g_qg"], ["g_qgp"], eng="pool")
                if GS <= 5:
                    continue
                for h in range(4):
                    hs = slice(h * 128, (h + 1) * 128)
                    self.mm(self.PB[2][:, hs], kgp[:, h % 2, h // 2, :], qg[:, h // 2, bsl], True, True, ["g_kgp", "g_qg"], ["pb2"])
                    self.mm(self.PB[3][:, hs], krp[:, h % 2, h // 2, :], qr[:, h // 2, bsl], True, True, ["g_krp", "g_qr"], ["pb3"])
                self.tt(at[:], self.PB[2][:], mp, ALU.mult, ["pb2", "mpast"], ["g_at"])
                self.tt(at2[:], self.PB[3][:], mf, ALU.mult, ["pb3", "mfut"], ["g_at2"])
                self.tt(atb[:], at[:], at2[:], ALU.add, ["g_at", "g_at2"], ["g_atb"], eng="pool")
                if GS <= 6:
                    continue
                po = self.PB[4]
                for h in range(4):
                    hs = slice(h * 128, (h + 1) * 128)
                    self.mm(po[:, hs], v_tok[:, hs], atb[:, hs], h == 0, False, ["g_vtok", "g_atb"], ["pb4"], skip=True)
                for cc in range(2):
                    for h in range(4):
                        self.mm(po[:, h * 128 + cc * 64:h * 128 + cc * 64 + 64], glS[:, h // 2, :], qgp[:, h % 2, h // 2, cc * 64:cc * 64 + 64],
                                False, True, ["glS", "g_qgp"], ["pb4"], skip=True)
                    for hp in range(2):
                        self.mm(self.PB[5][:, hp * 256:(hp + 1) * 256], kd2[:, cc, hp * 128:(hp + 1) * 128],
                                v_tok[:, hp * 256:(hp + 1) * 256], True, True, ["g_kd2", "g_vtok"], ["pb5"])
                    for hp in range(2):
                        for hh in range(2):
                            r2 = slice(hh * 64, hh * 64 + 64)
                            self.stt(glS[r2, hp, :], glS[r2, hp, :], Eg[r2, hp, b * 128 + cc * 64 + 63:b * 128 + cc * 64 + 64],
                                     self.PB[5][r2, hp * 256 + hh * 128:hp * 256 + (hh + 1) * 128], ALU.mult, ALU.add,
                                     ["glS", "g_Eg", "pb5"], ["glS"])
                if GS <= 7:
                    continue
                self.act(sq[:], po[:], AF.Square, ["pb4"], ["g_sq"])
                self.mm(self.PB[6][:], self.on_128[:], sq[:], True, True, ["on_128", "g_sq"], ["pb6"])
                self.act(rs[:], self.PB[6][:], AF.Sqrt, ["pb6", "c_eps"], ["g_rs"], bias=self.c_eps[:, 0:1])
                self.recip(rs[:], rs[:], ["g_rs"], ["g_rs"])
                self.tt(ot[:], po[:], rs[:], ALU.mult, ["pb4", "g_rs"], ["g_ot"])
                for h in range(4):
                    self.stt(self.yT[:, 4 + h, bsl], ot[:, h * 128:(h + 1) * 128], self.vec[:, gn + h:gn + h + 1], sgo[:, h, bsl],
                             ALU.mult, ALU.mult, ["g_ot", "vec", "g_sgo"], ["yT"])

    def mix_rwkv(self, slot, sub):
        TS, NT = self.TS, self.NT
        S = self.S
        C0 = float(np.exp(-0.5))
        V = lambda n, c: self.vec[:, VOFF[n] + c:VOFF[n] + c + 1]
        with contextlib.ExitStack() as st:
            wr = [self.sb(st, "r_w%d" % i, [128, 8, 128], BF16) for i in range(14)]
            for i in range(14):
                self.dma(wr[i][:].rearrange("p a b -> p (a b)"), self.w_in[slot, i], [], ["r_w%d" % i], eng="pool")
            w2p = self.sb(st, "r_w2p", [128, 512], F32)
            a2p = self.sb(st, "r_a2p", [128, 512], F32)
            g2 = self.sb(st, "r_g2", [128, 512], F32)
            v1 = self.sb(st, "r_v1", [128, 4, 32], F32)
            v2 = self.sb(st, "r_v2", [32, 512], F32)
            omk = self.sb(st, "r_omk", [128, 4], F32)
            self.memset(w2p[:], 0.0, ["r_w2p"])
            self.memset(a2p[:], 0.0, ["r_a2p"])
            self.dma(w2p[0:64, :], self.w_w2[slot], [], ["r_w2p"])
            self.dma(a2p[64:128, :], self.w_a2[slot], [], ["r_a2p"])
            self.dma(g2[:], self.w_g2[slot], [], ["r_g2"])
            self.dma(v1[:].rearrange("p a b -> p (a b)"), self.w_v1[slot], [], ["r_v1"])
            self.dma(v2[:], self.w_v2[slot], [], ["r_v2"])
            ka = VOFF["rwkv_k_a"]
            self.ts(omk[:], self.vec[:, ka:ka + 4], -1.0, 1.0, ALU.mult, ALU.add, ["vec"], ["r_omk"])
            nb = lambda n, sh=(128, 512), dt=F32: self.sb(st, n, list(sh), dt)
            ub = nb("r_ub", (128, 513))
            lor1, lor2 = nb("r_lor1"), nb("r_lor2")
            uvall = nb("r_uvall", (128, 4, 512))
            t1T = nb("r_t1T", (32, 512))
            ur, uk, a_, g_, sig = [nb("r_" + n) for n in ("ur", "uk", "a", "g", "sig")]
            Eg, Eng, kk, kh, KKt, Kt, Bt, Rt, kdec, bon, tmp, tmp2, vfb = [
                nb("r_" + n) for n in ("Eg", "Eng", "kk", "kh", "KKt", "Kt", "Bt", "Rt", "kdec", "bon", "tmp", "tmp2", "vfb")]
            cs = tmp2
            bdec = Eng
            Ysb = kdec
            egl = nb("r_egl", (128, 8))
            Ktp, Btp, KKp = [nb("r_" + n, (128, 2, 128)) for n in ("Ktp", "Btp", "KKp")]
            MW = nb("r_MW", (128, 2, 4, 128))
            Ab = [nb("r_A%d" % i, (128, 2, 128)) for i in range(2)]
            ATb = [nb("r_AT%d" % i, (128, 2, 128)) for i in range(2)]
            IAT = nb("r_IAT", (128, 2, 128))
            Tb = [nb("r_T%d" % i, (128, 2, 128)) for i in range(2)]
            v_tok = nb("r_vtok", (128, 128))
            vpad = nb("r_vpad", (128, 384))
            SApad = nb("r_SApad", (128, 384))
            kdec2, bdec2 = nb("r_kdec2", (128, 2, 128)), nb("r_bdec2", (128, 2, 128))
            negX, SA = nb("r_negX", (128, 128)), nb("r_SA", (128, 128))
            mask4 = nb("r_mask4", (128, 4, 128))
            id2 = nb("r_id2", (128, 2, 128))
            for t_, k_ in ((Ktp, "r_Ktp"), (Btp, "r_Btp"), (KKp, "r_KKp"), (kdec2, "r_kdec2"), (bdec2, "r_bdec2")):
                self.memset(t_[:].rearrange("p a b -> p (a b)"), 0.0, [k_])
            for t_, k_ in ((vpad, "r_vpad"), (SApad, "r_SApad"), (negX, "r_negX"), (SA, "r_SA")):
                self.memset(t_[:], 0.0, [k_])
            for q_, m_ in ((0, self.mstr), (1, self.mpast), (2, self.mstr), (3, self.mpast)):
                self.cp(mask4[:, q_, :], m_[:, 0, :], ["mstr", "mpast"], ["r_mask4"], eng="pool")
            for hh in range(2):
                self.cp(id2[:, hh, :], self.ident[:], ["ident"], ["r_id2"], eng="pool")
            rwS = self.rwS
            pbi = [0]

            def bank():
                k = pbi[0] % 4
                pbi[0] += 1
                return self.PB[k], "pb%d" % k

            def shift_proj(ci, tsl, out, okey):
                ps, pk = bank()
                self.proj(ps[:], pk, wr[ci], "r_w%d" % ci, tsl)
                self.cp(ub[:, 1:513], ps[:], [pk], ["r_ub"], eng="act_copy")
                self.cp(ub[:, 0:1], self.uprev[:, ci:ci + 1], ["uprev"], ["r_ub"], eng="pool")
                self.tt(out, ub[:, 0:512], ub[:, 1:513], ALU.subtract, ["r_ub"], [okey])
                self.stt(out, out, V("rwkv_mu", ci), ub[:, 1:513], ALU.mult, ALU.add, [okey, "vec", "r_ub"], [okey])
                self.cp(self.uprev[:, ci:ci + 1], ub[:, 512:513], ["r_ub"], ["uprev"], eng="pool")

            vf_src = self.vf_in if slot == 0 else self.vf_out
            fl0 = self.flg[:, 2 * slot:2 * slot + 1]
            import os
            RS = float(os.environ.get("R_STOP", "99"))
            for nt in range(NT):
                tsl = slice(nt * 512, (nt + 1) * 512)
                shift_proj(12, tsl, tmp[:], "r_tmp")
                self.act(lor1[0:64, :], tmp[0:64, :], AF.Tanh, ["r_tmp"], ["r_lor1"])
                self.cp(lor1[64:128, :], tmp[64:128, :], ["r_tmp"], ["r_lor1"])
                shift_proj(13, tsl, tmp[:], "r_tmp")
                self.act(lor2[:], tmp[:], AF.Sigmoid, ["r_tmp"], ["r_lor2"])
                for hp in range(4):
                    shift_proj(8 + hp, tsl, uvall[:, hp, :], "r_uvall")
                ps, pk = bank()
                for c in range(4):
                    self.mm(ps[0:32, :], v1[:, c, :], uvall[:, c, :], c == 0, c == 3, ["r_v1", "r_uvall"], [pk])
                self.cp(t1T[:], ps[0:32, :], [pk], ["r_t1T"])
                if RS <= 1:
                    continue
                for hp in range(4):
                    hs = slice(hp * 128, (hp + 1) * 128)
                    uv = uvall[:, hp, :]
                    shift_proj(hp, tsl, ur[:], "r_ur")
                    shift_proj(4 + hp, tsl, uk[:], "r_uk")
                    ps, pk = bank()
                    self.mm(ps[:], v2[0:32, hs], t1T[0:32, :], True, True, ["r_v2", "r_t1T"], [pk])
                    self.act(tmp[:], ps[:], AF.Sigmoid, [pk, "vec"], ["r_tmp"], bias=V("rwkv_v0", hp))
                    self.dma(vfb[:], vf_src[sub, :, hp, tsl], ["vfdram"], ["r_vfb"])
                    self.tt(tmp2[:], vfb[:], uv, ALU.subtract, ["r_vfb", "r_uvall"], ["r_tmp2"])
                    self.tt(tmp2[:], tmp2[:], tmp[:], ALU.mult, ["r_tmp2", "r_tmp"], ["r_tmp2"])
                    self.tt(uv, uv, tmp2[:], ALU.add, ["r_uvall", "r_tmp2"], ["r_uvall"])
                    self.tt(tmp2[:], uv, vfb[:], ALU.subtract, ["r_vfb", "r_uvall"], ["r_tmp2"])
                    self.stt(vfb[:], tmp2[:], fl0, vfb[:], ALU.mult, ALU.add, ["r_tmp2", "flg", "r_vfb"], ["r_vfb"])
                    self.dma(self.vf_out[sub, :, hp, tsl], vfb[:], ["r_vfb"], ["vfdram"])
                    if RS <= 2:
                        continue
                    ps, pk = bank()
                    self.mm(ps[:], w2p[:, hs], lor1[:], True, True, ["r_w2p", "r_lor1"], [pk])
                    self.act(sig[:], ps[:], AF.Sigmoid, [pk, "vec"], ["r_sig"], bias=V("rwkv_w0", hp))
                    ps, pk = bank()
                    self.mm(ps[:], a2p[:, hs], lor1[:], True, True, ["r_a2p", "r_lor1"], [pk])
                    self.act(a_[:], ps[:], AF.Sigmoid, [pk, "vec"], ["r_a"], bias=V("rwkv_a0", hp))
                    ps, pk = bank()
                    self.mm(ps[:], g2[:, hs], lor2[:], True, True, ["r_g2", "r_lor2"], [pk])
                    self.cp(g_[:], ps[:], [pk], ["r_g"], eng="act_copy")
                    S.add("dve", (lambda: (lambda e: e.tensor_tensor_scan(out=cs[:], data0=self.smask[:], data1=sig[:], initial=0.0,
                                                                         op0=ALU.mult, op1=ALU.add)))(), ["smask", "r_sig"], ["r_tmp2"])
                    self.tt(KKt[:], cs[:], sig[:], ALU.subtract, ["r_tmp2", "r_sig"], ["r_KKt"], eng="pool")
                    self.act(Eg[:], cs[:], AF.Exp, ["r_tmp2"], ["r_Eg"], scale=-C0)
                    self.act(Eng[:], cs[:], AF.Exp, ["r_tmp2"], ["r_Eng"], scale=C0)
                    self.act(KKt[:], KKt[:], AF.Exp, ["r_KKt"], ["r_KKt"], scale=-C0)
                    self.cp(egl[:], Eg[:].rearrange("p (c s) -> p c s", s=64)[:, :, 63], ["r_Eg"], ["r_egl"], eng="pool")
                    self.ts(kk[:], uk[:], V("rwkv_k_k", hp), None, ALU.mult, None, ["r_uk", "vec"], ["r_kk"])
                    self.act(tmp[:], kk[:], AF.Square, ["r_kk"], ["r_tmp"])
                    ps, pk = bank()
                    self.mm(ps[:], self.blk[:], tmp[:], True, True, ["blk", "r_tmp"], [pk])
                    self.act(tmp[:], ps[:], AF.Sqrt, [pk], ["r_tmp"])
                    self.ts(tmp[:], tmp[:], 1e-12, None, ALU.max, None, ["r_tmp"], ["r_tmp"])
                    self.recip(tmp[:], tmp[:], ["r_tmp"], ["r_tmp"])
                    self.tt(kk[:], kk[:], tmp[:], ALU.mult, ["r_kk", "r_tmp"], ["r_kk"])
                    self.ts(tmp[:], a_[:], V("rwkv_k_a", hp), omk[:, hp:hp + 1], ALU.mult, ALU.add, ["r_a", "vec", "r_omk"], ["r_tmp"])
                    self.tt(kh[:], uk[:], tmp[:], ALU.mult, ["r_uk", "r_tmp"], ["r_kh"])
                    self.tt(tmp[:], ur[:], kh[:], ALU.mult, ["r_ur", "r_kh"], ["r_tmp"])
                    self.ts(tmp[:], tmp[:], V("rwkv_r_k", hp), None, ALU.mult, None, ["r_tmp", "vec"], ["r_tmp"])
                    ps, pk = bank()
                    self.mm(ps[:], self.blk[:], tmp[:], True, True, ["blk", "r_tmp"], [pk])
                    self.tt(bon[:], ps[:], uv, ALU.mult, [pk, "r_uvall"], ["r_bon"])
                    self.tt(KKt[:], KKt[:], kk[:], ALU.mult, ["r_KKt", "r_kk"], ["r_KKt"])
                    self.tt(Kt[:], kh[:], Eng[:], ALU.mult, ["r_kh", "r_Eng"], ["r_Kt"])
                    self.tt(tmp[:], kk[:], a_[:], ALU.mult, ["r_kk", "r_a"], ["r_tmp"], eng="pool")
                    self.tt(Bt[:], tmp[:], Eng[:], ALU.mult, ["r_tmp", "r_Eng"], ["r_Bt"])
                    self.tt(Rt[:], ur[:], Eg[:], ALU.mult, ["r_ur", "r_Eg"], ["r_Rt"])
                    for ch in range(8):
                        cs_ = slice(ch * 64, ch * 64 + 64)
                        self.ts(kdec[:, cs_], Kt[:, cs_], egl[:, ch:ch + 1], None, ALU.mult, None, ["r_Kt", "r_egl"], ["r_kdec"])
                        self.ts(bdec[:, cs_], Bt[:, cs_], egl[:, ch:ch + 1], None, ALU.mult, None, ["r_Bt", "r_egl"], ["r_Eng"], eng="pool")
                    if RS <= 3:
                        continue
                    PY, pyk = self.PB[7], "pb7"
                    for b in range(4):
                        bl = slice(b * 128, (b + 1) * 128)
                        PT, ptk = self.PB[4], "pb4"
                        for q_, src, sk in ((0, uv, "r_uvall"), (1, kdec[:], "r_kdec"), (2, bdec[:], "r_Eng")):
                            S.add("pe", (lambda o, i_: (lambda e: e.transpose(o, i_, self.ident[:])))(PT[:, q_ * 128:(q_ + 1) * 128], src[:, bl]),
                                  [sk, "ident"], [ptk])
                        self.cp(v_tok[:], PT[:, 0:128], [ptk], ["r_vtok"], eng="act_copy")
                        self.cp(vpad[:].rearrange("p (a b) -> p a b", b=192)[:, :, 0:64], PT[:, 0:128].rearrange("p (a b) -> p a b", b=64),
                                [ptk], ["r_vpad"])
                        for cc in range(2):
                            crow = slice(cc * 64, cc * 64 + 64)
                            self.cp(kdec2[crow, cc, :], PT[crow, 128:256], [ptk], ["r_kdec2"], eng="act_copy")
                            self.cp(bdec2[crow, cc, :], PT[crow, 256:384], [ptk], ["r_bdec2"])
                        if RS <= 3.2:
                            continue
                        for hh in range(2):
                            r2 = slice(hh * 64, hh * 64 + 64)
                            self.cp(Ktp[r2, hh, :], Kt[r2, bl], ["r_Kt"], ["r_Ktp"], eng="pool")
                            self.cp(Btp[r2, hh, :], Bt[r2, bl], ["r_Bt"], ["r_Btp"], eng="pool")
                            self.cp(KKp[r2, hh, :], KKt[r2, bl], ["r_KKt"], ["r_KKp"], eng="pool")
                        if RS <= 3.5:
                            continue
                        for hh in range(2):
                            PM, pmk = self.PB[5 + hh], "pb%d" % (5 + hh)
                            self.mm(PM[:, 0:128], Ktp[:, hh, :], KKt[:, bl], True, True, ["r_Ktp", "r_KKt"], [pmk])
                            self.mm(PM[:, 128:256], Ktp[:, hh, :], Rt[:, bl], True, True, ["r_Ktp", "r_Rt"], [pmk])
                            self.mm(PM[:, 256:384], Btp[:, hh, :], KKt[:, bl], True, True, ["r_Btp", "r_KKt"], [pmk])
                            self.mm(PM[:, 384:512], Btp[:, hh, :], Rt[:, bl], True, True, ["r_Btp", "r_Rt"], [pmk])
                            self.tt(MW[:, hh, :, :].rearrange("p a b -> p (a b)"), PM[:], mask4[:].rearrange("p a b -> p (a b)"), ALU.mult,
                                    [pmk, "r_mask4"], ["r_MW"])
                        if RS <= 4:
                            continue
                        PA, pak = bank()
                        for hh in range(2):
                            self.mm(PA[:, hh * 128:(hh + 1) * 128], KKp[:, hh, :], Bt[:, bl], True, True, ["r_KKp", "r_Bt"], [pak])
                        A, AT, T = Ab[0], ATb[0], Tb[0]
                        self.tt(AT[:].rearrange("p a b -> p (a b)"), PA[:, 0:256], self.mfut[:, 0:2, :].rearrange("p a b -> p (a b)"), ALU.mult,
                                [pak, "mfut"], ["r_AT0"])
                        self.tt(T[:], id2[:], MW[:, :, 2, :], ALU.subtract, ["r_id2", "r_MW"], ["r_T0"])
                        self.cp(A[:], MW[:, :, 2, :], ["r_MW"], ["r_A0"], eng="pool")
                        cur = 0
                        for it in range(5):
                            nx = 1 - cur
                            A, AT, T = Ab[cur], ATb[cur], Tb[cur]
                            An, ATn, Tn = Ab[nx], ATb[nx], Tb[nx]
                            ak, atk, tk = "r_A%d" % cur, "r_AT%d" % cur, "r_T%d" % cur
                            akn, atkn, tkn = "r_A%d" % nx, "r_AT%d" % nx, "r_T%d" % nx
                            P1, p1k = bank()
                            P2, p2k = bank()
                            for hh in range(2):
                                hs2 = slice(hh * 128, (hh + 1) * 128)
                                if it < 4:
                                    self.mm(P1[:, hs2], AT[:, hh, :], A[:, hh, :], True, True, [ak, atk], [p1k])
                                self.mm(P2[:, hs2], A[:, hh, :], AT[:, hh, :], True, True, [ak, atk], [p2k])
                            if it < 4:
                                self.cp(An[:].rearrange("p a b -> p (a b)"), P1[:, 0:256], [p1k], [akn], eng="act_copy")
                                self.cp(ATn[:].rearrange("p a b -> p (a b)"), P2[:, 0:256], [p2k], [atkn], eng="act_copy")
                            self.tt(IAT[:].rearrange("p a b -> p (a b)"), P2[:, 0:256], id2[:].rearrange("p a b -> p (a b)"), ALU.add,
                                    [p2k, "r_id2"], ["r_IAT"])
                            P3, p3k = bank()
                            for hh in range(2):
                                self.mm(P3[:, hh * 128:(hh + 1) * 128], IAT[:, hh, :], T[:, hh, :], True, True, ["r_IAT", tk], [p3k])
                            self.cp(Tn[:].rearrange("p a b -> p (a b)"), P3[:, 0:256], [p3k], [tkn])
                            cur = nx
                        T, tk = Tb[cur], "r_T%d" % cur
                        if RS <= 5:
                            continue
                        for cc in range(2):
                            crow = slice(cc * 64, cc * 64 + 64)
                            cl = slice(b * 128 + cc * 64, b * 128 + cc * 64 + 64)
                            ccol = slice(cc * 64, cc * 64 + 64)
                            PX, pxk = bank()
                            self.mm(PX[:, 0:128], KKt[:, bl], rwS[:, hp, :], True, False, ["r_KKt", "rwS"], [pxk], skip=True)
                            for hh in range(2):
                                self.mm(PX[:, hh * 64:(hh + 1) * 64], MW[:, hh, 0, :], v_tok[:, hh * 64:(hh + 1) * 64], False, True,
                                        ["r_MW", "r_vtok"], [pxk], skip=True)
                            self.ts(negX[crow, :], PX[crow, 0:128], -1.0, None, ALU.mult, None, [pxk], ["r_negX"])
                            PS_, psk = bank()
                            for hh in range(2):
                                self.mm(PS_[:, hh * 64:(hh + 1) * 64], T[:, hh, :], negX[:, hh * 64:(hh + 1) * 64], True, True,
                                        [tk, "r_negX"], [psk])
                            self.cp(SA[crow, :], PS_[crow, 0:128], [psk], ["r_SA"], eng="act_copy")
                            self.cp(SApad[crow, :].rearrange("p (a b) -> p a b", b=192)[:, :, 0:64],
                                    PS_[crow, 0:128].rearrange("p (a b) -> p a b", b=64), [psk], ["r_SApad"])
                            self.mm(PY[:, cl], rwS[:, hp, :], Rt[:, cl], True, False, ["rwS", "r_Rt"], [pyk], skip=True)
                            for hh in range(2):
                                self.mm(PY[:, cl], SApad[:, hh * 128:(hh + 1) * 128], MW[:, hh, 3, ccol], False, False,
                                        ["r_SApad", "r_MW"], [pyk], skip=True)
                            for hh in range(2):
                                self.mm(PY[:, cl], vpad[:, hh * 128:(hh + 1) * 128], MW[:, hh, 1, ccol], False, hh == 1,
                                        ["r_vpad", "r_MW"], [pyk], skip=True)
                            PU, puk = bank()
                            self.mm(PU[:, 0:128], bdec2[:, cc, :], SA[:], True, False, ["r_bdec2", "r_SA"], [puk])
                            self.mm(PU[:, 0:128], kdec2[:, cc, :], v_tok[:], False, True, ["r_kdec2", "r_vtok"], [puk])
                            for hh in range(2):
                                r2 = slice(hh * 64, hh * 64 + 64)
                                self.stt(rwS[r2, hp, hh * 64:(hh + 1) * 64], rwS[r2, hp, hh * 64:(hh + 1) * 64], egl[r2, b * 2 + cc:b * 2 + cc + 1],
                                         PU[r2, hh * 64:(hh + 1) * 64], ALU.mult, ALU.add, ["rwS", "r_egl", puk], ["rwS"])
                    if RS <= 6:
                        continue
                    self.cp(Ysb[:], PY[:], [pyk], ["r_kdec"], eng="act_copy")
                    ps, pk = bank()
                    self.mm(ps[:], self.blk64[:], Ysb[:], True, True, ["blk64", "r_kdec"], [pk])
                    self.tt(Ysb[:], Ysb[:], ps[:], ALU.subtract, ["r_kdec", pk], ["r_kdec"])
                    self.act(tmp[:], Ysb[:], AF.Square, ["r_kdec"], ["r_tmp"])
                    ps, pk = bank()
                    self.mm(ps[:], self.blk64[:], tmp[:], True, True, ["blk64", "r_tmp"], [pk])
                    self.act(tmp[:], ps[:], AF.Sqrt, [pk, "c_eps"], ["r_tmp"], bias=self.c_eps[:, 1:2])
                    self.recip(tmp[:], tmp[:], ["r_tmp"], ["r_tmp"])
                    self.tt(Ysb[:], Ysb[:], tmp[:], ALU.mult, ["r_kdec", "r_tmp"], ["r_kdec"])
                    self.ts(Ysb[:], Ysb[:], V("rwkv_ln_w", hp), V("rwkv_ln_b", hp), ALU.mult, ALU.add, ["r_kdec", "vec"], ["r_kdec"])
                    self.tt(Ysb[:], Ysb[:], bon[:], ALU.add, ["r_kdec", "r_bon"], ["r_kdec"])
                    self.tt(self.yT[:, hp, tsl], Ysb[:], g_[:], ALU.mult, ["r_kdec", "r_g"], ["yT"])

    def mix_merge(self, slot):
        TS, NT = self.TS, self.NT
        with contextlib.ExitStack() as st:
            mg = self.sb(st, "mg", [128, 8, TS], BF16)
            wbr = [self.sb(st, "mg_wbr%d" % i, [128, 3, 4, 128], BF16) for i in range(2)]
            wo = [self.sb(st, "mg_wo%d" % i, [128, 8, 128], BF16) for i in range(2)]
            gsb = self.sb(st, "mg_g", [128, 512], F32)
            acc = self.sb(st, "mg_acc", [128, 512], F32)
            tmp = self.sb(st, "mg_tmp", [128, 512], F32)
            it = 0
            for dc in range(8):
                wb_, wbk = wbr[dc % 2], "mg_wbr%d" % (dc % 2)
                self.dma(wb_[:].rearrange("p a b c -> p (a b c)"), self.w_br[slot, dc], [], [wbk], eng="pool")
                wg = []
                for j in range(3):
                    wg.append(self.win_tile(slot, 30 + j * 8 + dc, self.wib, self.wcnt))
                for nt in range(NT):
                    tsl = slice(nt * 512, (nt + 1) * 512)
                    for j in range(3):
                        k = it % 2
                        it += 1
                        PG, pgk = self.PB[k], "pb%d" % k
                        PP, ppk = self.PB[2 + k], "pb%d" % (2 + k)
                        self.proj(PG[:], pgk, wg[j][0], wg[j][1], tsl)
                        for c in range(4):
                            self.mm(PP[:], wb_[:, j, c, :], self.yT[:, j * 4 + c, tsl], c == 0, c == 3, [wbk, "yT"], [ppk])
                        self.act(gsb[:], PG[:], AF.Sigmoid, [pgk], ["mg_g"])
                        if j == 0:
                            self.tt(acc[:], PP[:], gsb[:], ALU.mult, [ppk, "mg_g"], ["mg_acc"])
                        elif j == 1:
                            self.tt(tmp[:], PP[:], gsb[:], ALU.mult, [ppk, "mg_g"], ["mg_tmp"])
                            self.tt(acc[:], acc[:], tmp[:], ALU.add, ["mg_acc", "mg_tmp"], ["mg_acc"], eng="pool")
                        else:
                            self.tt(tmp[:], PP[:], gsb[:], ALU.mult, [ppk, "mg_g"], ["mg_tmp"])
                            self.tt(mg[:, dc, tsl], acc[:], tmp[:], ALU.add, ["mg_acc", "mg_tmp"], ["mg"], eng="pool")
            for dc2 in range(8):
                w, wk = wo[dc2 % 2], "mg_wo%d" % (dc2 % 2)
                self.dma(w[:].rearrange("p a b -> p (a b)"), self.w_o[slot, dc2], [], [wk], eng="pool")
                for nt in range(NT):
                    tsl = slice(nt * 512, (nt + 1) * 512)
                    k = 4 + it % 2
                    it += 1
                    PO, pok = self.PB[k], "pb%d" % k
                    for dc in range(8):
                        self.mm(PO[:], w[:, dc, :], mg[:, dc, tsl], dc == 0, dc == 7, [wk, "mg"], [pok])
                    self.tt(self.xT[:, dc2, tsl], self.xT[:, dc2, tsl], PO[:], ALU.add, ["xT", pok], ["xT"])

def _fm(v, ncol):
    return np.ascontiguousarray(np.asarray(v, np.float32).reshape(ncol, 128).T)


def _wtile(w, cols):
    K, C = w.shape
    nk = K // 128
    nt = C // cols
    a = np.asarray(w, np.float32).reshape(nk, 128, nt, cols).transpose(2, 1, 0, 3)
    return np.ascontiguousarray(a).reshape(nt, 128, nk * cols)


def prep_layer(inp, l, final_norm):
    f32 = np.float32
    if l is None:
        z = lambda *s: np.zeros(s, f32)
        vec = z(128, NV)
        vec[:, VOFF["rwkv_v0"]:VOFF["rwkv_v0"] + 4] = -1e4
        vec[:, VOFF["final_norm"]:VOFF["final_norm"] + 8] = _fm(final_norm, 8)
        return dict(vecs=vec, w_f1i=z(NFF, 128, 2048), w_f1o=z(8, 128, NFF * 128), w_f2i=z(NFF, 128, 2048),
                    w_f2o=z(8, 128, NFF * 128), w_in=z(54, 128, 1024), w_al=z(128, 128), w_kv=z(128, 8192),
                    w_br=z(8, 128, 1536), w_o=z(8, 128, 1024), w_w2=z(64, 512), w_a2=z(64, 512), w_g2=z(128, 512),
                    w_v1=z(128, 128), w_v2=z(32, 512), w_aup=z(16, 256))
    vec = np.zeros((128, NV), f32)

    def put(name, v, n):
        vec[:, VOFF[name]:VOFF[name] + n] = _fm(v, n)
    put("ffn1_norm", inp["ffn1_norm"][l], 8)
    put("mix_norm", inp["mix_norm"][l], 8)
    put("mem_norm", inp["mem_norm"][l], 8)
    put("ffn2_norm", inp["ffn2_norm"][l], 8)
    put("final_norm", final_norm, 8)
    put("rwkv_mu", inp["rwkv_mu"][l], 14)
    for n in ("rwkv_w0", "rwkv_a0", "rwkv_k_k", "rwkv_k_a", "rwkv_ln_w", "rwkv_ln_b"):
        put(n, inp[n][l], 4)
    put("rwkv_r_k", np.asarray(inp["rwkv_r_k"][l]).reshape(-1), 4)
    if l == 0:
        vec[:, VOFF["rwkv_v0"]:VOFF["rwkv_v0"] + 4] = -1e4
        v1 = np.zeros((512, 32), f32)
        v2 = np.zeros((32, 512), f32)
    else:
        put("rwkv_v0", inp["rwkv_v0"][l - 1], 4)
        v1 = np.asarray(inp["rwkv_v1"][l - 1], f32)
        v2 = np.asarray(inp["rwkv_v2"][l - 1], f32)
    for j in range(4):
        put("conv%d" % j, inp["gla_conv"][l][j], 8)
    put("gla_a_bias", inp["gla_a_bias"][l], 2)
    put("gla_norm", inp["gla_norm"][l], 4)

    def ffn_in(w):
        w = np.asarray(w, f32)
        g = _wtile(w[:, :DFF], 128).reshape(NFF, 128, 8, 128)
        u = _wtile(w[:, DFF:], 128).reshape(NFF, 128, 8, 128)
        return np.ascontiguousarray(np.concatenate([g, u], axis=3)).reshape(NFF, 128, 2048)

    def ffn_out(w):
        return _wtile(np.asarray(w, f32), 128)

    win = np.asarray(inp["w_in"][l], f32)
    full = np.concatenate([win[:, 0:2816], win[:, 2832:]], axis=1)
    wbr = np.asarray(inp["w_branch"][l], f32)
    br = np.stack([_wtile(wbr[j], 128).reshape(8, 128, 4 * 128) for j in range(3)], axis=2)
    return dict(
        vecs=vec,
        w_f1i=ffn_in(inp["ffn1_w_in"][l]), w_f1o=ffn_out(inp["ffn1_w_out"][l]),
        w_f2i=ffn_in(inp["ffn2_w_in"][l]), w_f2o=ffn_out(inp["ffn2_w_out"][l]),
        w_in=_wtile(full, 128), w_al=_wtile(win[:, 2816:2832], 16)[0],
        w_kv=_wtile(np.asarray(inp["xa_w_kv"][l], f32), 1024)[0],
        w_br=np.ascontiguousarray(br).reshape(8, 128, 1536),
        w_o=_wtile(np.asarray(inp["w_out"][l], f32), 128),
        w_w2=np.asarray(inp["rwkv_w2"][l], f32), w_a2=np.asarray(inp["rwkv_a2"][l], f32),
        w_g2=np.asarray(inp["rwkv_g2"][l], f32), w_v1=_wtile(v1, 32)[0], w_v2=v2,
        w_aup=np.asarray(inp["gla_a_up"][l], f32))


def x_to_fm(x, n_sub, TS):
    a = np.asarray(x, np.float32).reshape(n_sub, TS, 8, 128).transpose(0, 3, 2, 1)
    return np.ascontiguousarray(a)


def fm_to_x(a):
    n_sub, _, _, TS = a.shape
    return np.ascontiguousarray(a.transpose(0, 3, 2, 1)).reshape(n_sub * TS, 1024)


_PROG = {}


def _get_prog(key, **kw):
    if key not in _PROG:
        b = B(**kw)
        b.build()
        _PROG[key] = b
    return _PROG[key]


def _in_map(L, x_fm, vf, st_in, fl, memT):
    m = {k: v[None] for k, v in L.items()}
    m.update(x_in=x_fm, vf_in=vf, st_in=st_in, flags=fl, memT=memT)
    return m


def kernel(**inp):
    inp = {k: np.asarray(v) for k, v in inp.items()}
    x, mem = inp["x"], inp["mem"]
    Bn, SEQ, _ = x.shape
    NSUB, TS, NSL = 2, 1024, 5
    HALF = NSUB * TS
    assert SEQ == 2 * HALF and Bn == 4
    return _run_fused(inp, x, mem, Bn, NSUB, TS, NSL)


def _run_fused(inp, x, mem, Bn, NSUB, TS, NSL, nlayers=4):
    HALF = NSUB * TS
    fn = inp["final_norm"]
    layers = [prep_layer(inp, l, fn) for l in range(nlayers)]
    dummy = prep_layer(inp, None, fn)
    ncores = 2 * Bn
    prog = _get_prog(("fused", ncores, NSUB, TS, NSL), n_slots=NSL, n_sub=NSUB, TS=TS, norm_after=[NSL - 2, NSL - 1], n_cores=ncores)
    maps = []
    for c in range(ncores):
        b, half = c // 2, c % 2
        ls = [(k if half == 0 else k - 1) for k in range(NSL)]
        Ls = [layers[l] if 0 <= l < nlayers else dummy for l in ls]
        m = {key: np.stack([L[key] for L in Ls]) for key in dummy}
        fl = np.zeros((128, 2 * NSL), np.float32)
        for k, l in enumerate(ls):
            fl[:, 2 * k] = 1.0 if l == 0 else 0.0
            fl[:, 2 * k + 1] = 1.0 if half == 1 else 0.0
        m.update(x_in=x_to_fm(x[b, half * HALF:(half + 1) * HALF], NSUB, TS),
                 vf_in=np.zeros((NSUB, 128, 4, TS), np.float32), st_in=np.zeros((128, NST), np.float32), flags=fl,
                 memT=np.ascontiguousarray(mem[b].reshape(NMEM, 8, 128).transpose(2, 1, 0)).astype(np.float32))
        maps.append(m)
    res = run_bass_kernel_spmd(prog.nc, maps, core_ids=list(range(ncores))).results
    out = np.zeros((Bn, 2 * HALF, D), np.float32)
    for c in range(ncores):
        b, half = c // 2, c % 2
        y = np.asarray(res[c]["y_out"])[half]
        out[b, half * HALF:(half + 1) * HALF] = fm_to_x(y)
    return out


def kernel_unfused(**inp):
    inp = {k: np.asarray(v) for k, v in inp.items()}
    x, mem = inp["x"], inp["mem"]
    Bn, SEQ, _ = x.shape
    NSUB, TS = 2, 1024
    HALF = NSUB * TS
    assert SEQ == 2 * HALF and Bn == 4
    fn = inp["final_norm"]
    layers = [prep_layer(inp, l, fn) for l in range(4)]
    dummy = prep_layer(inp, None, fn)
    prog = _get_prog("slot1", n_slots=1, n_sub=NSUB, TS=TS, norm_after=[0])
    ncores = 8
    memT = [np.ascontiguousarray(mem[b].reshape(NMEM, 8, 128).transpose(2, 1, 0)).astype(np.float32) for b in range(Bn)]
    xs = [x_to_fm(x[c // 2, (c % 2) * HALF:(c % 2 + 1) * HALF], NSUB, TS) for c in range(ncores)]
    vfs = [np.zeros((NSUB, 128, 4, TS), np.float32) for _ in range(ncores)]
    st_prev = [np.zeros((128, NST), np.float32) for _ in range(Bn)]
    ys = [None] * ncores
    for k in range(5):
        maps = []
        for c in range(ncores):
            b, half = c // 2, c % 2
            l = k if half == 0 else k - 1
            L = layers[l] if 0 <= l <= 3 else dummy
            fl = np.zeros((128, 2), np.float32)
            fl[:, 0] = 1.0 if l == 0 else 0.0
            fl[:, 1] = 1.0 if half == 1 else 0.0
            maps.append(_in_map(L, xs[c], vfs[c], st_prev[b], fl, memT[b]))
        res = run_bass_kernel_spmd(prog.nc, maps, core_ids=list(range(ncores))).results
        for c in range(ncores):
            b, half = c // 2, c % 2
            l = k if half == 0 else k - 1
            xs[c] = np.asarray(res[c]["x_out"])
            vfs[c] = np.asarray(res[c]["vf_out"])
            if l == 3:
                ys[c] = np.asarray(res[c]["y_out"])[0]
        for c in range(0, ncores, 2):
            st_prev[c // 2] = np.asarray(res[c]["st_out"])
    out = np.zeros((Bn, SEQ, D), np.float32)
    for c in range(ncores):
        out[c // 2, (c % 2) * HALF:(c % 2 + 1) * HALF] = fm_to_x(ys[c])
    return out
```

```python
import contextlib
import numpy as np
import concourse.bass as bass
import concourse.mybir as mybir
from concourse.bass_utils import run_bass_kernel_spmd

F32 = mybir.dt.float32
BF16 = mybir.dt.bfloat16
AF = mybir.ActivationFunctionType
ALU = mybir.AluOpType

D = 1024
DFF = 2816
NFF = DFF // 128
NMEM = 256
EPS = 1e-6
LN_EPS = 64e-5
WCOLS = 6928

ENGS = ("pe", "act", "dve", "pool", "sp")
EPOCH = 8000
NDMASEM = 44
DMASEM_RANGE = {"sp": (0, 16), "pool": (16, 16), "act": (32, 8), "cc": (40, 4)}


class Sched:
    def __init__(self, nc):
        self.nc = nc
        self.ops = {e: [] for e in ENGS}
        self.last_write = {}
        self.rd_e = {}
        self.rd_d = {}
        self.seen = {e: {} for e in ENGS}
        self.dma_seen = {e: {} for e in ENGS}
        self.dma_rr_e = {}
        self.dma_val = [0] * NDMASEM
        self.dma_last = [None] * NDMASEM
        self.phase_tok = None

    def add(self, eng, fn, reads=(), writes=(), dma=False, cc=False):
        ops = self.ops[eng]
        idx = len(ops)
        deps = set()
        pbr = [k for k in reads if k.startswith("pb") and k not in writes]
        if pbr:
            writes = list(writes) + pbr
        if self.phase_tok is not None:
            deps.add(self.phase_tok)
        for b in reads:
            lw = self.last_write.get(b)
            if lw is not None:
                deps.add(lw)
        for b in writes:
            lw = self.last_write.get(b)
            if lw is not None:
                deps.add(lw)
            for e2, i2 in self.rd_e.get(b, {}).items():
                deps.add(("e", e2, i2))
            for tok in self.rd_d.get(b, ()):
                deps.add(tok)
        op = dict(fn=fn, waits=[], inc=False, dma=None)
        if dma or cc:
            rk = "cc" if cc else eng
            base, cnt = DMASEM_RANGE[rk]
            s = base + self.dma_rr_e.get(rk, 0)
            self.dma_rr_e[rk] = (self.dma_rr_e.get(rk, 0) + 1) % cnt
            if self.dma_last[s] is not None:
                deps.add(self.dma_last[s])
            amt = 1 if cc else 16
            self.dma_val[s] += amt
            tok = ("d", s, self.dma_val[s])
            self.dma_last[s] = tok
            op["dma"] = (s, self.dma_val[s], amt)
        else:
            tok = ("e", eng, idx)
        best = {}
        for d in deps:
            if d[0] == "e":
                _, se, si = d
                if se == eng and (eng == "pe" or si >= idx):
                    continue
                if self.seen[eng].get(se, -1) >= si:
                    continue
                if best.get(("e", se), -1) < si:
                    best[("e", se)] = si
            else:
                _, s, v = d
                if self.dma_seen[eng].get(s, 0) >= v:
                    continue
                if best.get(("d", s), 0) < v:
                    best[("d", s)] = v
        for k, v in best.items():
            if k[0] == "e":
                self.seen[eng][k[1]] = v
                self.ops[k[1]][v]["inc"] = True
                op["waits"].append(("e", k[1], v))
            else:
                self.dma_seen[eng][k[1]] = v
                op["waits"].append(("d", k[1], v))
        ops.append(op)
        for b in reads:
            if tok[0] == "e":
                m = self.rd_e.setdefault(b, {})
                m[eng] = idx
            else:
                self.rd_d.setdefault(b, set()).add(tok)
        for b in writes:
            self.last_write[b] = tok
            self.rd_e[b] = {}
            self.rd_d[b] = set()
        return tok

    def barrier(self, scratch_ap):
        deps = []
        for e in ENGS:
            if self.ops[e]:
                deps.append(e)
        keys_w = ["__barrier__"]
        idx = len(self.ops["dve"])
        op = dict(fn=lambda e: e.memset(scratch_ap, 0.0), waits=[], inc=True, dma=None)
        for e in ENGS:
            if not self.ops[e]:
                continue
            li = len(self.ops[e]) - 1
            while li >= 0 and self.ops[e][li]["dma"] is not None:
                li -= 1
            if li >= 0 and self.seen["dve"].get(e, -1) < li:
                self.seen["dve"][e] = li
                self.ops[e][li]["inc"] = True
                op["waits"].append(("e", e, li))
        for s in range(NDMASEM):
            if self.dma_last[s] is not None and self.dma_seen["dve"].get(s, 0) < self.dma_val[s]:
                self.dma_seen["dve"][s] = self.dma_val[s]
                op["waits"].append(("d", s, self.dma_val[s]))
        self.ops["dve"].append(op)
        self.phase_tok = ("e", "dve", idx)
        self.last_write = {}
        self.rd_e = {}
        self.rd_d = {}

    def emit(self, final_waits=()):
        nc = self.nc
        for t in final_waits:
            if t[0] == "e":
                self.ops[t[1]][t[2]]["inc"] = True
        with contextlib.ExitStack() as st:
            esems = {}
            counts = {}
            for e in ENGS:
                c = 0
                cl = []
                for op in self.ops[e]:
                    if op["inc"] and op["dma"] is None:
                        c += 1
                    cl.append(c)
                counts[e] = cl
                nep = max(0, c - 1) // EPOCH + 1
                esems[e] = [st.enter_context(nc.semaphore(f"s_{e}_{k}")) for k in range(nep)]
            dsems = [st.enter_context(nc.semaphore(f"s_dma_{k}")) for k in range(NDMASEM)]
            block = st.enter_context(nc.Block())

            def split(c):
                ep = (c - 1) // EPOCH
                return ep, c - ep * EPOCH

            def do_wait(engine, w):
                if w[0] == "e":
                    ep, r = split(counts[w[1]][w[2]])
                    engine.wait_ge(esems[w[1]][ep], r)
                else:
                    engine.wait_ge(dsems[w[1]], w[2])

            def run(e):
                def body(engine):
                    for i, op in enumerate(self.ops[e]):
                        for w in op["waits"]:
                            do_wait(engine, w)
                        ins = op["fn"](engine)
                        if op["dma"] is not None:
                            ins.then_inc(dsems[op["dma"][0]], op["dma"][2])
                        elif op["inc"]:
                            ep, r = split(counts[e][i])
                            ins.then_inc(esems[e][ep], 1)
                    if e == "sp":
                        for w in final_waits:
                            do_wait(engine, w)
                return body

            block.tensor(run("pe"))
            block.scalar(run("act"))
            block.vector(run("dve"))
            block.gpsimd(run("pool"))
            block.sync(run("sp"))


VEC_SPEC = [("ffn1_norm", 8), ("mix_norm", 8), ("mem_norm", 8), ("ffn2_norm", 8), ("final_norm", 8),
            ("rwkv_mu", 14), ("rwkv_w0", 4), ("rwkv_a0", 4), ("rwkv_k_k", 4), ("rwkv_k_a", 4),
            ("rwkv_r_k", 4), ("rwkv_ln_w", 4), ("rwkv_ln_b", 4), ("rwkv_v0", 4),
            ("conv0", 8), ("conv1", 8), ("conv2", 8), ("conv3", 8), ("gla_a_bias", 2), ("gla_norm", 4)]
VOFF = {}
_o = 0
for _n, _c in VEC_SPEC:
    VOFF[_n] = _o
    _o += _c
NV = _o
NST = 512 + 256 + 14 + 24


class B:
    def __init__(self, n_slots, n_sub, TS, norm_after, stages=("ffn1", "mix", "ffn2"), dbg=(), n_cores=8):
        self.n_cores = n_cores
        self.nc = nc = bass.Bass("TRN2", target_bir_lowering=False)
        self.S = Sched(nc)
        self.NSL, self.NSUB, self.TS = n_slots, n_sub, TS
        self.NT = TS // 512
        self.stages = stages
        self.norm_after = list(norm_after)
        self.dbg = dbg
        self.dbg_out = {}
        self.mix_parts = ("xa", "gla", "rwkv", "merge")
        NSL = n_slots
        din = lambda n, sh: nc.dram_tensor(n, sh, F32, kind="ExternalInput").ap()
        dout = lambda n, sh: nc.dram_tensor(n, sh, F32, kind="ExternalOutput").ap()
        self.x_in = din("x_in", [n_sub, 128, 8, TS])
        self.x_out = dout("x_out", [n_sub, 128, 8, TS])
        self.y_out = dout("y_out", [max(1, len(self.norm_after)), n_sub, 128, 8, TS])
        self.memT = din("memT", [128, 8, NMEM])
        self.vecs = din("vecs", [NSL, 128, NV])
        self.flags = din("flags", [128, 2 * NSL])
        self.st_in = din("st_in", [128, NST])
        self.st_out = dout("st_out", [128, NST])
        self.vf_in = din("vf_in", [n_sub, 128, 4, TS])
        self.vf_out = dout("vf_out", [n_sub, 128, 4, TS])
        self.w_f1i = din("w_f1i", [NSL, NFF, 128, 8 * 256])
        self.w_f1o = din("w_f1o", [NSL, 8, 128, NFF * 128])
        self.w_f2i = din("w_f2i", [NSL, NFF, 128, 8 * 256])
        self.w_f2o = din("w_f2o", [NSL, 8, 128, NFF * 128])
        self.w_in = din("w_in", [NSL, 54, 128, 8 * 128])
        self.w_al = din("w_al", [NSL, 128, 8 * 16])
        self.w_kv = din("w_kv", [NSL, 128, 8 * 1024])
        self.w_br = din("w_br", [NSL, 8, 128, 3 * 4 * 128])
        self.w_o = din("w_o", [NSL, 8, 128, 8 * 128])
        self.w_w2 = din("w_w2", [NSL, 64, 512])
        self.w_a2 = din("w_a2", [NSL, 64, 512])
        self.w_g2 = din("w_g2", [NSL, 128, 512])
        self.w_v1 = din("w_v1", [NSL, 128, 4 * 32])
        self.w_v2 = din("w_v2", [NSL, 32, 512])
        self.w_aup = din("w_aup", [NSL, 16, 256])
        self.uid = 0
        self.st_snd = nc.dram_tensor("st_snd", [128, NST], F32).ap()
        self.st_pair = nc.dram_tensor("st_pair", [256, NST], F32).ap()

    def mm(self, out, lhsT, rhs, start, stop, r, w, skip=False):
        if skip:
            self.S.add("pe", lambda e: e.matmul(out, lhsT, rhs, start=start, stop=stop, skip_group_check=True), r, w)
        else:
            self.S.add("pe", lambda e: e.matmul(out, lhsT, rhs, start=start, stop=stop), r, w)

    def act(self, out, in_, func, r, w, bias=None, scale=None, eng="act"):
        kw = {}
        if bias is not None:
            kw["bias"] = bias
        if scale is not None:
            kw["scale"] = scale
        self.S.add("act", lambda e: e.activation(out=out, in_=in_, func=func, **kw), r, w)

    def tt(self, out, in0, in1, op, r, w, eng="dve"):
        self.S.add(eng, lambda e: e.tensor_tensor(out=out, in0=in0, in1=in1, op=op), r, w)

    def ts(self, out, in0, s1, s2, op0, op1, r, w, eng="dve"):
        if op1 is None:
            self.S.add(eng, lambda e: e.tensor_scalar(out=out, in0=in0, scalar1=s1, scalar2=None, op0=op0), r, w)
        else:
            self.S.add(eng, lambda e: e.tensor_scalar(out=out, in0=in0, scalar1=s1, scalar2=s2, op0=op0, op1=op1), r, w)

    def stt(self, out, in0, scalar, in1, op0, op1, r, w):
        self.S.add("dve", lambda e: e.scalar_tensor_tensor(out=out, in0=in0, scalar=scalar, in1=in1, op0=op0, op1=op1), r, w)

    def cp(self, out, in_, r, w, eng="dve"):
        if eng == "act_copy":
            self.S.add("act", lambda e: e.activation(out=out, in_=in_, func=AF.Copy), r, w)
        else:
            self.S.add(eng, lambda e: e.tensor_copy(out=out, in_=in_), r, w)

    def recip(self, out, in_, r, w):
        self.S.add("dve", lambda e: e.reciprocal(out=out, in_=in_), r, w)

    def dma(self, out, in_, r, w, eng="sp"):
        return self.S.add(eng, lambda e: e.dma_start(out=out, in_=in_), r, w, dma=True)

    def memset(self, ap, val, w, eng="pool"):
        self.S.add(eng, lambda e: e.memset(ap, val), (), w)

    def dump(self, name, ap, shape, key):
        if name not in self.dbg:
            return
        t = self.nc.dram_tensor("dbg_" + name, list(shape), ap.dtype if hasattr(ap, "dtype") else F32, kind="ExternalOutput").ap()
        self.dbg_out[name] = self.dma(t, ap, [key], ["dbg_" + name])

    def sb(self, st, name, shape, dt):
        self.uid += 1
        return st.enter_context(self.nc.sbuf_tensor("%s_%d" % (name, self.uid), list(shape), dt))

    def build(self):
        nc, S, TS, NT = self.nc, self.S, self.TS, self.NT
        with contextlib.ExitStack() as st:
            self.PB = [st.enter_context(nc.psum_tensor(f"pb{i}", [128, 512], F32)) for i in range(8)]
            self.c_scr = self.sb(st, "c_scr", [128, 8], F32)
            self.alloc_consts(st)
            self.vec = self.sb(st, "vec", [128, NV], F32)
            self.flg = self.sb(st, "flg", [128, 2 * self.NSL], F32)
            self.rwS = self.sb(st, "rwS", [128, 4, 128], F32)
            self.glS = self.sb(st, "glS", [128, 2, 128], F32)
            self.glSb = self.sb(st, "glSb", [128, 2, 128], BF16)
            self.uprev = self.sb(st, "uprev", [128, 14], F32)
            self.convh = self.sb(st, "convh", [128, 8, 3], F32)
            self.xT = self.sb(st, "xT", [128, 8, TS], F32)
            self.hT = self.sb(st, "hT", [128, 8, TS], BF16)
            self.init_consts()
            self.dma(self.flg[:], self.flags, [], ["flg"])
            finals = []
            for slot in range(self.NSL):
                self.dma(self.vec[:], self.vecs[slot], [], ["vec"])
                self.load_state(slot)
                for sub in range(self.NSUB):
                    src = self.x_in if slot == 0 else self.x_out
                    self.dma(self.xT[:], src[sub], ["xdram%d" % sub], ["xT"])
                    if "ffn1" in self.stages:
                        self.ffn(st, slot, self.w_f1i, self.w_f1o, VOFF["ffn1_norm"])
                    if "mix" in self.stages:
                        self.mixer(st, slot, sub)
                    if sub == self.NSUB - 1:
                        finals.append(self.store_state(slot))
                    if "ffn2" in self.stages:
                        self.ffn(st, slot, self.w_f2i, self.w_f2o, VOFF["ffn2_norm"])
                    t = self.dma(self.x_out[sub], self.xT[:], ["xT"], ["xdram%d" % sub])
                    if slot == self.NSL - 1:
                        finals.append(t)
                    if slot in self.norm_after:
                        self.rmsnorm(VOFF["final_norm"], out_f32=True)
                        t = self.dma(self.y_out[self.norm_after.index(slot), sub], self.xT[:], ["xT"], ["ydram"])
                        finals.append(t)
            for k, t in self.dbg_out.items():
                finals.append(t)
            S.emit(final_waits=finals)
        return nc

    def alloc_consts(self, st):
        self.ident = self.sb(st, "ident", [128, 128], F32)
        self.on_d = self.sb(st, "on_d", [128, 128], BF16)
        self.on_1 = self.sb(st, "on_1", [128, 128], BF16)
        self.on_128 = self.sb(st, "on_128", [128, 128], BF16)
        self.blk = self.sb(st, "blk", [128, 128], F32)
        self.blk64 = self.sb(st, "blk64", [128, 128], F32)
        self.triu = self.sb(st, "triu", [128, 128], F32)
        self.mpast = self.sb(st, "mpast", [128, 4, 128], F32)
        self.mfut = self.sb(st, "mfut", [128, 4, 128], F32)
        self.mstr = self.sb(st, "mstr", [128, 4, 128], F32)
        self.smask = self.sb(st, "smask", [128, 512], F32)
        self.c_eps = self.sb(st, "c_eps", [128, 4], F32)

    def init_consts(self):
        S = self.S
        ms = self.memset
        ms(self.ident[:], 1.0, ["ident"])
        S.add("pool", lambda e: e.affine_select(out=self.ident[:], in_=self.ident[:], pattern=[[-1, 128]],
                                                compare_op=ALU.is_equal, fill=0.0, base=0, channel_multiplier=1),
              ["ident"], ["ident"])
        ms(self.on_d[:], 1.0 / D, ["on_d"])
        ms(self.on_1[:], 1.0, ["on_1"])
        ms(self.on_128[:], 1.0 / 128, ["on_128"])
        ms(self.blk[:], 0.0, ["blk"])
        ms(self.blk[0:64, 0:64], 1.0, ["blk"])
        ms(self.blk[64:128, 64:128], 1.0, ["blk"])
        self.ts(self.blk64[:], self.blk[:], 1.0 / 64, None, ALU.mult, None, ["blk"], ["blk64"], eng="pool")
        ms(self.triu[:], 1.0, ["triu"])
        S.add("pool", lambda e: e.affine_select(out=self.triu[:], in_=self.triu[:], pattern=[[1, 128]],
                                                compare_op=ALU.is_ge, fill=0.0, base=0, channel_multiplier=-1),
              ["triu"], ["triu"])
        for h in range(4):
            self.tt(self.mpast[:, h, :], self.triu[:], self.blk[:], ALU.mult, ["triu", "blk"], ["mpast"], eng="pool")
        for h in range(4):
            self.tt(self.mfut[:, h, :], self.blk[:], self.mpast[:, h, :], ALU.subtract, ["blk", "mpast"], ["mfut"], eng="pool")
            self.tt(self.mstr[:, h, :], self.mpast[:, h, :], self.ident[:], ALU.subtract, ["ident", "mpast"], ["mstr"], eng="pool")
        ms(self.c_eps[:, 0:1], EPS, ["c_eps"])
        ms(self.c_eps[:, 1:2], LN_EPS, ["c_eps"])
        ms(self.c_eps[:, 2:3], 1.0, ["c_eps"])
        ms(self.c_eps[:, 3:4], 0.0, ["c_eps"])
        ms(self.smask[:], 1.0, ["smask"])
        ms(self.smask[:].rearrange("p (c s) -> p c s", s=64)[:, :, 0:1], 0.0, ["smask"])

    def load_state(self, slot):
        f = self.flg[:, 2 * slot + 1:2 * slot + 2]
        tmp = [(self.rwS[:].rearrange("p a b -> p (a b)"), 0, 512, "rwS"),
               (self.glS[:].rearrange("p a b -> p (a b)"), 512, 256, "glS"),
               (self.uprev[:], 768, 14, "uprev"),
               (self.convh[:].rearrange("p a b -> p (a b)"), 782, 24, "convh")]
        for ap, o, n, key in tmp:
            if slot == 0:
                self.dma(ap, self.st_in[:, o:o + n], [], [key])
            else:
                self.dma(ap, self.st_pair[0:128, o:o + n], ["st_pair"], [key])
            self.ts(ap, ap, f, None, ALU.mult, None, [key, "flg"], [key])

    def store_state(self, slot):
        t = None
        tmp = [(self.rwS[:].rearrange("p a b -> p (a b)"), 0, 512, "rwS"),
               (self.glS[:].rearrange("p a b -> p (a b)"), 512, 256, "glS"),
               (self.uprev[:], 768, 14, "uprev"),
               (self.convh[:].rearrange("p a b -> p (a b)"), 782, 24, "convh")]
        last = slot == self.NSL - 1
        for ap, o, n, key in tmp:
            if last:
                t = self.dma(self.st_out[:, o:o + n], ap, [key], ["st_out%d" % o])
            else:
                t = self.dma(self.st_snd[:, o:o + n], ap, [key], ["st_snd"])
        if not last:
            groups = [[2 * i, 2 * i + 1] for i in range(self.n_cores // 2)]
            snd, pair = self.st_snd, self.st_pair
            t = self.S.add("pool", lambda e: e.collective_compute("AllGather", ALU.bypass, replica_groups=groups,
                                                                  ins=[snd.opt()], outs=[pair.opt()]),
                           ["st_snd"], ["st_pair"], cc=True)
        return t

    def rmsnorm(self, goff, out_f32=False):
        with contextlib.ExitStack() as st:
            sq = self.sb(st, "n_sq", [128, 8, 512], BF16)
            rs = self.sb(st, "n_rs", [128, 512], F32)
            for nt in range(self.NT):
                tsl = slice(nt * 512, (nt + 1) * 512)
                ps = self.PB[7]
                for c in range(8):
                    self.act(sq[:, c, :], self.xT[:, c, tsl], AF.Square, ["xT"], ["n_sq%d" % c])
                    self.mm(ps[:], self.on_d[:], sq[:, c, :], c == 0, c == 7, ["on_d", "n_sq%d" % c], ["pb7"])
                self.act(rs[:], ps[:], AF.Sqrt, ["pb7", "c_eps"], ["n_rs"], bias=self.c_eps[:, 0:1])
                self.recip(rs[:], rs[:], ["n_rs"], ["n_rs"])
                for c in range(8):
                    g = self.vec[:, goff + c:goff + c + 1]
                    dst = self.xT[:, c, tsl] if out_f32 else self.hT[:, c, tsl]
                    self.stt(dst, self.xT[:, c, tsl], g, rs[:], ALU.mult, ALU.mult, ["xT", "vec", "n_rs"],
                             ["xT" if out_f32 else "hT"])
            self.S.barrier(self.c_scr[:, 0:1])

    def ffn(self, st0, slot, w_i, w_o, goff):
        self.rmsnorm(goff)
        NT = self.NT
        with contextlib.ExitStack() as st:
            actT = self.sb(st, "f_act", [128, NFF, self.TS], BF16)
            wb = [self.sb(st, "f_wi%d" % i, [128, 8, 256], BF16) for i in range(3)]
            wo = [self.sb(st, "f_wo%d" % i, [128, NFF, 128], BF16) for i in range(2)]
            sg = [self.sb(st, "f_sg%d" % i, [128, 512], F32) for i in range(2)]
            it = 0
            for j in range(NFF):
                w = wb[j % 3]
                wk = "f_wi%d" % (j % 3)
                self.dma(w[:].rearrange("p a b -> p (a b)"), w_i[slot, j], [], [wk], eng="pool")
                for nt in range(NT):
                    tsl = slice(nt * 512, (nt + 1) * 512)
                    k = it % 2
                    it += 1
                    pg, pu = self.PB[k], self.PB[2 + k]
                    for kc in range(8):
                        self.mm(pg[:], w[:, kc, 0:128], self.hT[:, kc, tsl], kc == 0, kc == 7, [wk, "hT"], ["pb%d" % k])
                    for kc in range(8):
                        self.mm(pu[:], w[:, kc, 128:256], self.hT[:, kc, tsl], kc == 0, kc == 7, [wk, "hT"], ["pb%d" % (2 + k)])
                    self.act(sg[k][:], pg[:], AF.Silu, ["pb%d" % k], ["f_sg%d" % k])
                    self.tt(actT[:, j, tsl], sg[k][:], pu[:], ALU.mult, ["f_sg%d" % k, "pb%d" % (2 + k)], ["f_act%d" % j])
            it = 0
            for dc in range(8):
                w = wo[dc % 2]
                wk = "f_wo%d" % (dc % 2)
                self.dma(w[:].rearrange("p a b -> p (a b)"), w_o[slot, dc], [], [wk], eng="pool")
                for nt in range(NT):
                    tsl = slice(nt * 512, (nt + 1) * 512)
                    k = 4 + it % 2
                    it += 1
                    po = self.PB[k]
                    for j in range(NFF):
                        self.mm(po[:], w[:, j, :], actT[:, j, tsl], j == 0, j == NFF - 1, [wk, "f_act%d" % j], ["pb%d" % k])
                    self.stt(self.xT[:, dc, tsl], po[:], 0.5, self.xT[:, dc, tsl], ALU.mult, ALU.add,
                             ["pb%d" % k, "xT"], ["xT"])
            self.S.barrier(self.c_scr[:, 0:1])

    def win_tile(self, slot, ci, bufs, cnt):
        k = cnt[0] % len(bufs)
        cnt[0] += 1
        w = bufs[k]
        key = "m_wi%d" % k
        self.dma(w[:].rearrange("p a b -> p (a b)"), self.w_in[slot, ci], [], [key], eng="pool")
        return w, key

    def proj(self, ps, pkey, w, wkey, tsl, M=128):
        for kc in range(8):
            self.mm(ps, w[:, kc, 0:M], self.hT[:, kc, tsl], kc == 0, kc == 7, [wkey, "hT"], [pkey])

    def mixer(self, st0, slot, sub):
        self.rmsnorm(VOFF["mix_norm"])
        TS, NT = self.TS, self.NT
        self.dma(self.x_out[sub], self.xT[:], ["xT"], ["xdram%d" % sub])
        self.S.barrier(self.c_scr[:, 0:1])
        self.xslots = [self.xT[:, c, h * 512:(h + 1) * 512] for c in range(8) for h in range(TS // 512)]
        with contextlib.ExitStack() as st:
            self.yT = self.sb(st, "yT", [128, 12, TS], BF16)
            self.wib = [self.sb(st, "m_wi%d" % i, [128, 8, 128], BF16) for i in range(4)]
            self.wcnt = [0]
            if len(self.mix_parts) < 4:
                self.memset(self.yT[:].rearrange("p a b -> p (a b)"), 0.0, ["yT"])
            if "xa" in self.mix_parts:
                self.mix_xa(slot)
                self.S.barrier(self.c_scr[:, 0:1])
            if "gla" in self.mix_parts:
                self.mix_gla(slot, sub)
                self.S.barrier(self.c_scr[:, 0:1])
            if "rwkv" in self.mix_parts:
                self.mix_rwkv(slot, sub)
                self.S.barrier(self.c_scr[:, 0:1])
            self.dma(self.xT[:], self.x_out[sub], ["xdram%d" % sub], ["xT"])
            if "merge" in self.mix_parts:
                self.mix_merge(slot)
                self.S.barrier(self.c_scr[:, 0:1])
            self.dump("yT", self.yT[:], [128, 12, TS], "yT")
            self.S.barrier(self.c_scr[:, 0:1])

    def mix_xa(self, slot):
        TS, NT = self.TS, self.NT
        with contextlib.ExitStack() as st:
            mem = self.sb(st, "xa_mem", [128, 8, NMEM], F32)
            msq = self.sb(st, "xa_msq", [128, 8, NMEM], BF16)
            mrs = self.sb(st, "xa_mrs", [128, NMEM], F32)
            memn = self.sb(st, "xa_memn", [128, 8, NMEM], BF16)
            wkv = self.sb(st, "xa_wkv", [128, 8, 1024], BF16)
            KT = self.sb(st, "xa_KT", [128, 4, NMEM], BF16)
            V = self.sb(st, "xa_V", [128, 2, 512], BF16)
            qT = self.sb(st, "xa_q", [128, 512], BF16)
            E = self.sb(st, "xa_E", [128, 2, 512], BF16)
            rden = self.sb(st, "xa_rden", [128, 512], F32)
            self.dma(mem[:], self.memT, [], ["xa_mem"])
            self.dma(wkv[:].rearrange("p a b -> p (a b)"), self.w_kv[slot], [], ["xa_wkv"], eng="pool")
            ps = self.PB[0]
            for c in range(8):
                self.act(msq[:, c, :], mem[:, c, :], AF.Square, ["xa_mem"], ["xa_msq"])
                self.mm(ps[:, 0:NMEM], self.on_d[:], msq[:, c, :], c == 0, c == 7, ["on_d", "xa_msq"], ["pb0"])
            self.act(mrs[:], ps[:, 0:NMEM], AF.Sqrt, ["pb0", "c_eps"], ["xa_mrs"], bias=self.c_eps[:, 0:1])
            self.recip(mrs[:], mrs[:], ["xa_mrs"], ["xa_mrs"])
            go = VOFF["mem_norm"]
            for c in range(8):
                self.stt(memn[:, c, :], mem[:, c, :], self.vec[:, go + c:go + c + 1], mrs[:], ALU.mult, ALU.mult,
                         ["xa_mem", "vec", "xa_mrs"], ["xa_memn"])
            for h in range(4):
                ps = self.PB[h % 2]
                pk = "pb%d" % (h % 2)
                for kc in range(8):
                    self.mm(ps[:, 0:NMEM], wkv[:, kc, h * 128:(h + 1) * 128], memn[:, kc, :], kc == 0, kc == 7,
                            ["xa_wkv", "xa_memn"], [pk])
                self.act(KT[:, h, :], ps[:, 0:NMEM], AF.Copy, [pk], ["xa_KT"])
            for mb in range(2):
                ps = self.PB[2 + mb]
                pk = "pb%d" % (2 + mb)
                for kc in range(8):
                    self.mm(ps[:], memn[:, kc, mb * 128:(mb + 1) * 128], wkv[:, kc, 512:1024], kc == 0, kc == 7,
                            ["xa_wkv", "xa_memn"], [pk])
                self.act(V[:, mb, :], ps[:], AF.Copy, [pk], ["xa_V"])
            for h in range(4):
                w, wk = self.win_tile(slot, 26 + h, self.wib, self.wcnt)
                for nt in range(NT):
                    tsl = slice(nt * 512, (nt + 1) * 512)
                    self.proj(self.PB[4][:], "pb4", w, wk, tsl)
                    self.act(qT[:], self.PB[4][:], AF.Copy, ["pb4"], ["xa_q"], scale=float(128 ** -0.5))
                    for mb in range(2):
                        self.mm(self.PB[5][:], KT[:, h, mb * 128:(mb + 1) * 128], qT[:], True, True, ["xa_KT", "xa_q"], ["pb5"])
                        self.act(E[:, mb, :], self.PB[5][:], AF.Exp, ["pb5"], ["xa_E%d" % mb])
                    for mb in range(2):
                        self.mm(self.PB[6][:], V[:, mb, h * 128:(h + 1) * 128], E[:, mb, :], mb == 0, mb == 1,
                                ["xa_V", "xa_E%d" % mb], ["pb6"])
                    for mb in range(2):
                        self.mm(self.PB[7][:], self.on_1[:], E[:, mb, :], mb == 0, mb == 1, ["on_1", "xa_E%d" % mb], ["pb7"])
                    self.recip(rden[:], self.PB[7][:], ["pb7"], ["xa_rden"])
                    self.tt(self.yT[:, 8 + h, tsl], self.PB[6][:], rden[:], ALU.mult, ["pb6", "xa_rden"], ["yT"])

    def mix_gla(self, slot, sub):
        TS, NT = self.TS, self.NT
        NBLK = TS // 128
        S = self.S
        import os
        GS = int(os.environ.get("GLA_STOP", "99"))
        with contextlib.ExitStack() as st:
            sgo = self.sb(st, "g_sgo", [128, 4, TS], BF16)
            v = self.sb(st, "g_v", [128, 4, TS], F32)
            Eg = self.sb(st, "g_Eg", [128, 2, TS], F32)
            qg = self.sb(st, "g_qg", [128, 2, TS], BF16)
            qr = self.sb(st, "g_qr", [128, 2, TS], BF16)
            kg = self.sb(st, "g_kg", [128, 2, TS], BF16)
            kr = self.sb(st, "g_kr", [128, 2, TS], BF16)
            kd = self.sb(st, "g_kd", [128, 2, TS], F32)
            it = 0
            with contextlib.ExitStack() as stB:
                qk = self.sb(stB, "g_qk", [128, 4, TS], F32)
                Eng = self.sb(stB, "g_Eng", [128, 2, TS], F32)
                alT = self.sb(stB, "g_alT", [16, TS], F32)
                aup = self.sb(stB, "g_aup", [16, 256], F32)
                wal = self.sb(stB, "g_wal", [128, 8, 16], BF16)
                self.dma(aup[:], self.w_aup[slot], [], ["g_aup"])
                self.dma(wal[:].rearrange("p a b -> p (a b)"), self.w_al[slot], [], ["g_wal"], eng="pool")
                for c in range(4):
                    w, wk = self.win_tile(slot, 22 + c, self.wib, self.wcnt)
                    for nt in range(NT):
                        tsl = slice(nt * 512, (nt + 1) * 512)
                        k = it % 2
                        it += 1
                        self.proj(self.PB[k][:], "pb%d" % k, w, wk, tsl)
                        self.act(sgo[:, c, tsl], self.PB[k][:], AF.Silu, ["pb%d" % k], ["g_sgo"])
                for nt in range(NT):
                    tsl = slice(nt * 512, (nt + 1) * 512)
                    k = it % 2
                    it += 1
                    self.proj(self.PB[k][0:16, :], "pb%d" % k, wal, "g_wal", tsl, M=16)
                    self.act(alT[:, tsl], self.PB[k][0:16, :], AF.Copy, ["pb%d" % k], ["g_alT"])
                for half in range(2):
                    with contextlib.ExitStack() as stC:
                        ug = self.sb(stC, "g_ug", [128, 4, 3 + TS], F32)
                        tmp = self.sb(stC, "g_ctmp", [128, TS], F32)
                        self.cp(ug[:, :, 0:3], self.convh[:, half * 4:half * 4 + 4, :], ["convh"], ["g_ug"])
                        for c4 in range(4):
                            c = half * 4 + c4
                            w, wk = self.win_tile(slot, 14 + c, self.wib, self.wcnt)
                            for nt in range(NT):
                                tsl = slice(nt * 512, (nt + 1) * 512)
                                k = it % 2
                                it += 1
                                self.proj(self.PB[k][:], "pb%d" % k, w, wk, tsl)
                                self.act(ug[:, c4, 3 + nt * 512:3 + (nt + 1) * 512], self.PB[k][:], AF.Copy, ["pb%d" % k], ["g_ug"])
                        self.cp(self.convh[:, half * 4:half * 4 + 4, :], ug[:, :, TS:TS + 3], ["g_ug"], ["convh"])
                        for c4 in range(4):
                            c = half * 4 + c4
                            cw = [self.vec[:, VOFF["conv%d" % jj] + c:VOFF["conv%d" % jj] + c + 1] for jj in range(4)]
                            self.ts(tmp[:], ug[:, c4, 3:3 + TS], cw[3], None, ALU.mult, None, ["g_ug", "vec"], ["g_ctmp"])
                            for jj in (2, 1, 0):
                                self.stt(tmp[:], ug[:, c4, jj:jj + TS], cw[jj], tmp[:], ALU.mult, ALU.add, ["g_ug", "vec", "g_ctmp"], ["g_ctmp"])
                            dst = qk[:, c, :] if c < 4 else v[:, c - 4, :]
                            self.act(dst, tmp[:], AF.Silu, ["g_ctmp"], ["g_qk" if c < 4 else "g_v"])
                        self.S.barrier(self.c_scr[:, 0:1])
                if GS <= 2:
                    return
                with contextlib.ExitStack() as stC:
                    la = self.sb(stC, "g_la", [128, 2, TS], F32)
                    sgm = self.sb(stC, "g_sgm", [128, 512], F32)
                    ab = VOFF["gla_a_bias"]
                    for c2 in range(2):
                        for nt in range(NT):
                            tsl = slice(nt * 512, (nt + 1) * 512)
                            k = it % 2
                            it += 1
                            self.mm(self.PB[k][:], aup[0:16, c2 * 128:(c2 + 1) * 128], alT[0:16, tsl], True, True, ["g_aup", "g_alT"], ["pb%d" % k])
                            self.act(sgm[:], self.PB[k][:], AF.Sigmoid, ["pb%d" % k, "vec"], ["g_sgm"], bias=self.vec[:, ab + c2:ab + c2 + 1])
                            self.act(sgm[:], sgm[:], AF.Ln, ["g_sgm"], ["g_sgm"])
                            S.add("dve", (lambda o, d1: (lambda e: e.tensor_tensor_scan(out=o, data0=self.smask[:], data1=d1, initial=0.0,
                                                                                       op0=ALU.mult, op1=ALU.add)))(la[:, c2, tsl], sgm[:]),
                                  ["smask", "g_sgm"], ["g_la"])
                    self.act(Eg[:].rearrange("p a b -> p (a b)"), la[:].rearrange("p a b -> p (a b)"), AF.Exp, ["g_la"], ["g_Eg"], scale=1.0 / 16)
                    self.act(Eng[:].rearrange("p a b -> p (a b)"), la[:].rearrange("p a b -> p (a b)"), AF.Exp, ["g_la"], ["g_Eng"], scale=-1.0 / 16)
                    self.S.barrier(self.c_scr[:, 0:1])
                for c2 in range(2):
                    self.stt(qg[:, c2, :], qk[:, c2, :], 0.125, Eg[:, c2, :], ALU.mult, ALU.mult, ["g_qk", "g_Eg"], ["g_qg"])
                    self.stt(qr[:, c2, :], qk[:, c2, :], 0.125, Eng[:, c2, :], ALU.mult, ALU.mult, ["g_qk", "g_Eng"], ["g_qr"])
                    self.tt(kg[:, c2, :], qk[:, 2 + c2, :], Eng[:, c2, :], ALU.mult, ["g_qk", "g_Eng"], ["g_kg"])
                    self.tt(kr[:, c2, :], qk[:, 2 + c2, :], Eg[:, c2, :], ALU.mult, ["g_qk", "g_Eg"], ["g_kr"])
                    for ch in range(TS // 64):
                        cs_ = slice(ch * 64, ch * 64 + 64)
                        self.stt(kd[:, c2, cs_], qk[:, 2 + c2, cs_], Eg[:, c2, ch * 64 + 63:ch * 64 + 64], Eng[:, c2, cs_],
                                 ALU.mult, ALU.mult, ["g_qk", "g_Eg", "g_Eng"], ["g_kd"])
                self.S.barrier(self.c_scr[:, 0:1])
            if GS <= 4:
                return
            v_tok = self.sb(st, "g_vtok", [128, 512], F32)
            kd2 = self.sb(st, "g_kd2", [128, 2, 256], F32)
            kgp = self.sb(st, "g_kgp", [128, 2, 2, 128], BF16)
            krp = self.sb(st, "g_krp", [128, 2, 2, 128], BF16)
            qgp = self.sb(st, "g_qgp", [128, 2, 2, 128], F32)
            at = self.sb(st, "g_at", [128, 512], F32)
            at2 = self.sb(st, "g_at2", [128, 512], F32)
            atb = self.sb(st, "g_atb", [128, 512], F32)
            sq = self.sb(st, "g_sq", [128, 512], BF16)
            rs = self.sb(st, "g_rs", [128, 512], F32)
            ot = self.sb(st, "g_ot", [128, 512], F32)
            glS = self.glS
            for t_, k_ in ((kd2, "g_kd2"), (kgp, "g_kgp"), (krp, "g_krp"), (qgp, "g_qgp")):
                self.memset(t_[:].rearrange("p a b -> p (a b)") if len(t_.shape) == 3 else t_[:].rearrange("p a b c -> p (a b c)"), 0.0, [k_])
            mp = self.mpast[:].rearrange("p a b -> p (a b)")
            mf = self.mfut[:].rearrange("p a b -> p (a b)")
            gn = VOFF["gla_norm"]
            for b in range(NBLK):
                bsl = slice(b * 128, (b + 1) * 128)
                for c in range(4):
                    S.add("pe", (lambda o, i_: (lambda e: e.transpose(o, i_, self.ident[:])))(self.PB[0][:, c * 128:(c + 1) * 128], v[:, c, bsl]),
                          ["g_v", "ident"], ["pb0"])
                self.cp(v_tok[:], self.PB[0][:], ["pb0"], ["g_vtok"], eng="act_copy")
                for c2 in range(2):
                    S.add("pe", (lambda o, i_: (lambda e: e.transpose(o, i_, self.ident[:])))(self.PB[1][:, c2 * 128:(c2 + 1) * 128], kd[:, c2, bsl]),
                          ["g_kd", "ident"], ["pb1"])
                for cc in range(2):
                    crow = slice(cc * 64, cc * 64 + 64)
                    self.cp(kd2[crow, cc, :], self.PB[1][crow, 0:256], ["pb1"], ["g_kd2"], eng="act_copy")
                for hh in range(2):
                    r2 = slice(hh * 64, hh * 64 + 64)
                    self.cp(kgp[r2, hh, :, :], kg[r2, :, bsl], ["g_kg"], ["g_kgp"], eng="pool")
                    self.cp(krp[r2, hh, :, :], kr[r2, :, bsl], ["g_kr"], ["g_krp"], eng="pool")
                    self.cp(qgp[r2, hh, :, :], qg[r2, :, bsl], ["g_qg"], ["g_qgp"], eng="pool")
                if GS <= 5:
                    continue
                for h in range(4):
                    hs = slice(h * 128, (h + 1) * 128)
                    self.mm(self.PB[2][:, hs], kgp[:, h % 2, h // 2, :], qg[:, h // 2, bsl], True, True, ["g_kgp", "g_qg"], ["pb2"])
                    self.mm(self.PB[3][:, hs], krp[:, h % 2, h // 2, :], qr[:, h // 2, bsl], True, True, ["g_krp", "g_qr"], ["pb3"])
                self.tt(at[:], self.PB[2][:], mp, ALU.mult, ["pb2", "mpast"], ["g_at"])
                self.tt(at2[:], self.PB[3][:], mf, ALU.mult, ["pb3", "mfut"], ["g_at2"])
                self.tt(atb[:], at[:], at2[:], ALU.add, ["g_at", "g_at2"], ["g_atb"], eng="pool")
                if GS <= 6:
                    continue
                po = self.PB[4]
                for h in range(4):
                    hs = slice(h * 128, (h + 1) * 128)
                    self.mm(po[:, hs], v_tok[:, hs], atb[:, hs], h == 0, False, ["g_vtok", "g_atb"], ["pb4"], skip=True)
                for cc in range(2):
                    for h in range(4):
                        self.mm(po[:, h * 128 + cc * 64:h * 128 + cc * 64 + 64], glS[:, h // 2, :], qgp[:, h % 2, h // 2, cc * 64:cc * 64 + 64],
                                False, True, ["glS", "g_qgp"], ["pb4"], skip=True)
                    for hp in range(2):
                        self.mm(self.PB[5][:, hp * 256:(hp + 1) * 256], kd2[:, cc, hp * 128:(hp + 1) * 128],
                                v_tok[:, hp * 256:(hp + 1) * 256], True, True, ["g_kd2", "g_vtok"], ["pb5"])
                    for hp in range(2):
                        for hh in range(2):
                            r2 = slice(hh * 64, hh * 64 + 64)
                            self.stt(glS[r2, hp, :], glS[r2, hp, :], Eg[r2, hp, b * 128 + cc * 64 + 63:b * 128 + cc * 64 + 64],
                                     self.PB[5][r2, hp * 256 + hh * 128:hp * 256 + (hh + 1) * 128], ALU.mult, ALU.add,
                                     ["glS", "g_Eg", "pb5"], ["glS"])
                if GS <= 7:
                    continue
                self.act(sq[:], po[:], AF.Square, ["pb4"], ["g_sq"])
                self.mm(self.PB[6][:], self.on_128[:], sq[:], True, True, ["on_128", "g_sq"], ["pb6"])
                self.act(rs[:], self.PB[6][:], AF.Sqrt, ["pb6", "c_eps"], ["g_rs"], bias=self.c_eps[:, 0:1])
                self.recip(rs[:], rs[:], ["g_rs"], ["g_rs"])
                self.tt(ot[:], po[:], rs[:], ALU.mult, ["pb4", "g_rs"], ["g_ot"])
                for h in range(4):
                    self.stt(self.yT[:, 4 + h, bsl], ot[:, h * 128:(h + 1) * 128], self.vec[:, gn + h:gn + h + 1], sgo[:, h, bsl],
                             ALU.mult, ALU.mult, ["g_ot", "vec", "g_sgo"], ["yT"])

    def mix_rwkv(self, slot, sub):
        TS, NT = self.TS, self.NT
        S = self.S
        C0 = float(np.exp(-0.5))
        V = lambda n, c: self.vec[:, VOFF[n] + c:VOFF[n] + c + 1]
        with contextlib.ExitStack() as st:
            wr = [self.sb(st, "r_w%d" % i, [128, 8, 128], BF16) for i in range(14)]
            for i in range(14):
                self.dma(wr[i][:].rearrange("p a b -> p (a b)"), self.w_in[slot, i], [], ["r_w%d" % i], eng="pool")
            w2p = self.sb(st, "r_w2p", [128, 512], F32)
            a2p = self.sb(st, "r_a2p", [128, 512], F32)
            g2 = self.sb(st, "r_g2", [128, 512], F32)
            v1 = self.sb(st, "r_v1", [128, 4, 32], F32)
            v2 = self.sb(st, "r_v2", [32, 512], F32)
            omk = self.sb(st, "r_omk", [128, 4], F32)
            self.memset(w2p[:], 0.0, ["r_w2p"])
            self.memset(a2p[:], 0.0, ["r_a2p"])
            self.dma(w2p[0:64, :], self.w_w2[slot], [], ["r_w2p"])
            self.dma(a2p[64:128, :], self.w_a2[slot], [], ["r_a2p"])
            self.dma(g2[:], self.w_g2[slot], [], ["r_g2"])
            self.dma(v1[:].rearrange("p a b -> p (a b)"), self.w_v1[slot], [], ["r_v1"])
            self.dma(v2[:], self.w_v2[slot], [], ["r_v2"])
            ka = VOFF["rwkv_k_a"]
            self.ts(omk[:], self.vec[:, ka:ka + 4], -1.0, 1.0, ALU.mult, ALU.add, ["vec"], ["r_omk"])
            def nb(n, sh=(128, 512), dt=F32):
                if tuple(sh) == (128, 512) and dt == F32 and self.xslots:
                    return self.xslots.pop()
                return self.sb(st, n, list(sh), dt)
            ub = nb("r_ub", (128, 513))
            lor1, lor2 = nb("r_lor1"), nb("r_lor2")
            uvall = nb("r_uvall", (128, 4, 512))
            t1T = nb("r_t1T", (32, 512))
            ur, uk, a_, g_, sig = [nb("r_" + n) for n in ("ur", "uk", "a", "g", "sig")]
            Eg, Eng, kk, kh, KKt, Kt, Bt, Rt, kdec, bon, tmp, tmp2, vfb = [
                nb("r_" + n) for n in ("Eg", "Eng", "kk", "kh", "KKt", "Kt", "Bt", "Rt", "kdec", "bon", "tmp", "tmp2", "vfb")]
            cs = tmp2
            bdec = Eng
            Ysb = kdec
            egl = nb("r_egl", (128, 8))
            BS = []
            for q in range(2):
                d_ = dict(q=q)
                for n in ("Ktp", "Btp", "KKp", "IAT", "kdec2", "bdec2"):
                    d_[n] = nb("r_%s_%d" % (n, q), (128, 2, 128))
                d_["MW"] = nb("r_MW_%d" % q, (128, 2, 4, 128))
                d_["A"] = [nb("r_A%d_%d" % (i, q), (128, 2, 128)) for i in range(2)]
                d_["AT"] = [nb("r_AT%d_%d" % (i, q), (128, 2, 128)) for i in range(2)]
                d_["T"] = [nb("r_T%d_%d" % (i, q), (128, 2, 128)) for i in range(2)]
                d_["v_tok"] = nb("r_vtok_%d" % q, (128, 128))
                d_["vpad"] = nb("r_vpad_%d" % q, (128, 384))
                BS.append(d_)
            SApad = nb("r_SApad", (128, 384))
            negX, SA = nb("r_negX", (128, 128)), nb("r_SA", (128, 128))
            mask4 = nb("r_mask4", (128, 4, 128))
            id2 = nb("r_id2", (128, 2, 128))
            for d_ in BS:
                for n in ("Ktp", "Btp", "KKp", "kdec2", "bdec2"):
                    self.memset(d_[n][:].rearrange("p a b -> p (a b)"), 0.0, ["r_%s_%d" % (n, d_["q"])])
                self.memset(d_["vpad"][:], 0.0, ["r_vpad_%d" % d_["q"]])
            for t_, k_ in ((SApad, "r_SApad"), (negX, "r_negX"), (SA, "r_SA")):
                self.memset(t_[:], 0.0, [k_])
            for q_, m_ in ((0, self.mstr), (1, self.mpast), (2, self.mstr), (3, self.mpast)):
                self.cp(mask4[:, q_, :], m_[:, 0, :], ["mstr", "mpast"], ["r_mask4"], eng="pool")
            for hh in range(2):
                self.cp(id2[:, hh, :], self.ident[:], ["ident"], ["r_id2"], eng="pool")
            rwS = self.rwS
            pbi = [0]

            def bank():
                k = pbi[0] % 4
                pbi[0] += 1
                return self.PB[k], "pb%d" % k

            def shift_proj(ci, tsl, out, okey):
                ps, pk = bank()
                self.proj(ps[:], pk, wr[ci], "r_w%d" % ci, tsl)
                self.cp(ub[:, 1:513], ps[:], [pk], ["r_ub"], eng="act_copy")
                self.cp(ub[:, 0:1], self.uprev[:, ci:ci + 1], ["uprev"], ["r_ub"], eng="pool")
                self.tt(out, ub[:, 0:512], ub[:, 1:513], ALU.subtract, ["r_ub"], [okey])
                self.stt(out, out, V("rwkv_mu", ci), ub[:, 1:513], ALU.mult, ALU.add, [okey, "vec", "r_ub"], [okey])
                self.cp(self.uprev[:, ci:ci + 1], ub[:, 512:513], ["r_ub"], ["uprev"], eng="pool")

            vf_src = self.vf_in if slot == 0 else self.vf_out
            fl0 = self.flg[:, 2 * slot:2 * slot + 1]
            import os
            RS = float(os.environ.get("R_STOP", "99"))
            for nt in range(NT):
                tsl = slice(nt * 512, (nt + 1) * 512)
                shift_proj(12, tsl, tmp[:], "r_tmp")
                self.act(lor1[0:64, :], tmp[0:64, :], AF.Tanh, ["r_tmp"], ["r_lor1"])
                self.cp(lor1[64:128, :], tmp[64:128, :], ["r_tmp"], ["r_lor1"])
                shift_proj(13, tsl, tmp[:], "r_tmp")
                self.act(lor2[:], tmp[:], AF.Sigmoid, ["r_tmp"], ["r_lor2"])
                for hp in range(4):
                    shift_proj(8 + hp, tsl, uvall[:, hp, :], "r_uvall")
                ps, pk = bank()
                for c in range(4):
                    self.mm(ps[0:32, :], v1[:, c, :], uvall[:, c, :], c == 0, c == 3, ["r_v1", "r_uvall"], [pk])
                self.cp(t1T[:], ps[0:32, :], [pk], ["r_t1T"])
                if RS <= 1:
                    continue
                for hp in range(4):
                    hs = slice(hp * 128, (hp + 1) * 128)
                    uv = uvall[:, hp, :]
                    shift_proj(hp, tsl, ur[:], "r_ur")
                    shift_proj(4 + hp, tsl, uk[:], "r_uk")
                    ps, pk = bank()
                    self.mm(ps[:], v2[0:32, hs], t1T[0:32, :], True, True, ["r_v2", "r_t1T"], [pk])
                    self.act(tmp[:], ps[:], AF.Sigmoid, [pk, "vec"], ["r_tmp"], bias=V("rwkv_v0", hp))
                    self.dma(vfb[:], vf_src[sub, :, hp, tsl], ["vfdram"], ["r_vfb"])
                    self.tt(tmp2[:], vfb[:], uv, ALU.subtract, ["r_vfb", "r_uvall"], ["r_tmp2"])
                    self.tt(tmp2[:], tmp2[:], tmp[:], ALU.mult, ["r_tmp2", "r_tmp"], ["r_tmp2"])
                    self.tt(uv, uv, tmp2[:], ALU.add, ["r_uvall", "r_tmp2"], ["r_uvall"])
                    self.tt(tmp2[:], uv, vfb[:], ALU.subtract, ["r_vfb", "r_uvall"], ["r_tmp2"])
                    self.stt(vfb[:], tmp2[:], fl0, vfb[:], ALU.mult, ALU.add, ["r_tmp2", "flg", "r_vfb"], ["r_vfb"])
                    self.dma(self.vf_out[sub, :, hp, tsl], vfb[:], ["r_vfb"], ["vfdram"])
                    if RS <= 2:
                        continue
                    ps, pk = bank()
                    self.mm(ps[:], w2p[:, hs], lor1[:], True, True, ["r_w2p", "r_lor1"], [pk])
                    self.act(sig[:], ps[:], AF.Sigmoid, [pk, "vec"], ["r_sig"], bias=V("rwkv_w0", hp))
                    ps, pk = bank()
                    self.mm(ps[:], a2p[:, hs], lor1[:], True, True, ["r_a2p", "r_lor1"], [pk])
                    self.act(a_[:], ps[:], AF.Sigmoid, [pk, "vec"], ["r_a"], bias=V("rwkv_a0", hp))
                    ps, pk = bank()
                    self.mm(ps[:], g2[:, hs], lor2[:], True, True, ["r_g2", "r_lor2"], [pk])
                    self.cp(g_[:], ps[:], [pk], ["r_g"], eng="act_copy")
                    S.add("dve", (lambda: (lambda e: e.tensor_tensor_scan(out=cs[:], data0=self.smask[:], data1=sig[:], initial=0.0,
                                                                         op0=ALU.mult, op1=ALU.add)))(), ["smask", "r_sig"], ["r_tmp2"])
                    self.tt(KKt[:], cs[:], sig[:], ALU.subtract, ["r_tmp2", "r_sig"], ["r_KKt"], eng="pool")
                    self.act(Eg[:], cs[:], AF.Exp, ["r_tmp2"], ["r_Eg"], scale=-C0)
                    self.act(Eng[:], cs[:], AF.Exp, ["r_tmp2"], ["r_Eng"], scale=C0)
                    self.act(KKt[:], KKt[:], AF.Exp, ["r_KKt"], ["r_KKt"], scale=-C0)
                    self.cp(egl[:], Eg[:].rearrange("p (c s) -> p c s", s=64)[:, :, 63], ["r_Eg"], ["r_egl"], eng="pool")
                    self.ts(kk[:], uk[:], V("rwkv_k_k", hp), None, ALU.mult, None, ["r_uk", "vec"], ["r_kk"])
                    self.act(tmp[:], kk[:], AF.Square, ["r_kk"], ["r_tmp"])
                    ps, pk = bank()
                    self.mm(ps[:], self.blk[:], tmp[:], True, True, ["blk", "r_tmp"], [pk])
                    self.act(tmp[:], ps[:], AF.Sqrt, [pk], ["r_tmp"])
                    self.ts(tmp[:], tmp[:], 1e-12, None, ALU.max, None, ["r_tmp"], ["r_tmp"])
                    self.recip(tmp[:], tmp[:], ["r_tmp"], ["r_tmp"])
                    self.tt(kk[:], kk[:], tmp[:], ALU.mult, ["r_kk", "r_tmp"], ["r_kk"])
                    self.ts(tmp[:], a_[:], V("rwkv_k_a", hp), omk[:, hp:hp + 1], ALU.mult, ALU.add, ["r_a", "vec", "r_omk"], ["r_tmp"])
                    self.tt(kh[:], uk[:], tmp[:], ALU.mult, ["r_uk", "r_tmp"], ["r_kh"])
                    self.tt(tmp[:], ur[:], kh[:], ALU.mult, ["r_ur", "r_kh"], ["r_tmp"])
                    self.ts(tmp[:], tmp[:], V("rwkv_r_k", hp), None, ALU.mult, None, ["r_tmp", "vec"], ["r_tmp"])
                    ps, pk = bank()
                    self.mm(ps[:], self.blk[:], tmp[:], True, True, ["blk", "r_tmp"], [pk])
                    self.tt(bon[:], ps[:], uv, ALU.mult, [pk, "r_uvall"], ["r_bon"])
                    self.tt(KKt[:], KKt[:], kk[:], ALU.mult, ["r_KKt", "r_kk"], ["r_KKt"])
                    self.tt(Kt[:], kh[:], Eng[:], ALU.mult, ["r_kh", "r_Eng"], ["r_Kt"])
                    self.tt(tmp[:], kk[:], a_[:], ALU.mult, ["r_kk", "r_a"], ["r_tmp"], eng="pool")
                    self.tt(Bt[:], tmp[:], Eng[:], ALU.mult, ["r_tmp", "r_Eng"], ["r_Bt"])
                    self.tt(Rt[:], ur[:], Eg[:], ALU.mult, ["r_ur", "r_Eg"], ["r_Rt"])
                    for ch in range(8):
                        cs_ = slice(ch * 64, ch * 64 + 64)
                        self.ts(kdec[:, cs_], Kt[:, cs_], egl[:, ch:ch + 1], None, ALU.mult, None, ["r_Kt", "r_egl"], ["r_kdec"])
                        self.ts(bdec[:, cs_], Bt[:, cs_], egl[:, ch:ch + 1], None, ALU.mult, None, ["r_Bt", "r_egl"], ["r_Eng"], eng="pool")
                    if RS <= 3:
                        continue
                    PY, pyk = self.PB[7], "pb7"

                    def gen_pre(b, B_):
                        bl = slice(b * 128, (b + 1) * 128)
                        q = B_["q"]
                        K_ = lambda n: "%s_%d" % (n, q)
                        PT, ptk = self.PB[4], "pb4"
                        for q_, src, sk in ((0, uv, "r_uvall"), (1, kdec[:], "r_kdec"), (2, bdec[:], "r_Eng")):
                            S.add("pe", (lambda o, i_: (lambda e: e.transpose(o, i_, self.ident[:])))(PT[:, q_ * 128:(q_ + 1) * 128], src[:, bl]),
                                  [sk, "ident"], [ptk])
                        self.cp(B_["v_tok"][:], PT[:, 0:128], [ptk], [K_("r_vtok")], eng="act_copy")
                        self.cp(B_["vpad"][:].rearrange("p (a b) -> p a b", b=192)[:, :, 0:64], PT[:, 0:128].rearrange("p (a b) -> p a b", b=64),
                                [ptk], [K_("r_vpad")])
                        for cc in range(2):
                            crow = slice(cc * 64, cc * 64 + 64)
                            self.cp(B_["kdec2"][crow, cc, :], PT[crow, 128:256], [ptk], [K_("r_kdec2")], eng="act_copy")
                            self.cp(B_["bdec2"][crow, cc, :], PT[crow, 256:384], [ptk], [K_("r_bdec2")])
                        for hh in range(2):
                            r2 = slice(hh * 64, hh * 64 + 64)
                            self.cp(B_["Ktp"][r2, hh, :], Kt[r2, bl], ["r_Kt"], [K_("r_Ktp")], eng="pool")
                            self.cp(B_["Btp"][r2, hh, :], Bt[r2, bl], ["r_Bt"], [K_("r_Btp")], eng="pool")
                            self.cp(B_["KKp"][r2, hh, :], KKt[r2, bl], ["r_KKt"], [K_("r_KKp")], eng="pool")
                        yield
                        MW = B_["MW"]
                        for hh in range(2):
                            PM, pmk = self.PB[5 + hh], "pb%d" % (5 + hh)
                            self.mm(PM[:, 0:128], B_["Ktp"][:, hh, :], KKt[:, bl], True, True, [K_("r_Ktp"), "r_KKt"], [pmk])
                            self.mm(PM[:, 128:256], B_["Ktp"][:, hh, :], Rt[:, bl], True, True, [K_("r_Ktp"), "r_Rt"], [pmk])
                            self.mm(PM[:, 256:384], B_["Btp"][:, hh, :], KKt[:, bl], True, True, [K_("r_Btp"), "r_KKt"], [pmk])
                            self.mm(PM[:, 384:512], B_["Btp"][:, hh, :], Rt[:, bl], True, True, [K_("r_Btp"), "r_Rt"], [pmk])
                            self.tt(MW[:, hh, :, :].rearrange("p a b -> p (a b)"), PM[:], mask4[:].rearrange("p a b -> p (a b)"), ALU.mult,
                                    [pmk, "r_mask4"], [K_("r_MW")])
                        PA, pak = bank()
                        for hh in range(2):
                            self.mm(PA[:, hh * 128:(hh + 1) * 128], B_["KKp"][:, hh, :], Bt[:, bl], True, True, [K_("r_KKp"), "r_Bt"], [pak])
                        Ab_, ATb_, Tb_, IAT_ = B_["A"], B_["AT"], B_["T"], B_["IAT"]
                        self.tt(ATb_[0][:].rearrange("p a b -> p (a b)"), PA[:, 0:256], self.mfut[:, 0:2, :].rearrange("p a b -> p (a b)"), ALU.mult,
                                [pak, "mfut"], [K_("r_AT0")])
                        self.tt(Tb_[0][:], id2[:], MW[:, :, 2, :], ALU.subtract, ["r_id2", K_("r_MW")], [K_("r_T0")])
                        self.cp(Ab_[0][:], MW[:, :, 2, :], [K_("r_MW")], [K_("r_A0")], eng="pool")
                        yield

                        def sq(cur, need_a):
                            A, AT = Ab_[cur], ATb_[cur]
                            ak, atk = K_("r_A%d" % cur), K_("r_AT%d" % cur)
                            P1, p1k = bank()
                            P2, p2k = bank()
                            for hh in range(2):
                                hs2 = slice(hh * 128, (hh + 1) * 128)
                                if need_a:
                                    self.mm(P1[:, hs2], AT[:, hh, :], A[:, hh, :], True, True, [ak, atk], [p1k])
                                self.mm(P2[:, hs2], A[:, hh, :], AT[:, hh, :], True, True, [ak, atk], [p2k])
                            return P1, p1k, P2, p2k
                        cur = 0
                        pend = sq(0, True)
                        yield
                        for it in range(5):
                            nx = 1 - cur
                            P1, p1k, P2, p2k = pend
                            if it < 4:
                                self.cp(Ab_[nx][:].rearrange("p a b -> p (a b)"), P1[:, 0:256], [p1k], [K_("r_A%d" % nx)], eng="act_copy")
                                self.cp(ATb_[nx][:].rearrange("p a b -> p (a b)"), P2[:, 0:256], [p2k], [K_("r_AT%d" % nx)], eng="act_copy")
                            self.tt(IAT_[:].rearrange("p a b -> p (a b)"), P2[:, 0:256], id2[:].rearrange("p a b -> p (a b)"), ALU.add,
                                    [p2k, "r_id2"], [K_("r_IAT")])
                            if it < 4:
                                pend = sq(nx, it < 3)
                            P3, p3k = bank()
                            for hh in range(2):
                                self.mm(P3[:, hh * 128:(hh + 1) * 128], IAT_[:, hh, :], Tb_[cur][:, hh, :], True, True,
                                        [K_("r_IAT"), K_("r_T%d" % cur)], [p3k])
                            self.cp(Tb_[nx][:].rearrange("p a b -> p (a b)"), P3[:, 0:256], [p3k], [K_("r_T%d" % nx)])
                            cur = nx
                            yield
                        B_["Tcur"] = cur

                    def gen_chain(b, B_):
                        q = B_["q"]
                        K_ = lambda n: "%s_%d" % (n, q)
                        bl = slice(b * 128, (b + 1) * 128)
                        MW = B_["MW"]
                        T, tk = B_["T"][B_["Tcur"]], K_("r_T%d" % B_["Tcur"])
                        v_tok = B_["v_tok"]
                        for cc in range(2):
                            crow = slice(cc * 64, cc * 64 + 64)
                            cl = slice(b * 128 + cc * 64, b * 128 + cc * 64 + 64)
                            ccol = slice(cc * 64, cc * 64 + 64)
                            PX, pxk = bank()
                            self.mm(PX[:, 0:128], KKt[:, bl], rwS[:, hp, :], True, False, ["r_KKt", "rwS"], [pxk], skip=True)
                            for hh in range(2):
                                self.mm(PX[:, hh * 64:(hh + 1) * 64], MW[:, hh, 0, :], v_tok[:, hh * 64:(hh + 1) * 64], False, True,
                                        [K_("r_MW"), K_("r_vtok")], [pxk], skip=True)
                            self.ts(negX[crow, :], PX[crow, 0:128], -1.0, None, ALU.mult, None, [pxk], ["r_negX"])
                            yield
                            PS_, psk = bank()
                            for hh in range(2):
                                self.mm(PS_[:, hh * 64:(hh + 1) * 64], T[:, hh, :], negX[:, hh * 64:(hh + 1) * 64], True, True,
                                        [tk, "r_negX"], [psk])
                            self.cp(SA[crow, :], PS_[crow, 0:128], [psk], ["r_SA"], eng="act_copy")
                            self.cp(SApad[crow, :].rearrange("p (a b) -> p a b", b=192)[:, :, 0:64],
                                    PS_[crow, 0:128].rearrange("p (a b) -> p a b", b=64), [psk], ["r_SApad"])
                            yield
                            PU, puk = bank()
                            self.mm(PU[:, 0:128], B_["bdec2"][:, cc, :], SA[:], True, False, [K_("r_bdec2"), "r_SA"], [puk])
                            self.mm(PU[:, 0:128], B_["kdec2"][:, cc, :], v_tok[:], False, True, [K_("r_kdec2"), K_("r_vtok")], [puk])
                            self.mm(PY[:, cl], rwS[:, hp, :], Rt[:, cl], True, False, ["rwS", "r_Rt"], [pyk], skip=True)
                            for hh in range(2):
                                self.mm(PY[:, cl], SApad[:, hh * 128:(hh + 1) * 128], MW[:, hh, 3, ccol], False, False,
                                        ["r_SApad", K_("r_MW")], [pyk], skip=True)
                            for hh in range(2):
                                self.mm(PY[:, cl], B_["vpad"][:, hh * 128:(hh + 1) * 128], MW[:, hh, 1, ccol], False, hh == 1,
                                        [K_("r_vpad"), K_("r_MW")], [pyk], skip=True)
                            for hh in range(2):
                                r2 = slice(hh * 64, hh * 64 + 64)
                                self.stt(rwS[r2, hp, hh * 64:(hh + 1) * 64], rwS[r2, hp, hh * 64:(hh + 1) * 64], egl[r2, b * 2 + cc:b * 2 + cc + 1],
                                         PU[r2, hh * 64:(hh + 1) * 64], ALU.mult, ALU.add, ["rwS", "r_egl", puk], ["rwS"])
                            yield

                    def drive(gens):
                        gens = list(gens)
                        while gens:
                            for g in list(gens):
                                try:
                                    next(g)
                                except StopIteration:
                                    gens.remove(g)

                    drive([gen_pre(0, BS[0])])
                    for b in range(4):
                        gl_ = [gen_chain(b, BS[b % 2])]
                        if b + 1 < 4:
                            gl_.append(gen_pre(b + 1, BS[(b + 1) % 2]))
                        drive(gl_)
                    if RS <= 6:
                        continue
                    self.cp(Ysb[:], PY[:], [pyk], ["r_kdec"], eng="act_copy")
                    ps, pk = bank()
                    self.mm(ps[:], self.blk64[:], Ysb[:], True, True, ["blk64", "r_kdec"], [pk])
                    self.tt(Ysb[:], Ysb[:], ps[:], ALU.subtract, ["r_kdec", pk], ["r_kdec"])
                    self.act(tmp[:], Ysb[:], AF.Square, ["r_kdec"], ["r_tmp"])
                    ps, pk = bank()
                    self.mm(ps[:], self.blk64[:], tmp[:], True, True, ["blk64", "r_tmp"], [pk])
                    self.act(tmp[:], ps[:], AF.Sqrt, [pk, "c_eps"], ["r_tmp"], bias=self.c_eps[:, 1:2])
                    self.recip(tmp[:], tmp[:], ["r_tmp"], ["r_tmp"])
                    self.tt(Ysb[:], Ysb[:], tmp[:], ALU.mult, ["r_kdec", "r_tmp"], ["r_kdec"])
                    self.ts(Ysb[:], Ysb[:], V("rwkv_ln_w", hp), V("rwkv_ln_b", hp), ALU.mult, ALU.add, ["r_kdec", "vec"], ["r_kdec"])
                    self.tt(Ysb[:], Ysb[:], bon[:], ALU.add, ["r_kdec", "r_bon"], ["r_kdec"])
                    self.tt(self.yT[:, hp, tsl], Ysb[:], g_[:], ALU.mult, ["r_kdec", "r_g"], ["yT"])

    def mix_merge(self, slot):
        TS, NT = self.TS, self.NT
        with contextlib.ExitStack() as st:
            mg = self.sb(st, "mg", [128, 8, TS], BF16)
            wbr = [self.sb(st, "mg_wbr%d" % i, [128, 3, 4, 128], BF16) for i in range(2)]
            wo = [self.sb(st, "mg_wo%d" % i, [128, 8, 128], BF16) for i in range(2)]
            gsb = self.sb(st, "mg_g", [128, 512], F32)
            acc = self.sb(st, "mg_acc", [128, 512], F32)
            tmp = self.sb(st, "mg_tmp", [128, 512], F32)
            it = 0
            for dc in range(8):
                wb_, wbk = wbr[dc % 2], "mg_wbr%d" % (dc % 2)
                self.dma(wb_[:].rearrange("p a b c -> p (a b c)"), self.w_br[slot, dc], [], [wbk], eng="pool")
                wg = []
                for j in range(3):
                    wg.append(self.win_tile(slot, 30 + j * 8 + dc, self.wib, self.wcnt))
                for nt in range(NT):
                    tsl = slice(nt * 512, (nt + 1) * 512)
                    for j in range(3):
                        k = it % 2
                        it += 1
                        PG, pgk = self.PB[k], "pb%d" % k
                        PP, ppk = self.PB[2 + k], "pb%d" % (2 + k)
                        self.proj(PG[:], pgk, wg[j][0], wg[j][1], tsl)
                        for c in range(4):
                            self.mm(PP[:], wb_[:, j, c, :], self.yT[:, j * 4 + c, tsl], c == 0, c == 3, [wbk, "yT"], [ppk])
                        self.act(gsb[:], PG[:], AF.Sigmoid, [pgk], ["mg_g"])
                        if j == 0:
                            self.tt(acc[:], PP[:], gsb[:], ALU.mult, [ppk, "mg_g"], ["mg_acc"])
                        elif j == 1:
                            self.tt(tmp[:], PP[:], gsb[:], ALU.mult, [ppk, "mg_g"], ["mg_tmp"])
                            self.tt(acc[:], acc[:], tmp[:], ALU.add, ["mg_acc", "mg_tmp"], ["mg_acc"], eng="pool")
                        else:
                            self.tt(tmp[:], PP[:], gsb[:], ALU.mult, [ppk, "mg_g"], ["mg_tmp"])
                            self.tt(mg[:, dc, tsl], acc[:], tmp[:], ALU.add, ["mg_acc", "mg_tmp"], ["mg"], eng="pool")
            for dc2 in range(8):
                w, wk = wo[dc2 % 2], "mg_wo%d" % (dc2 % 2)
                self.dma(w[:].rearrange("p a b -> p (a b)"), self.w_o[slot, dc2], [], [wk], eng="pool")
                for nt in range(NT):
                    tsl = slice(nt * 512, (nt + 1) * 512)
                    k = 4 + it % 2
                    it += 1
                    PO, pok = self.PB[k], "pb%d" % k
                    for dc in range(8):
                        self.mm(PO[:], w[:, dc, :], mg[:, dc, tsl], dc == 0, dc == 7, [wk, "mg"], [pok])
                    self.tt(self.xT[:, dc2, tsl], self.xT[:, dc2, tsl], PO[:], ALU.add, ["xT", pok], ["xT"])

def _fm(v, ncol):
    return np.ascontiguousarray(np.asarray(v, np.float32).reshape(ncol, 128).T)


def _wtile(w, cols):
    K, C = w.shape
    nk = K // 128
    nt = C // cols
    a = np.asarray(w, np.float32).reshape(nk, 128, nt, cols).transpose(2, 1, 0, 3)
    return np.ascontiguousarray(a).reshape(nt, 128, nk * cols)


def prep_layer(inp, l, final_norm):
    f32 = np.float32
    if l is None:
        z = lambda *s: np.zeros(s, f32)
        vec = z(128, NV)
        vec[:, VOFF["rwkv_v0"]:VOFF["rwkv_v0"] + 4] = -1e4
        vec[:, VOFF["final_norm"]:VOFF["final_norm"] + 8] = _fm(final_norm, 8)
        return dict(vecs=vec, w_f1i=z(NFF, 128, 2048), w_f1o=z(8, 128, NFF * 128), w_f2i=z(NFF, 128, 2048),
                    w_f2o=z(8, 128, NFF * 128), w_in=z(54, 128, 1024), w_al=z(128, 128), w_kv=z(128, 8192),
                    w_br=z(8, 128, 1536), w_o=z(8, 128, 1024), w_w2=z(64, 512), w_a2=z(64, 512), w_g2=z(128, 512),
                    w_v1=z(128, 128), w_v2=z(32, 512), w_aup=z(16, 256))
    vec = np.zeros((128, NV), f32)

    def put(name, v, n):
        vec[:, VOFF[name]:VOFF[name] + n] = _fm(v, n)
    put("ffn1_norm", inp["ffn1_norm"][l], 8)
    put("mix_norm", inp["mix_norm"][l], 8)
    put("mem_norm", inp["mem_norm"][l], 8)
    put("ffn2_norm", inp["ffn2_norm"][l], 8)
    put("final_norm", final_norm, 8)
    put("rwkv_mu", inp["rwkv_mu"][l], 14)
    for n in ("rwkv_w0", "rwkv_a0", "rwkv_k_k", "rwkv_k_a", "rwkv_ln_w", "rwkv_ln_b"):
        put(n, inp[n][l], 4)
    put("rwkv_r_k", np.asarray(inp["rwkv_r_k"][l]).reshape(-1), 4)
    if l == 0:
        vec[:, VOFF["rwkv_v0"]:VOFF["rwkv_v0"] + 4] = -1e4
        v1 = np.zeros((512, 32), f32)
        v2 = np.zeros((32, 512), f32)
    else:
        put("rwkv_v0", inp["rwkv_v0"][l - 1], 4)
        v1 = np.asarray(inp["rwkv_v1"][l - 1], f32)
        v2 = np.asarray(inp["rwkv_v2"][l - 1], f32)
    for j in range(4):
        put("conv%d" % j, inp["gla_conv"][l][j], 8)
    put("gla_a_bias", inp["gla_a_bias"][l], 2)
    put("gla_norm", inp["gla_norm"][l], 4)

    def ffn_in(w):
        w = np.asarray(w, f32)
        g = _wtile(w[:, :DFF], 128).reshape(NFF, 128, 8, 128)
        u = _wtile(w[:, DFF:], 128).reshape(NFF, 128, 8, 128)
        return np.ascontiguousarray(np.concatenate([g, u], axis=3)).reshape(NFF, 128, 2048)

    def ffn_out(w):
        return _wtile(np.asarray(w, f32), 128)

    win = np.asarray(inp["w_in"][l], f32)
    full = np.concatenate([win[:, 0:2816], win[:, 2832:]], axis=1)
    wbr = np.asarray(inp["w_branch"][l], f32)
    br = np.stack([_wtile(wbr[j], 128).reshape(8, 128, 4 * 128) for j in range(3)], axis=2)
    return dict(
        vecs=vec,
        w_f1i=ffn_in(inp["ffn1_w_in"][l]), w_f1o=ffn_out(inp["ffn1_w_out"][l]),
        w_f2i=ffn_in(inp["ffn2_w_in"][l]), w_f2o=ffn_out(inp["ffn2_w_out"][l]),
        w_in=_wtile(full, 128), w_al=_wtile(win[:, 2816:2832], 16)[0],
        w_kv=_wtile(np.asarray(inp["xa_w_kv"][l], f32), 1024)[0],
        w_br=np.ascontiguousarray(br).reshape(8, 128, 1536),
        w_o=_wtile(np.asarray(inp["w_out"][l], f32), 128),
        w_w2=np.asarray(inp["rwkv_w2"][l], f32), w_a2=np.asarray(inp["rwkv_a2"][l], f32),
        w_g2=np.asarray(inp["rwkv_g2"][l], f32), w_v1=_wtile(v1, 32)[0], w_v2=v2,
        w_aup=np.asarray(inp["gla_a_up"][l], f32))


def x_to_fm(x, n_sub, TS):
    a = np.asarray(x, np.float32).reshape(n_sub, TS, 8, 128).transpose(0, 3, 2, 1)
    return np.ascontiguousarray(a)


def fm_to_x(a):
    n_sub, _, _, TS = a.shape
    return np.ascontiguousarray(a.transpose(0, 3, 2, 1)).reshape(n_sub * TS, 1024)


_PROG = {}


def _get_prog(key, **kw):
    if key not in _PROG:
        b = B(**kw)
        b.build()
        _PROG[key] = b
    return _PROG[key]


def _in_map(L, x_fm, vf, st_in, fl, memT):
    m = {k: v[None] for k, v in L.items()}
    m.update(x_in=x_fm, vf_in=vf, st_in=st_in, flags=fl, memT=memT)
    return m


def kernel(**inp):
    inp = {k: np.asarray(v) for k, v in inp.items()}
    x, mem = inp["x"], inp["mem"]
    Bn, SEQ, _ = x.shape
    NSUB, TS, NSL = 2, 1024, 5
    HALF = NSUB * TS
    assert SEQ == 2 * HALF and Bn == 4
    return _run_fused(inp, x, mem, Bn, NSUB, TS, NSL)


def _run_fused(inp, x, mem, Bn, NSUB, TS, NSL, nlayers=4):
    HALF = NSUB * TS
    fn = inp["final_norm"]
    layers = [prep_layer(inp, l, fn) for l in range(nlayers)]
    dummy = prep_layer(inp, None, fn)
    ncores = 2 * Bn
    prog = _get_prog(("fused", ncores, NSUB, TS, NSL), n_slots=NSL, n_sub=NSUB, TS=TS, norm_after=[NSL - 2, NSL - 1], n_cores=ncores)
    maps = []
    for c in range(ncores):
        b, half = c // 2, c % 2
        ls = [(k if half == 0 else k - 1) for k in range(NSL)]
        Ls = [layers[l] if 0 <= l < nlayers else dummy for l in ls]
        m = {key: np.stack([L[key] for L in Ls]) for key in dummy}
        fl = np.zeros((128, 2 * NSL), np.float32)
        for k, l in enumerate(ls):
            fl[:, 2 * k] = 1.0 if l == 0 else 0.0
            fl[:, 2 * k + 1] = 1.0 if half == 1 else 0.0
        m.update(x_in=x_to_fm(x[b, half * HALF:(half + 1) * HALF], NSUB, TS),
                 vf_in=np.zeros((NSUB, 128, 4, TS), np.float32), st_in=np.zeros((128, NST), np.float32), flags=fl,
                 memT=np.ascontiguousarray(mem[b].reshape(NMEM, 8, 128).transpose(2, 1, 0)).astype(np.float32))
        maps.append(m)
    res = run_bass_kernel_spmd(prog.nc, maps, core_ids=list(range(ncores))).results
    out = np.zeros((Bn, 2 * HALF, D), np.float32)
    for c in range(ncores):
        b, half = c // 2, c % 2
        y = np.asarray(res[c]["y_out"])[half]
        out[b, half * HALF:(half + 1) * HALF] = fm_to_x(y)
    return out


def kernel_unfused(**inp):
    inp = {k: np.asarray(v) for k, v in inp.items()}
    x, mem = inp["x"], inp["mem"]
    Bn, SEQ, _ = x.shape
    NSUB, TS = 2, 1024
    HALF = NSUB * TS
    assert SEQ == 2 * HALF and Bn == 4
    fn = inp["final_norm"]
    layers = [prep_layer(inp, l, fn) for l in range(4)]
    dummy = prep_layer(inp, None, fn)
    prog = _get_prog("slot1", n_slots=1, n_sub=NSUB, TS=TS, norm_after=[0])
    ncores = 8
    memT = [np.ascontiguousarray(mem[b].reshape(NMEM, 8, 128).transpose(2, 1, 0)).astype(np.float32) for b in range(Bn)]
    xs = [x_to_fm(x[c // 2, (c % 2) * HALF:(c % 2 + 1) * HALF], NSUB, TS) for c in range(ncores)]
    vfs = [np.zeros((NSUB, 128, 4, TS), np.float32) for _ in range(ncores)]
    st_prev = [np.zeros((128, NST), np.float32) for _ in range(Bn)]
    ys = [None] * ncores
    for k in range(5):
        maps = []
        for c in range(ncores):
            b, half = c // 2, c % 2
            l = k if half == 0 else k - 1
            L = layers[l] if 0 <= l <= 3 else dummy
            fl = np.zeros((128, 2), np.float32)
            fl[:, 0] = 1.0 if l == 0 else 0.0
            fl[:, 1] = 1.0 if half == 1 else 0.0
            maps.append(_in_map(L, xs[c], vfs[c], st_prev[b], fl, memT[b]))
        res = run_bass_kernel_spmd(prog.nc, maps, core_ids=list(range(ncores))).results
        for c in range(ncores):
            b, half = c // 2, c % 2
            l = k if half == 0 else k - 1
            xs[c] = np.asarray(res[c]["x_out"])
            vfs[c] = np.asarray(res[c]["vf_out"])
            if l == 3:
                ys[c] = np.asarray(res[c]["y_out"])[0]
        for c in range(0, ncores, 2):
            st_prev[c // 2] = np.asarray(res[c]["st_out"])
    out = np.zeros((Bn, SEQ, D), np.float32)
    for c in range(ncores):
        out[c // 2, (c % 2) * HALF:(c % 2 + 1) * HALF] = fm_to_x(ys[c])
    return out
```

```python
import contextlib
import numpy as np
import concourse.bass as bass
import concourse.mybir as mybir
from concourse.bass_utils import run_bass_kernel_spmd

F32 = mybir.dt.float32
BF16 = mybir.dt.bfloat16
AF = mybir.ActivationFunctionType
ALU = mybir.AluOpType

D = 1024
DFF = 2816
NFF = DFF // 128
NMEM = 256
EPS = 1e-6
LN_EPS = 64e-5
WCOLS = 6928

ENGS = ("pe", "act", "dve", "pool", "sp")
EPOCH = 8000
NDMASEM = 44
DMASEM_RANGE = {"sp": (0, 16), "pool": (16, 16), "act": (32, 8), "cc": (40, 4)}


class Sched:
    def __init__(self, nc):
        self.nc = nc
        self.ops = {e: [] for e in ENGS}
        self.last_write = {}
        self.rd_e = {}
        self.rd_d = {}
        self.seen = {e: {} for e in ENGS}
        self.dma_seen = {e: {} for e in ENGS}
        self.dma_rr_e = {}
        self.dma_val = [0] * NDMASEM
        self.dma_last = [None] * NDMASEM
        self.phase_tok = None

    def add(self, eng, fn, reads=(), writes=(), dma=False, cc=False):
        ops = self.ops[eng]
        idx = len(ops)
        deps = set()
        pbr = [k for k in reads if k.startswith("pb") and k not in writes]
        if pbr:
            writes = list(writes) + pbr
        if self.phase_tok is not None:
            deps.add(self.phase_tok)
        for b in reads:
            lw = self.last_write.get(b)
            if lw is not None:
                deps.add(lw)
        for b in writes:
            lw = self.last_write.get(b)
            if lw is not None:
                deps.add(lw)
            for e2, i2 in self.rd_e.get(b, {}).items():
                deps.add(("e", e2, i2))
            for tok in self.rd_d.get(b, ()):
                deps.add(tok)
        op = dict(fn=fn, waits=[], inc=False, dma=None)
        if dma or cc:
            rk = "cc" if cc else eng
            base, cnt = DMASEM_RANGE[rk]
            s = base + self.dma_rr_e.get(rk, 0)
            self.dma_rr_e[rk] = (self.dma_rr_e.get(rk, 0) + 1) % cnt
            if self.dma_last[s] is not None:
                deps.add(self.dma_last[s])
            amt = 1 if cc else 16
            self.dma_val[s] += amt
            tok = ("d", s, self.dma_val[s])
            self.dma_last[s] = tok
            op["dma"] = (s, self.dma_val[s], amt)
        else:
            tok = ("e", eng, idx)
        best = {}
        for d in deps:
            if d[0] == "e":
                _, se, si = d
                if se == eng and (eng == "pe" or si >= idx):
                    continue
                if self.seen[eng].get(se, -1) >= si:
                    continue
                if best.get(("e", se), -1) < si:
                    best[("e", se)] = si
            else:
                _, s, v = d
                if self.dma_seen[eng].get(s, 0) >= v:
                    continue
                if best.get(("d", s), 0) < v:
                    best[("d", s)] = v
        for k, v in best.items():
            if k[0] == "e":
                self.seen[eng][k[1]] = v
                self.ops[k[1]][v]["inc"] = True
                op["waits"].append(("e", k[1], v))
            else:
                self.dma_seen[eng][k[1]] = v
                op["waits"].append(("d", k[1], v))
        ops.append(op)
        for b in reads:
            if tok[0] == "e":
                m = self.rd_e.setdefault(b, {})
                m[eng] = idx
            else:
                self.rd_d.setdefault(b, set()).add(tok)
        for b in writes:
            self.last_write[b] = tok
            self.rd_e[b] = {}
            self.rd_d[b] = set()
        return tok

    def barrier(self, scratch_ap):
        deps = []
        for e in ENGS:
            if self.ops[e]:
                deps.append(e)
        keys_w = ["__barrier__"]
        idx = len(self.ops["dve"])
        op = dict(fn=lambda e: e.memset(scratch_ap, 0.0), waits=[], inc=True, dma=None)
        for e in ENGS:
            if not self.ops[e]:
                continue
            li = len(self.ops[e]) - 1
            while li >= 0 and self.ops[e][li]["dma"] is not None:
                li -= 1
            if li >= 0 and self.seen["dve"].get(e, -1) < li:
                self.seen["dve"][e] = li
                self.ops[e][li]["inc"] = True
                op["waits"].append(("e", e, li))
        for s in range(NDMASEM):
            if self.dma_last[s] is not None and self.dma_seen["dve"].get(s, 0) < self.dma_val[s]:
                self.dma_seen["dve"][s] = self.dma_val[s]
                op["waits"].append(("d", s, self.dma_val[s]))
        self.ops["dve"].append(op)
        self.phase_tok = ("e", "dve", idx)
        self.last_write = {}
        self.rd_e = {}
        self.rd_d = {}

    def emit(self, final_waits=()):
        nc = self.nc
        for t in final_waits:
            if t[0] == "e":
                self.ops[t[1]][t[2]]["inc"] = True
        with contextlib.ExitStack() as st:
            esems = {}
            counts = {}
            for e in ENGS:
                c = 0
                cl = []
                for op in self.ops[e]:
                    if op["inc"] and op["dma"] is None:
                        c += 1
                    cl.append(c)
                counts[e] = cl
                nep = max(0, c - 1) // EPOCH + 1
                esems[e] = [st.enter_context(nc.semaphore(f"s_{e}_{k}")) for k in range(nep)]
            dsems = [st.enter_context(nc.semaphore(f"s_dma_{k}")) for k in range(NDMASEM)]
            block = st.enter_context(nc.Block())

            def split(c):
                ep = (c - 1) // EPOCH
                return ep, c - ep * EPOCH

            def do_wait(engine, w):
                if w[0] == "e":
                    ep, r = split(counts[w[1]][w[2]])
                    engine.wait_ge(esems[w[1]][ep], r)
                else:
                    engine.wait_ge(dsems[w[1]], w[2])

            def run(e):
                def body(engine):
                    for i, op in enumerate(self.ops[e]):
                        for w in op["waits"]:
                            do_wait(engine, w)
                        ins = op["fn"](engine)
                        if op["dma"] is not None:
                            ins.then_inc(dsems[op["dma"][0]], op["dma"][2])
                        elif op["inc"]:
                            ep, r = split(counts[e][i])
                            ins.then_inc(esems[e][ep], 1)
                    if e == "sp":
                        for w in final_waits:
                            do_wait(engine, w)
                return body

            block.tensor(run("pe"))
            block.scalar(run("act"))
            block.vector(run("dve"))
            block.gpsimd(run("pool"))
            block.sync(run("sp"))


VEC_SPEC = [("ffn1_norm", 8), ("mix_norm", 8), ("mem_norm", 8), ("ffn2_norm", 8), ("final_norm", 8),
            ("rwkv_mu", 14), ("rwkv_w0", 4), ("rwkv_a0", 4), ("rwkv_k_k", 4), ("rwkv_k_a", 4),
            ("rwkv_r_k", 4), ("rwkv_ln_w", 4), ("rwkv_ln_b", 4), ("rwkv_v0", 4),
            ("conv0", 8), ("conv1", 8), ("conv2", 8), ("conv3", 8), ("gla_a_bias", 2), ("gla_norm", 4)]
VOFF = {}
_o = 0
for _n, _c in VEC_SPEC:
    VOFF[_n] = _o
    _o += _c
NV = _o
NST = 512 + 256 + 14 + 24


class B:
    def __init__(self, n_slots, n_sub, TS, norm_after, stages=("ffn1", "mix", "ffn2"), dbg=(), n_cores=8):
        self.n_cores = n_cores
        self.nc = nc = bass.Bass("TRN2", target_bir_lowering=False)
        self.S = Sched(nc)
        self.NSL, self.NSUB, self.TS = n_slots, n_sub, TS
        self.NT = TS // 512
        self.stages = stages
        self.norm_after = list(norm_after)
        self.dbg = dbg
        self.dbg_out = {}
        self.mix_parts = ("xa", "gla", "rwkv", "merge")
        NSL = n_slots
        din = lambda n, sh: nc.dram_tensor(n, sh, F32, kind="ExternalInput").ap()
        dout = lambda n, sh: nc.dram_tensor(n, sh, F32, kind="ExternalOutput").ap()
        self.x_in = din("x_in", [n_sub, 128, 8, TS])
        self.x_out = dout("x_out", [n_sub, 128, 8, TS])
        self.y_out = dout("y_out", [max(1, len(self.norm_after)), n_sub, 128, 8, TS])
        self.memT = din("memT", [128, 8, NMEM])
        self.vecs = din("vecs", [NSL, 128, NV])
        self.flags = din("flags", [128, 2 * NSL])
        self.st_in = din("st_in", [128, NST])
        self.st_out = dout("st_out", [128, NST])
        self.vf_in = din("vf_in", [n_sub, 128, 4, TS])
        self.vf_out = dout("vf_out", [n_sub, 128, 4, TS])
        self.w_f1i = din("w_f1i", [NSL, NFF, 128, 8 * 256])
        self.w_f1o = din("w_f1o", [NSL, 8, 128, NFF * 128])
        self.w_f2i = din("w_f2i", [NSL, NFF, 128, 8 * 256])
        self.w_f2o = din("w_f2o", [NSL, 8, 128, NFF * 128])
        self.w_in = din("w_in", [NSL, 54, 128, 8 * 128])
        self.w_al = din("w_al", [NSL, 128, 8 * 16])
        self.w_kv = din("w_kv", [NSL, 128, 8 * 1024])
        self.w_br = din("w_br", [NSL, 8, 128, 3 * 4 * 128])
        self.w_o = din("w_o", [NSL, 8, 128, 8 * 128])
        self.w_w2 = din("w_w2", [NSL, 64, 512])
        self.w_a2 = din("w_a2", [NSL, 64, 512])
        self.w_g2 = din("w_g2", [NSL, 128, 512])
        self.w_v1 = din("w_v1", [NSL, 128, 4 * 32])
        self.w_v2 = din("w_v2", [NSL, 32, 512])
        self.w_aup = din("w_aup", [NSL, 16, 256])
        self.uid = 0
        self.st_snd = nc.dram_tensor("st_snd", [128, NST], F32).ap()
        self.st_pair = nc.dram_tensor("st_pair", [256, NST], F32).ap()

    def mm(self, out, lhsT, rhs, start, stop, r, w, skip=False):
        if skip:
            self.S.add("pe", lambda e: e.matmul(out, lhsT, rhs, start=start, stop=stop, skip_group_check=True), r, w)
        else:
            self.S.add("pe", lambda e: e.matmul(out, lhsT, rhs, start=start, stop=stop), r, w)

    def act(self, out, in_, func, r, w, bias=None, scale=None, eng="act"):
        kw = {}
        if bias is not None:
            kw["bias"] = bias
        if scale is not None:
            kw["scale"] = scale
        self.S.add("act", lambda e: e.activation(out=out, in_=in_, func=func, **kw), r, w)

    def tt(self, out, in0, in1, op, r, w, eng="dve"):
        self.S.add(eng, lambda e: e.tensor_tensor(out=out, in0=in0, in1=in1, op=op), r, w)

    def ts(self, out, in0, s1, s2, op0, op1, r, w, eng="dve"):
        if op1 is None:
            self.S.add(eng, lambda e: e.tensor_scalar(out=out, in0=in0, scalar1=s1, scalar2=None, op0=op0), r, w)
        else:
            self.S.add(eng, lambda e: e.tensor_scalar(out=out, in0=in0, scalar1=s1, scalar2=s2, op0=op0, op1=op1), r, w)

    def stt(self, out, in0, scalar, in1, op0, op1, r, w):
        self.S.add("dve", lambda e: e.scalar_tensor_tensor(out=out, in0=in0, scalar=scalar, in1=in1, op0=op0, op1=op1), r, w)

    def cp(self, out, in_, r, w, eng="dve"):
        if eng == "act_copy":
            self.S.add("act", lambda e: e.activation(out=out, in_=in_, func=AF.Copy), r, w)
        else:
            self.S.add(eng, lambda e: e.tensor_copy(out=out, in_=in_), r, w)

    def recip(self, out, in_, r, w):
        self.S.add("dve", lambda e: e.reciprocal(out=out, in_=in_), r, w)

    def dma(self, out, in_, r, w, eng="sp"):
        return self.S.add(eng, lambda e: e.dma_start(out=out, in_=in_), r, w, dma=True)

    def memset(self, ap, val, w, eng="pool"):
        self.S.add(eng, lambda e: e.memset(ap, val), (), w)

    def dump(self, name, ap, shape, key):
        if name not in self.dbg:
            return
        t = self.nc.dram_tensor("dbg_" + name, list(shape), ap.dtype if hasattr(ap, "dtype") else F32, kind="ExternalOutput").ap()
        self.dbg_out[name] = self.dma(t, ap, [key], ["dbg_" + name])

    def sb(self, st, name, shape, dt):
        self.uid += 1
        return st.enter_context(self.nc.sbuf_tensor("%s_%d" % (name, self.uid), list(shape), dt))

    def build(self):
        nc, S, TS, NT = self.nc, self.S, self.TS, self.NT
        with contextlib.ExitStack() as st:
            self.PB = [st.enter_context(nc.psum_tensor(f"pb{i}", [128, 512], F32)) for i in range(8)]
            self.c_scr = self.sb(st, "c_scr", [128, 8], F32)
            self.alloc_consts(st)
            self.vec = self.sb(st, "vec", [128, NV], F32)
            self.flg = self.sb(st, "flg", [128, 2 * self.NSL], F32)
            self.rwS = self.sb(st, "rwS", [128, 4, 128], F32)
            self.glS = self.sb(st, "glS", [128, 2, 128], F32)
            self.glSb = self.sb(st, "glSb", [128, 2, 128], BF16)
            self.uprev = self.sb(st, "uprev", [128, 14], F32)
            self.convh = self.sb(st, "convh", [128, 8, 3], F32)
            self.xT = self.sb(st, "xT", [128, 8, TS], F32)
            self.hT = self.sb(st, "hT", [128, 8, TS], BF16)
            self.init_consts()
            self.dma(self.flg[:], self.flags, [], ["flg"])
            finals = []
            for slot in range(self.NSL):
                self.dma(self.vec[:], self.vecs[slot], [], ["vec"])
                self.load_state(slot)
                for sub in range(self.NSUB):
                    src = self.x_in if slot == 0 else self.x_out
                    self.dma(self.xT[:], src[sub], ["xdram%d" % sub], ["xT"])
                    if "ffn1" in self.stages:
                        self.ffn(st, slot, self.w_f1i, self.w_f1o, VOFF["ffn1_norm"])
                    if "mix" in self.stages:
                        self.mixer(st, slot, sub)
                    if sub == self.NSUB - 1:
                        finals.append(self.store_state(slot))
                    if "ffn2" in self.stages:
                        self.ffn(st, slot, self.w_f2i, self.w_f2o, VOFF["ffn2_norm"])
                    t = self.dma(self.x_out[sub], self.xT[:], ["xT"], ["xdram%d" % sub])
                    if slot == self.NSL - 1:
                        finals.append(t)
                    if slot in self.norm_after:
                        self.rmsnorm(VOFF["final_norm"], out_f32=True)
                        t = self.dma(self.y_out[self.norm_after.index(slot), sub], self.xT[:], ["xT"], ["ydram"])
                        finals.append(t)
            for k, t in self.dbg_out.items():
                finals.append(t)
            S.emit(final_waits=finals)
        return nc

    def alloc_consts(self, st):
        self.ident = self.sb(st, "ident", [128, 128], F32)
        self.on_d = self.sb(st, "on_d", [128, 128], BF16)
        self.on_1 = self.sb(st, "on_1", [128, 128], BF16)
        self.on_128 = self.sb(st, "on_128", [128, 128], BF16)
        self.blk = self.sb(st, "blk", [128, 128], F32)
        self.blk64 = self.sb(st, "blk64", [128, 128], F32)
        self.triu = self.sb(st, "triu", [128, 128], F32)
        self.mpast = self.sb(st, "mpast", [128, 4, 128], F32)
        self.mfut = self.sb(st, "mfut", [128, 4, 128], F32)
        self.mstr = self.sb(st, "mstr", [128, 4, 128], F32)
        self.smask = self.sb(st, "smask", [128, 512], F32)
        self.c_eps = self.sb(st, "c_eps", [128, 4], F32)

    def init_consts(self):
        S = self.S
        ms = self.memset
        ms(self.ident[:], 1.0, ["ident"])
        S.add("pool", lambda e: e.affine_select(out=self.ident[:], in_=self.ident[:], pattern=[[-1, 128]],
                                                compare_op=ALU.is_equal, fill=0.0, base=0, channel_multiplier=1),
              ["ident"], ["ident"])
        ms(self.on_d[:], 1.0 / D, ["on_d"])
        ms(self.on_1[:], 1.0, ["on_1"])
        ms(self.on_128[:], 1.0 / 128, ["on_128"])
        ms(self.blk[:], 0.0, ["blk"])
        ms(self.blk[0:64, 0:64], 1.0, ["blk"])
        ms(self.blk[64:128, 64:128], 1.0, ["blk"])
        self.ts(self.blk64[:], self.blk[:], 1.0 / 64, None, ALU.mult, None, ["blk"], ["blk64"], eng="pool")
        ms(self.triu[:], 1.0, ["triu"])
        S.add("pool", lambda e: e.affine_select(out=self.triu[:], in_=self.triu[:], pattern=[[1, 128]],
                                                compare_op=ALU.is_ge, fill=0.0, base=0, channel_multiplier=-1),
              ["triu"], ["triu"])
        for h in range(4):
            self.tt(self.mpast[:, h, :], self.triu[:], self.blk[:], ALU.mult, ["triu", "blk"], ["mpast"], eng="pool")
        for h in range(4):
            self.tt(self.mfut[:, h, :], self.blk[:], self.mpast[:, h, :], ALU.subtract, ["blk", "mpast"], ["mfut"], eng="pool")
            self.tt(self.mstr[:, h, :], self.mpast[:, h, :], self.ident[:], ALU.subtract, ["ident", "mpast"], ["mstr"], eng="pool")
        ms(self.c_eps[:, 0:1], EPS, ["c_eps"])
        ms(self.c_eps[:, 1:2], LN_EPS, ["c_eps"])
        ms(self.c_eps[:, 2:3], 1.0, ["c_eps"])
        ms(self.c_eps[:, 3:4], 0.0, ["c_eps"])
        ms(self.smask[:], 1.0, ["smask"])
        ms(self.smask[:].rearrange("p (c s) -> p c s", s=64)[:, :, 0:1], 0.0, ["smask"])

    def load_state(self, slot):
        f = self.flg[:, 2 * slot + 1:2 * slot + 2]
        tmp = [(self.rwS[:].rearrange("p a b -> p (a b)"), 0, 512, "rwS"),
               (self.glS[:].rearrange("p a b -> p (a b)"), 512, 256, "glS"),
               (self.uprev[:], 768, 14, "uprev"),
               (self.convh[:].rearrange("p a b -> p (a b)"), 782, 24, "convh")]
        for ap, o, n, key in tmp:
            if slot == 0:
                self.dma(ap, self.st_in[:, o:o + n], [], [key])
            else:
                self.dma(ap, self.st_pair[0:128, o:o + n], ["st_pair"], [key])
            self.ts(ap, ap, f, None, ALU.mult, None, [key, "flg"], [key])

    def store_state(self, slot):
        t = None
        tmp = [(self.rwS[:].rearrange("p a b -> p (a b)"), 0, 512, "rwS"),
               (self.glS[:].rearrange("p a b -> p (a b)"), 512, 256, "glS"),
               (self.uprev[:], 768, 14, "uprev"),
               (self.convh[:].rearrange("p a b -> p (a b)"), 782, 24, "convh")]
        last = slot == self.NSL - 1
        for ap, o, n, key in tmp:
            if last:
                t = self.dma(self.st_out[:, o:o + n], ap, [key], ["st_out%d" % o])
            else:
                t = self.dma(self.st_snd[:, o:o + n], ap, [key], ["st_snd"])
        if not last:
            groups = [[2 * i, 2 * i + 1] for i in range(self.n_cores // 2)]
            snd, pair = self.st_snd, self.st_pair
            t = self.S.add("pool", lambda e: e.collective_compute("AllGather", ALU.bypass, replica_groups=groups,
                                                                  ins=[snd.opt()], outs=[pair.opt()]),
                           ["st_snd"], ["st_pair"], cc=True)
        return t

    def rmsnorm(self, goff, out_f32=False):
        with contextlib.ExitStack() as st:
            sq = self.sb(st, "n_sq", [128, 8, 512], BF16)
            rs = self.sb(st, "n_rs", [128, 512], F32)
            for nt in range(self.NT):
                tsl = slice(nt * 512, (nt + 1) * 512)
                ps = self.PB[7]
                for c in range(8):
                    self.act(sq[:, c, :], self.xT[:, c, tsl], AF.Square, ["xT"], ["n_sq%d" % c])
                    self.mm(ps[:], self.on_d[:], sq[:, c, :], c == 0, c == 7, ["on_d", "n_sq%d" % c], ["pb7"])
                self.act(rs[:], ps[:], AF.Sqrt, ["pb7", "c_eps"], ["n_rs"], bias=self.c_eps[:, 0:1])
                self.recip(rs[:], rs[:], ["n_rs"], ["n_rs"])
                for c in range(8):
                    g = self.vec[:, goff + c:goff + c + 1]
                    dst = self.xT[:, c, tsl] if out_f32 else self.hT[:, c, tsl]
                    self.stt(dst, self.xT[:, c, tsl], g, rs[:], ALU.mult, ALU.mult, ["xT", "vec", "n_rs"],
                             ["xT" if out_f32 else "hT"])
            self.S.barrier(self.c_scr[:, 0:1])

    def ffn(self, st0, slot, w_i, w_o, goff):
        self.rmsnorm(goff)
        NT = self.NT
        with contextlib.ExitStack() as st:
            actT = self.sb(st, "f_act", [128, NFF, self.TS], BF16)
            wb = [self.sb(st, "f_wi%d" % i, [128, 8, 256], BF16) for i in range(3)]
            wo = [self.sb(st, "f_wo%d" % i, [128, NFF, 128], BF16) for i in range(2)]
            sg = [self.sb(st, "f_sg%d" % i, [128, 512], F32) for i in range(2)]
            it = 0
            for j in range(NFF):
                w = wb[j % 3]
                wk = "f_wi%d" % (j % 3)
                self.dma(w[:].rearrange("p a b -> p (a b)"), w_i[slot, j], [], [wk], eng="pool")
                for nt in range(NT):
                    tsl = slice(nt * 512, (nt + 1) * 512)
                    k = it % 2
                    it += 1
                    pg, pu = self.PB[k], self.PB[2 + k]
                    for kc in range(8):
                        self.mm(pg[:], w[:, kc, 0:128], self.hT[:, kc, tsl], kc == 0, kc == 7, [wk, "hT"], ["pb%d" % k])
                    for kc in range(8):
                        self.mm(pu[:], w[:, kc, 128:256], self.hT[:, kc, tsl], kc == 0, kc == 7, [wk, "hT"], ["pb%d" % (2 + k)])
                    self.act(sg[k][:], pg[:], AF.Silu, ["pb%d" % k], ["f_sg%d" % k])
                    self.tt(actT[:, j, tsl], sg[k][:], pu[:], ALU.mult, ["f_sg%d" % k, "pb%d" % (2 + k)], ["f_act%d" % j])
            it = 0
            for dc in range(8):
                w = wo[dc % 2]
                wk = "f_wo%d" % (dc % 2)
                self.dma(w[:].rearrange("p a b -> p (a b)"), w_o[slot, dc], [], [wk], eng="pool")
                for nt in range(NT):
                    tsl = slice(nt * 512, (nt + 1) * 512)
                    k = 4 + it % 2
                    it += 1
                    po = self.PB[k]
                    for j in range(NFF):
                        self.mm(po[:], w[:, j, :], actT[:, j, tsl], j == 0, j == NFF - 1, [wk, "f_act%d" % j], ["pb%d" % k])
                    self.stt(self.xT[:, dc, tsl], po[:], 0.5, self.xT[:, dc, tsl], ALU.mult, ALU.add,
                             ["pb%d" % k, "xT"], ["xT"])
            self.S.barrier(self.c_scr[:, 0:1])

    def win_tile(self, slot, ci, bufs, cnt):
        k = cnt[0] % len(bufs)
        cnt[0] += 1
        w = bufs[k]
        key = "m_wi%d" % k
        self.dma(w[:].rearrange("p a b -> p (a b)"), self.w_in[slot, ci], [], [key], eng="pool")
        return w, key

    def proj(self, ps, pkey, w, wkey, tsl, M=128):
        for kc in range(8):
            self.mm(ps, w[:, kc, 0:M], self.hT[:, kc, tsl], kc == 0, kc == 7, [wkey, "hT"], [pkey])

    def mixer(self, st0, slot, sub):
        self.rmsnorm(VOFF["mix_norm"])
        TS, NT = self.TS, self.NT
        self.dma(self.x_out[sub], self.xT[:], ["xT"], ["xdram%d" % sub])
        self.S.barrier(self.c_scr[:, 0:1])
        self.xslots = [self.xT[:, c, h * 512:(h + 1) * 512] for c in range(8) for h in range(TS // 512)]
        with contextlib.ExitStack() as st:
            self.yT = self.sb(st, "yT", [128, 12, TS], BF16)
            self.wib = [self.sb(st, "m_wi%d" % i, [128, 8, 128], BF16) for i in range(4)]
            self.wcnt = [0]
            if len(self.mix_parts) < 4:
                self.memset(self.yT[:].rearrange("p a b -> p (a b)"), 0.0, ["yT"])
            if "xa" in self.mix_parts:
                self.mix_xa(slot)
                self.S.barrier(self.c_scr[:, 0:1])
            if "gla" in self.mix_parts:
                self.mix_gla(slot, sub)
                self.S.barrier(self.c_scr[:, 0:1])
            if "rwkv" in self.mix_parts:
                self.mix_rwkv(slot, sub)
                self.S.barrier(self.c_scr[:, 0:1])
            self.dma(self.xT[:], self.x_out[sub], ["xdram%d" % sub], ["xT"])
            if "merge" in self.mix_parts:
                self.mix_merge(slot)
                self.S.barrier(self.c_scr[:, 0:1])
            self.dump("yT", self.yT[:], [128, 12, TS], "yT")
            self.S.barrier(self.c_scr[:, 0:1])

    def mix_xa(self, slot):
        TS, NT = self.TS, self.NT
        with contextlib.ExitStack() as st:
            mem = self.sb(st, "xa_mem", [128, 8, NMEM], F32)
            msq = self.sb(st, "xa_msq", [128, 8, NMEM], BF16)
            mrs = self.sb(st, "xa_mrs", [128, NMEM], F32)
            memn = self.sb(st, "xa_memn", [128, 8, NMEM], BF16)
            wkv = self.sb(st, "xa_wkv", [128, 8, 1024], BF16)
            KT = self.sb(st, "xa_KT", [128, 4, NMEM], BF16)
            V = self.sb(st, "xa_V", [128, 2, 512], BF16)
            qT = self.sb(st, "xa_q", [128, 512], BF16)
            E = self.sb(st, "xa_E", [128, 2, 512], BF16)
            rden = self.sb(st, "xa_rden", [128, 512], F32)
            self.dma(mem[:], self.memT, [], ["xa_mem"])
            self.dma(wkv[:].rearrange("p a b -> p (a b)"), self.w_kv[slot], [], ["xa_wkv"], eng="pool")
            ps = self.PB[0]
            for c in range(8):
                self.act(msq[:, c, :], mem[:, c, :], AF.Square, ["xa_mem"], ["xa_msq"])
                self.mm(ps[:, 0:NMEM], self.on_d[:], msq[:, c, :], c == 0, c == 7, ["on_d", "xa_msq"], ["pb0"])
            self.act(mrs[:], ps[:, 0:NMEM], AF.Sqrt, ["pb0", "c_eps"], ["xa_mrs"], bias=self.c_eps[:, 0:1])
            self.recip(mrs[:], mrs[:], ["xa_mrs"], ["xa_mrs"])
            go = VOFF["mem_norm"]
            for c in range(8):
                self.stt(memn[:, c, :], mem[:, c, :], self.vec[:, go + c:go + c + 1], mrs[:], ALU.mult, ALU.mult,
                         ["xa_mem", "vec", "xa_mrs"], ["xa_memn"])
            for h in range(4):
                ps = self.PB[h % 2]
                pk = "pb%d" % (h % 2)
                for kc in range(8):
                    self.mm(ps[:, 0:NMEM], wkv[:, kc, h * 128:(h + 1) * 128], memn[:, kc, :], kc == 0, kc == 7,
                            ["xa_wkv", "xa_memn"], [pk])
                self.act(KT[:, h, :], ps[:, 0:NMEM], AF.Copy, [pk], ["xa_KT"])
            for mb in range(2):
                ps = self.PB[2 + mb]
                pk = "pb%d" % (2 + mb)
                for kc in range(8):
                    self.mm(ps[:], memn[:, kc, mb * 128:(mb + 1) * 128], wkv[:, kc, 512:1024], kc == 0, kc == 7,
                            ["xa_wkv", "xa_memn"], [pk])
                self.act(V[:, mb, :], ps[:], AF.Copy, [pk], ["xa_V"])
            for h in range(4):
                w, wk = self.win_tile(slot, 26 + h, self.wib, self.wcnt)
                for nt in range(NT):
                    tsl = slice(nt * 512, (nt + 1) * 512)
                    self.proj(self.PB[4][:], "pb4", w, wk, tsl)
                    self.act(qT[:], self.PB[4][:], AF.Copy, ["pb4"], ["xa_q"], scale=float(128 ** -0.5))
                    for mb in range(2):
                        self.mm(self.PB[5][:], KT[:, h, mb * 128:(mb + 1) * 128], qT[:], True, True, ["xa_KT", "xa_q"], ["pb5"])
                        self.act(E[:, mb, :], self.PB[5][:], AF.Exp, ["pb5"], ["xa_E%d" % mb])
                    for mb in range(2):
                        self.mm(self.PB[6][:], V[:, mb, h * 128:(h + 1) * 128], E[:, mb, :], mb == 0, mb == 1,
                                ["xa_V", "xa_E%d" % mb], ["pb6"])
                    for mb in range(2):
                        self.mm(self.PB[7][:], self.on_1[:], E[:, mb, :], mb == 0, mb == 1, ["on_1", "xa_E%d" % mb], ["pb7"])
                    self.recip(rden[:], self.PB[7][:], ["pb7"], ["xa_rden"])
                    self.tt(self.yT[:, 8 + h, tsl], self.PB[6][:], rden[:], ALU.mult, ["pb6", "xa_rden"], ["yT"])

    def mix_gla(self, slot, sub):
        TS, NT = self.TS, self.NT
        NBLK = TS // 128
        S = self.S
        import os
        GS = int(os.environ.get("GLA_STOP", "99"))
        with contextlib.ExitStack() as st:
            sgo = self.sb(st, "g_sgo", [128, 4, TS], BF16)
            v = self.sb(st, "g_v", [128, 4, TS], F32)
            Eg = self.sb(st, "g_Eg", [128, 2, TS], F32)
            qg = self.sb(st, "g_qg", [128, 2, TS], BF16)
            qr = self.sb(st, "g_qr", [128, 2, TS], BF16)
            kg = self.sb(st, "g_kg", [128, 2, TS], BF16)
            kr = self.sb(st, "g_kr", [128, 2, TS], BF16)
            kd = self.sb(st, "g_kd", [128, 2, TS], F32)
            it = 0
            with contextlib.ExitStack() as stB:
                qk = self.sb(stB, "g_qk", [128, 4, TS], F32)
                Eng = self.sb(stB, "g_Eng", [128, 2, TS], F32)
                alT = self.sb(stB, "g_alT", [16, TS], F32)
                aup = self.sb(stB, "g_aup", [16, 256], F32)
                wal = self.sb(stB, "g_wal", [128, 8, 16], BF16)
                self.dma(aup[:], self.w_aup[slot], [], ["g_aup"])
                self.dma(wal[:].rearrange("p a b -> p (a b)"), self.w_al[slot], [], ["g_wal"], eng="pool")
                for c in range(4):
                    w, wk = self.win_tile(slot, 22 + c, self.wib, self.wcnt)
                    for nt in range(NT):
                        tsl = slice(nt * 512, (nt + 1) * 512)
                        k = it % 2
                        it += 1
                        self.proj(self.PB[k][:], "pb%d" % k, w, wk, tsl)
                        self.act(sgo[:, c, tsl], self.PB[k][:], AF.Silu, ["pb%d" % k], ["g_sgo"])
                for nt in range(NT):
                    tsl = slice(nt * 512, (nt + 1) * 512)
                    k = it % 2
                    it += 1
                    self.proj(self.PB[k][0:16, :], "pb%d" % k, wal, "g_wal", tsl, M=16)
                    self.act(alT[:, tsl], self.PB[k][0:16, :], AF.Copy, ["pb%d" % k], ["g_alT"])
                for half in range(2):
                    with contextlib.ExitStack() as stC:
                        ug = self.sb(stC, "g_ug", [128, 4, 3 + TS], F32)
                        tmp = self.sb(stC, "g_ctmp", [128, TS], F32)
                        self.cp(ug[:, :, 0:3], self.convh[:, half * 4:half * 4 + 4, :], ["convh"], ["g_ug"])
                        for c4 in range(4):
                            c = half * 4 + c4
                            w, wk = self.win_tile(slot, 14 + c, self.wib, self.wcnt)
                            for nt in range(NT):
                                tsl = slice(nt * 512, (nt + 1) * 512)
                                k = it % 2
                                it += 1
                                self.proj(self.PB[k][:], "pb%d" % k, w, wk, tsl)
                                self.act(ug[:, c4, 3 + nt * 512:3 + (nt + 1) * 512], self.PB[k][:], AF.Copy, ["pb%d" % k], ["g_ug"])
                        self.cp(self.convh[:, half * 4:half * 4 + 4, :], ug[:, :, TS:TS + 3], ["g_ug"], ["convh"])
                        for c4 in range(4):
                            c = half * 4 + c4
                            cw = [self.vec[:, VOFF["conv%d" % jj] + c:VOFF["conv%d" % jj] + c + 1] for jj in range(4)]
                            self.ts(tmp[:], ug[:, c4, 3:3 + TS], cw[3], None, ALU.mult, None, ["g_ug", "vec"], ["g_ctmp"])
                            for jj in (2, 1, 0):
                                self.stt(tmp[:], ug[:, c4, jj:jj + TS], cw[jj], tmp[:], ALU.mult, ALU.add, ["g_ug", "vec", "g_ctmp"], ["g_ctmp"])
                            dst = qk[:, c, :] if c < 4 else v[:, c - 4, :]
                            self.act(dst, tmp[:], AF.Silu, ["g_ctmp"], ["g_qk" if c < 4 else "g_v"])
                        self.S.barrier(self.c_scr[:, 0:1])
                if GS <= 2:
                    return
                with contextlib.ExitStack() as stC:
                    la = self.sb(stC, "g_la", [128, 2, TS], F32)
                    sgm = self.sb(stC, "g_sgm", [128, 512], F32)
                    ab = VOFF["gla_a_bias"]
                    for c2 in range(2):
                        for nt in range(NT):
                            tsl = slice(nt * 512, (nt + 1) * 512)
                            k = it % 2
                            it += 1
                            self.mm(self.PB[k][:], aup[0:16, c2 * 128:(c2 + 1) * 128], alT[0:16, tsl], True, True, ["g_aup", "g_alT"], ["pb%d" % k])
                            self.act(sgm[:], self.PB[k][:], AF.Sigmoid, ["pb%d" % k, "vec"], ["g_sgm"], bias=self.vec[:, ab + c2:ab + c2 + 1])
                            self.act(sgm[:], sgm[:], AF.Ln, ["g_sgm"], ["g_sgm"])
                            S.add("dve", (lambda o, d1: (lambda e: e.tensor_tensor_scan(out=o, data0=self.smask[:], data1=d1, initial=0.0,
                                                                                       op0=ALU.mult, op1=ALU.add)))(la[:, c2, tsl], sgm[:]),
                                  ["smask", "g_sgm"], ["g_la"])
                    self.act(Eg[:].rearrange("p a b -> p (a b)"), la[:].rearrange("p a b -> p (a b)"), AF.Exp, ["g_la"], ["g_Eg"], scale=1.0 / 16)
                    self.act(Eng[:].rearrange("p a b -> p (a b)"), la[:].rearrange("p a b -> p (a b)"), AF.Exp, ["g_la"], ["g_Eng"], scale=-1.0 / 16)
                    self.S.barrier(self.c_scr[:, 0:1])
                for c2 in range(2):
                    self.stt(qg[:, c2, :], qk[:, c2, :], 0.125, Eg[:, c2, :], ALU.mult, ALU.mult, ["g_qk", "g_Eg"], ["g_qg"])
                    self.stt(qr[:, c2, :], qk[:, c2, :], 0.125, Eng[:, c2, :], ALU.mult, ALU.mult, ["g_qk", "g_Eng"], ["g_qr"])
                    self.tt(kg[:, c2, :], qk[:, 2 + c2, :], Eng[:, c2, :], ALU.mult, ["g_qk", "g_Eng"], ["g_kg"])
                    self.tt(kr[:, c2, :], qk[:, 2 + c2, :], Eg[:, c2, :], ALU.mult, ["g_qk", "g_Eg"], ["g_kr"])
                    for ch in range(TS // 64):
                        cs_ = slice(ch * 64, ch * 64 + 64)
                        self.stt(kd[:, c2, cs_], qk[:, 2 + c2, cs_], Eg[:, c2, ch * 64 + 63:ch * 64 + 64], Eng[:, c2, cs_],
                                 ALU.mult, ALU.mult, ["g_qk", "g_Eg", "g_Eng"], ["g_kd"])
                self.S.barrier(self.c_scr[:, 0:1])
            if GS <= 4:
                return
            v_tok = self.sb(st, "g_vtok", [128, 512], F32)
            kd2 = self.sb(st, "g_kd2", [128, 2, 256], F32)
            kgp = self.sb(st, "g_kgp", [128, 2, 2, 128], BF16)
            krp = self.sb(st, "g_krp", [128, 2, 2, 128], BF16)
            qgp = self.sb(st, "g_qgp", [128, 2, 2, 128], F32)
            at = self.sb(st, "g_at", [128, 512], F32)
            at2 = self.sb(st, "g_at2", [128, 512], F32)
            atb = self.sb(st, "g_atb", [128, 512], F32)
            sq = self.sb(st, "g_sq", [128, 512], BF16)
            rs = self.sb(st, "g_rs", [128, 512], F32)
            ot = self.sb(st, "g_ot", [128, 512], F32)
            glS = self.glS
            for t_, k_ in ((kd2, "g_kd2"), (kgp, "g_kgp"), (krp, "g_krp"), (qgp, "g_qgp")):
                self.memset(t_[:].rearrange("p a b -> p (a b)") if len(t_.shape) == 3 else t_[:].rearrange("p a b c -> p (a b c)"), 0.0, [k_])
            mp = self.mpast[:].rearrange("p a b -> p (a b)")
            mf = self.mfut[:].rearrange("p a b -> p (a b)")
            gn = VOFF["gla_norm"]
            for b in range(NBLK):
                bsl = slice(b * 128, (b + 1) * 128)
                for c in range(4):
                    S.add("pe", (lambda o, i_: (lambda e: e.transpose(o, i_, self.ident[:])))(self.PB[0][:, c * 128:(c + 1) * 128], v[:, c, bsl]),
                          ["g_v", "ident"], ["pb0"])
                self.cp(v_tok[:], self.PB[0][:], ["pb0"], ["g_vtok"], eng="act_copy")
                for c2 in range(2):
                    S.add("pe", (lambda o, i_: (lambda e: e.transpose(o, i_, self.ident[:])))(self.PB[1][:, c2 * 128:(c2 + 1) * 128], kd[:, c2, bsl]),
                          ["g_kd", "ident"], ["pb1"])
                for cc in range(2):
                    crow = slice(cc * 64, cc * 64 + 64)
                    self.cp(kd2[crow, cc, :], self.PB[1][crow, 0:256], ["pb1"], ["g_kd2"], eng="act_copy")
                for hh in range(2):
                    r2 = slice(hh * 64, hh * 64 + 64)
                    self.cp(kgp[r2, hh, :, :], kg[r2, :, bsl], ["g_kg"], ["g_kgp"], eng="pool")
                    self.cp(krp[r2, hh, :, :], kr[r2, :, bsl], ["g_kr"], ["g_krp"], eng="pool")
                    self.cp(qgp[r2, hh, :, :], qg[r2, :, bsl], ["g_qg"], ["g_qgp"], eng="pool")
                if GS <= 5:
                    continue
                for h in range(4):
                    hs = slice(h * 128, (h + 1) * 128)
                    self.mm(self.PB[2][:, hs], kgp[:, h % 2, h // 2, :], qg[:, h // 2, bsl], True, True, ["g_kgp", "g_qg"], ["pb2"])
                    self.mm(self.PB[3][:, hs], krp[:, h % 2, h // 2, :], qr[:, h // 2, bsl], True, True, ["g_krp", "g_qr"], ["pb3"])
                self.tt(at[:], self.PB[2][:], mp, ALU.mult, ["pb2", "mpast"], ["g_at"])
                self.tt(at2[:], self.PB[3][:], mf, ALU.mult, ["pb3", "mfut"], ["g_at2"])
                self.tt(atb[:], at[:], at2[:], ALU.add, ["g_at", "g_at2"], ["g_atb"], eng="pool")
                if GS <= 6:
                    continue
                po = self.PB[4]
                for h in range(4):
                    hs = slice(h * 128, (h + 1) * 128)
                    self.mm(po[:, hs], v_tok[:, hs], atb[:, hs], h == 0, False, ["g_vtok", "g_atb"], ["pb4"], skip=True)
                for cc in range(2):
                    for h in range(4):
                        self.mm(po[:, h * 128 + cc * 64:h * 128 + cc * 64 + 64], glS[:, h // 2, :], qgp[:, h % 2, h // 2, cc * 64:cc * 64 + 64],
                                False, True, ["glS", "g_qgp"], ["pb4"], skip=True)
                    for hp in range(2):
                        self.mm(self.PB[5][:, hp * 256:(hp + 1) * 256], kd2[:, cc, hp * 128:(hp + 1) * 128],
                                v_tok[:, hp * 256:(hp + 1) * 256], True, True, ["g_kd2", "g_vtok"], ["pb5"])
                    for hp in range(2):
                        for hh in range(2):
                            r2 = slice(hh * 64, hh * 64 + 64)
                            self.stt(glS[r2, hp, :], glS[r2, hp, :], Eg[r2, hp, b * 128 + cc * 64 + 63:b * 128 + cc * 64 + 64],
                                     self.PB[5][r2, hp * 256 + hh * 128:hp * 256 + (hh + 1) * 128], ALU.mult, ALU.add,
                                     ["glS", "g_Eg", "pb5"], ["glS"])
                if GS <= 7:
                    continue
                self.act(sq[:], po[:], AF.Square, ["pb4"], ["g_sq"])
                self.mm(self.PB[6][:], self.on_128[:], sq[:], True, True, ["on_128", "g_sq"], ["pb6"])
                self.act(rs[:], self.PB[6][:], AF.Sqrt, ["pb6", "c_eps"], ["g_rs"], bias=self.c_eps[:, 0:1])
                self.recip(rs[:], rs[:], ["g_rs"], ["g_rs"])
                self.tt(ot[:], po[:], rs[:], ALU.mult, ["pb4", "g_rs"], ["g_ot"])
                for h in range(4):
                    self.stt(self.yT[:, 4 + h, bsl], ot[:, h * 128:(h + 1) * 128], self.vec[:, gn + h:gn + h + 1], sgo[:, h, bsl],
                             ALU.mult, ALU.mult, ["g_ot", "vec", "g_sgo"], ["yT"])

    def mix_rwkv(self, slot, sub):
        TS, NT = self.TS, self.NT
        S = self.S
        C0 = float(np.exp(-0.5))
        V = lambda n, c: self.vec[:, VOFF[n] + c:VOFF[n] + c + 1]
        with contextlib.ExitStack() as st:
            wr = [self.sb(st, "r_w%d" % i, [128, 8, 128], BF16) for i in range(14)]
            for i in range(14):
                self.dma(wr[i][:].rearrange("p a b -> p (a b)"), self.w_in[slot, i], [], ["r_w%d" % i], eng="pool")
            w2p = self.sb(st, "r_w2p", [128, 512], F32)
            a2p = self.sb(st, "r_a2p", [128, 512], F32)
            g2 = self.sb(st, "r_g2", [128, 512], F32)
            v1 = self.sb(st, "r_v1", [128, 4, 32], F32)
            v2 = self.sb(st, "r_v2", [32, 512], F32)
            omk = self.sb(st, "r_omk", [128, 4], F32)
            self.memset(w2p[:], 0.0, ["r_w2p"])
            self.memset(a2p[:], 0.0, ["r_a2p"])
            self.dma(w2p[0:64, :], self.w_w2[slot], [], ["r_w2p"])
            self.dma(a2p[64:128, :], self.w_a2[slot], [], ["r_a2p"])
            self.dma(g2[:], self.w_g2[slot], [], ["r_g2"])
            self.dma(v1[:].rearrange("p a b -> p (a b)"), self.w_v1[slot], [], ["r_v1"])
            self.dma(v2[:], self.w_v2[slot], [], ["r_v2"])
            ka = VOFF["rwkv_k_a"]
            self.ts(omk[:], self.vec[:, ka:ka + 4], -1.0, 1.0, ALU.mult, ALU.add, ["vec"], ["r_omk"])
            def nb(n, sh=(128, 512), dt=F32):
                if tuple(sh) == (128, 512) and dt == F32 and self.xslots:
                    return self.xslots.pop()
                return self.sb(st, n, list(sh), dt)
            ub = nb("r_ub", (128, 513))
            lor1, lor2 = nb("r_lor1"), nb("r_lor2")
            uvall = nb("r_uvall", (128, 4, 512))
            t1T = nb("r_t1T", (32, 512))
            ur, uk, a_, g_, sig = [nb("r_" + n) for n in ("ur", "uk", "a", "g", "sig")]
            Eg, Eng, kk, kh, KKt, Kt, Bt, Rt, kdec, bon, tmp, tmp2, vfb = [
                nb("r_" + n) for n in ("Eg", "Eng", "kk", "kh", "KKt", "Kt", "Bt", "Rt", "kdec", "bon", "tmp", "tmp2", "vfb")]
            cs = tmp2
            bdec = Eng
            Ysb = kdec
            egl = nb("r_egl", (128, 8))
            BS = []
            for q in range(2):
                d_ = dict(q=q)
                for n in ("Ktp", "Btp", "KKp", "kdec2", "bdec2"):
                    d_[n] = nb("r_%s_%d" % (n, q), (128, 2, 128), BF16)
                d_["IAT"] = nb("r_IAT_%d" % q, (128, 2, 128), BF16)
                d_["MW"] = nb("r_MW_%d" % q, (128, 2, 4, 128), BF16)
                d_["A"] = [nb("r_A%d_%d" % (i, q), (128, 2, 128), BF16) for i in range(2)]
                d_["AT"] = [nb("r_AT%d_%d" % (i, q), (128, 2, 128), BF16) for i in range(2)]
                d_["T"] = [nb("r_T%d_%d" % (i, q), (128, 2, 128), BF16) for i in range(2)]
                d_["v_tok"] = nb("r_vtok_%d" % q, (128, 128), BF16)
                d_["vpad"] = nb("r_vpad_%d" % q, (128, 384), BF16)
                BS.append(d_)
            SApad = nb("r_SApad", (128, 384), BF16)
            negX, SA = nb("r_negX", (128, 128), BF16), nb("r_SA", (128, 128), BF16)
            KKb, Rb, Bb = [nb("r_" + n, (128, 512), BF16) for n in ("KKb", "Rb", "Bb")]
            mask4 = nb("r_mask4", (128, 4, 128))
            id2 = nb("r_id2", (128, 2, 128))
            for d_ in BS:
                for n in ("Ktp", "Btp", "KKp", "kdec2", "bdec2"):
                    self.memset(d_[n][:].rearrange("p a b -> p (a b)"), 0.0, ["r_%s_%d" % (n, d_["q"])])
                self.memset(d_["vpad"][:], 0.0, ["r_vpad_%d" % d_["q"]])
            for t_, k_ in ((SApad, "r_SApad"), (negX, "r_negX"), (SA, "r_SA")):
                self.memset(t_[:], 0.0, [k_])
            for q_, m_ in ((0, self.mstr), (1, self.mpast), (2, self.mstr), (3, self.mpast)):
                self.cp(mask4[:, q_, :], m_[:, 0, :], ["mstr", "mpast"], ["r_mask4"], eng="pool")
            for hh in range(2):
                self.cp(id2[:, hh, :], self.ident[:], ["ident"], ["r_id2"], eng="pool")
            rwS = self.rwS
            pbi = [0]

            def bank():
                k = pbi[0] % 4
                pbi[0] += 1
                return self.PB[k], "pb%d" % k

            def shift_proj(ci, tsl, out, okey):
                ps, pk = bank()
                self.proj(ps[:], pk, wr[ci], "r_w%d" % ci, tsl)
                self.cp(ub[:, 1:513], ps[:], [pk], ["r_ub"], eng="act_copy")
                self.cp(ub[:, 0:1], self.uprev[:, ci:ci + 1], ["uprev"], ["r_ub"], eng="pool")
                self.tt(out, ub[:, 0:512], ub[:, 1:513], ALU.subtract, ["r_ub"], [okey])
                self.stt(out, out, V("rwkv_mu", ci), ub[:, 1:513], ALU.mult, ALU.add, [okey, "vec", "r_ub"], [okey])
                self.cp(self.uprev[:, ci:ci + 1], ub[:, 512:513], ["r_ub"], ["uprev"], eng="pool")

            vf_src = self.vf_in if slot == 0 else self.vf_out
            fl0 = self.flg[:, 2 * slot:2 * slot + 1]
            import os
            RS = float(os.environ.get("R_STOP", "99"))
            for nt in range(NT):
                tsl = slice(nt * 512, (nt + 1) * 512)
                shift_proj(12, tsl, tmp[:], "r_tmp")
                self.act(lor1[0:64, :], tmp[0:64, :], AF.Tanh, ["r_tmp"], ["r_lor1"])
                self.cp(lor1[64:128, :], tmp[64:128, :], ["r_tmp"], ["r_lor1"])
                shift_proj(13, tsl, tmp[:], "r_tmp")
                self.act(lor2[:], tmp[:], AF.Sigmoid, ["r_tmp"], ["r_lor2"])
                for hp in range(4):
                    shift_proj(8 + hp, tsl, uvall[:, hp, :], "r_uvall")
                ps, pk = bank()
                for c in range(4):
                    self.mm(ps[0:32, :], v1[:, c, :], uvall[:, c, :], c == 0, c == 3, ["r_v1", "r_uvall"], [pk])
                self.cp(t1T[:], ps[0:32, :], [pk], ["r_t1T"])
                if RS <= 1:
                    continue
                for hp in range(4):
                    hs = slice(hp * 128, (hp + 1) * 128)
                    uv = uvall[:, hp, :]
                    shift_proj(hp, tsl, ur[:], "r_ur")
                    shift_proj(4 + hp, tsl, uk[:], "r_uk")
                    ps, pk = bank()
                    self.mm(ps[:], v2[0:32, hs], t1T[0:32, :], True, True, ["r_v2", "r_t1T"], [pk])
                    self.act(tmp[:], ps[:], AF.Sigmoid, [pk, "vec"], ["r_tmp"], bias=V("rwkv_v0", hp))
                    self.dma(vfb[:], vf_src[sub, :, hp, tsl], ["vfdram"], ["r_vfb"])
                    self.tt(tmp2[:], vfb[:], uv, ALU.subtract, ["r_vfb", "r_uvall"], ["r_tmp2"])
                    self.tt(tmp2[:], tmp2[:], tmp[:], ALU.mult, ["r_tmp2", "r_tmp"], ["r_tmp2"])
                    self.tt(uv, uv, tmp2[:], ALU.add, ["r_uvall", "r_tmp2"], ["r_uvall"])
                    self.tt(tmp2[:], uv, vfb[:], ALU.subtract, ["r_vfb", "r_uvall"], ["r_tmp2"])
                    self.stt(vfb[:], tmp2[:], fl0, vfb[:], ALU.mult, ALU.add, ["r_tmp2", "flg", "r_vfb"], ["r_vfb"])
                    self.dma(self.vf_out[sub, :, hp, tsl], vfb[:], ["r_vfb"], ["vfdram"])
                    if RS <= 2:
                        continue
                    ps, pk = bank()
                    self.mm(ps[:], w2p[:, hs], lor1[:], True, True, ["r_w2p", "r_lor1"], [pk])
                    self.act(sig[:], ps[:], AF.Sigmoid, [pk, "vec"], ["r_sig"], bias=V("rwkv_w0", hp))
                    ps, pk = bank()
                    self.mm(ps[:], a2p[:, hs], lor1[:], True, True, ["r_a2p", "r_lor1"], [pk])
                    self.act(a_[:], ps[:], AF.Sigmoid, [pk, "vec"], ["r_a"], bias=V("rwkv_a0", hp))
                    ps, pk = bank()
                    self.mm(ps[:], g2[:, hs], lor2[:], True, True, ["r_g2", "r_lor2"], [pk])
                    self.cp(g_[:], ps[:], [pk], ["r_g"], eng="act_copy")
                    S.add("dve", (lambda: (lambda e: e.tensor_tensor_scan(out=cs[:], data0=self.smask[:], data1=sig[:], initial=0.0,
                                                                         op0=ALU.mult, op1=ALU.add)))(), ["smask", "r_sig"], ["r_tmp2"])
                    self.tt(KKt[:], cs[:], sig[:], ALU.subtract, ["r_tmp2", "r_sig"], ["r_KKt"], eng="pool")
                    self.act(Eg[:], cs[:], AF.Exp, ["r_tmp2"], ["r_Eg"], scale=-C0)
                    self.act(Eng[:], cs[:], AF.Exp, ["r_tmp2"], ["r_Eng"], scale=C0)
                    self.act(KKt[:], KKt[:], AF.Exp, ["r_KKt"], ["r_KKt"], scale=-C0)
                    self.cp(egl[:], Eg[:].rearrange("p (c s) -> p c s", s=64)[:, :, 63], ["r_Eg"], ["r_egl"], eng="pool")
                    self.ts(kk[:], uk[:], V("rwkv_k_k", hp), None, ALU.mult, None, ["r_uk", "vec"], ["r_kk"])
                    self.act(tmp[:], kk[:], AF.Square, ["r_kk"], ["r_tmp"])
                    ps, pk = bank()
                    self.mm(ps[:], self.blk[:], tmp[:], True, True, ["blk", "r_tmp"], [pk])
                    self.act(tmp[:], ps[:], AF.Sqrt, [pk], ["r_tmp"])
                    self.ts(tmp[:], tmp[:], 1e-12, None, ALU.max, None, ["r_tmp"], ["r_tmp"])
                    self.recip(tmp[:], tmp[:], ["r_tmp"], ["r_tmp"])
                    self.tt(kk[:], kk[:], tmp[:], ALU.mult, ["r_kk", "r_tmp"], ["r_kk"])
                    self.ts(tmp[:], a_[:], V("rwkv_k_a", hp), omk[:, hp:hp + 1], ALU.mult, ALU.add, ["r_a", "vec", "r_omk"], ["r_tmp"])
                    self.tt(kh[:], uk[:], tmp[:], ALU.mult, ["r_uk", "r_tmp"], ["r_kh"])
                    self.tt(tmp[:], ur[:], kh[:], ALU.mult, ["r_ur", "r_kh"], ["r_tmp"])
                    self.ts(tmp[:], tmp[:], V("rwkv_r_k", hp), None, ALU.mult, None, ["r_tmp", "vec"], ["r_tmp"])
                    ps, pk = bank()
                    self.mm(ps[:], self.blk[:], tmp[:], True, True, ["blk", "r_tmp"], [pk])
                    self.tt(bon[:], ps[:], uv, ALU.mult, [pk, "r_uvall"], ["r_bon"])
                    self.tt(KKt[:], KKt[:], kk[:], ALU.mult, ["r_KKt", "r_kk"], ["r_KKt"])
                    self.tt(Kt[:], kh[:], Eng[:], ALU.mult, ["r_kh", "r_Eng"], ["r_Kt"])
                    self.tt(tmp[:], kk[:], a_[:], ALU.mult, ["r_kk", "r_a"], ["r_tmp"], eng="pool")
                    self.tt(Bt[:], tmp[:], Eng[:], ALU.mult, ["r_tmp", "r_Eng"], ["r_Bt"])
                    self.tt(Rt[:], ur[:], Eg[:], ALU.mult, ["r_ur", "r_Eg"], ["r_Rt"])
                    self.cp(KKb[:], KKt[:], ["r_KKt"], ["r_KKb"], eng="act_copy")
                    self.cp(Rb[:], Rt[:], ["r_Rt"], ["r_Rb"], eng="pool")
                    self.cp(Bb[:], Bt[:], ["r_Bt"], ["r_Bb"], eng="act_copy")
                    for ch in range(8):
                        cs_ = slice(ch * 64, ch * 64 + 64)
                        self.ts(kdec[:, cs_], Kt[:, cs_], egl[:, ch:ch + 1], None, ALU.mult, None, ["r_Kt", "r_egl"], ["r_kdec"])
                        self.ts(bdec[:, cs_], Bt[:, cs_], egl[:, ch:ch + 1], None, ALU.mult, None, ["r_Bt", "r_egl"], ["r_Eng"], eng="pool")
                    if RS <= 3:
                        continue
                    PY, pyk = self.PB[7], "pb7"

                    def gen_pre(b, B_):
                        bl = slice(b * 128, (b + 1) * 128)
                        q = B_["q"]
                        K_ = lambda n: "%s_%d" % (n, q)
                        PT, ptk = self.PB[4], "pb4"
                        for q_, src, sk in ((0, uv, "r_uvall"), (1, kdec[:], "r_kdec"), (2, bdec[:], "r_Eng")):
                            S.add("pe", (lambda o, i_: (lambda e: e.transpose(o, i_, self.ident[:])))(PT[:, q_ * 128:(q_ + 1) * 128], src[:, bl]),
                                  [sk, "ident"], [ptk])
                        self.cp(B_["v_tok"][:], PT[:, 0:128], [ptk], [K_("r_vtok")], eng="act_copy")
                        self.cp(B_["vpad"][:].rearrange("p (a b) -> p a b", b=192)[:, :, 0:64], PT[:, 0:128].rearrange("p (a b) -> p a b", b=64),
                                [ptk], [K_("r_vpad")])
                        for cc in range(2):
                            crow = slice(cc * 64, cc * 64 + 64)
                            self.cp(B_["kdec2"][crow, cc, :], PT[crow, 128:256], [ptk], [K_("r_kdec2")], eng="act_copy")
                            self.cp(B_["bdec2"][crow, cc, :], PT[crow, 256:384], [ptk], [K_("r_bdec2")])
                        for hh in range(2):
                            r2 = slice(hh * 64, hh * 64 + 64)
                            self.cp(B_["Ktp"][r2, hh, :], Kt[r2, bl], ["r_Kt"], [K_("r_Ktp")], eng="pool")
                            self.cp(B_["Btp"][r2, hh, :], Bt[r2, bl], ["r_Bt"], [K_("r_Btp")], eng="pool")
                            self.cp(B_["KKp"][r2, hh, :], KKt[r2, bl], ["r_KKt"], [K_("r_KKp")], eng="pool")
                        yield
                        MW = B_["MW"]
                        for hh in range(2):
                            PM, pmk = self.PB[5 + hh], "pb%d" % (5 + hh)
                            self.mm(PM[:, 0:128], B_["Ktp"][:, hh, :], KKb[:, bl], True, True, [K_("r_Ktp"), "r_KKb"], [pmk])
                            self.mm(PM[:, 128:256], B_["Ktp"][:, hh, :], Rb[:, bl], True, True, [K_("r_Ktp"), "r_Rb"], [pmk])
                            self.mm(PM[:, 256:384], B_["Btp"][:, hh, :], KKb[:, bl], True, True, [K_("r_Btp"), "r_KKb"], [pmk])
                            self.mm(PM[:, 384:512], B_["Btp"][:, hh, :], Rb[:, bl], True, True, [K_("r_Btp"), "r_Rb"], [pmk])
                            self.tt(MW[:, hh, :, :].rearrange("p a b -> p (a b)"), PM[:], mask4[:].rearrange("p a b -> p (a b)"), ALU.mult,
                                    [pmk, "r_mask4"], [K_("r_MW")])
                        PA, pak = bank()
                        for hh in range(2):
                            self.mm(PA[:, hh * 128:(hh + 1) * 128], B_["KKp"][:, hh, :], Bb[:, bl], True, True, [K_("r_KKp"), "r_Bb"], [pak])
                        Ab_, ATb_, Tb_, IAT_ = B_["A"], B_["AT"], B_["T"], B_["IAT"]
                        self.tt(ATb_[0][:].rearrange("p a b -> p (a b)"), PA[:, 0:256], self.mfut[:, 0:2, :].rearrange("p a b -> p (a b)"), ALU.mult,
                                [pak, "mfut"], [K_("r_AT0")])
                        self.tt(Tb_[0][:], id2[:], MW[:, :, 2, :], ALU.subtract, ["r_id2", K_("r_MW")], [K_("r_T0")])
                        self.cp(Ab_[0][:], MW[:, :, 2, :], [K_("r_MW")], [K_("r_A0")], eng="pool")
                        yield

                        def sq(cur, need_a):
                            A, AT = Ab_[cur], ATb_[cur]
                            ak, atk = K_("r_A%d" % cur), K_("r_AT%d" % cur)
                            P1, p1k = bank()
                            P2, p2k = bank()
                            for hh in range(2):
                                hs2 = slice(hh * 128, (hh + 1) * 128)
                                if need_a:
                                    self.mm(P1[:, hs2], AT[:, hh, :], A[:, hh, :], True, True, [ak, atk], [p1k])
                                self.mm(P2[:, hs2], A[:, hh, :], AT[:, hh, :], True, True, [ak, atk], [p2k])
                            return P1, p1k, P2, p2k
                        cur = 0
                        pend = sq(0, True)
                        yield
                        for it in range(5):
                            nx = 1 - cur
                            P1, p1k, P2, p2k = pend
                            if it < 4:
                                self.cp(Ab_[nx][:].rearrange("p a b -> p (a b)"), P1[:, 0:256], [p1k], [K_("r_A%d" % nx)], eng="act_copy")
                                self.cp(ATb_[nx][:].rearrange("p a b -> p (a b)"), P2[:, 0:256], [p2k], [K_("r_AT%d" % nx)], eng="act_copy")
                            self.tt(IAT_[:].rearrange("p a b -> p (a b)"), P2[:, 0:256], id2[:].rearrange("p a b -> p (a b)"), ALU.add,
                                    [p2k, "r_id2"], [K_("r_IAT")])
                            if it < 4:
                                pend = sq(nx, it < 3)
                            P3, p3k = bank()
                            for hh in range(2):
                                self.mm(P3[:, hh * 128:(hh + 1) * 128], IAT_[:, hh, :], Tb_[cur][:, hh, :], True, True,
                                        [K_("r_IAT"), K_("r_T%d" % cur)], [p3k])
                            self.cp(Tb_[nx][:].rearrange("p a b -> p (a b)"), P3[:, 0:256], [p3k], [K_("r_T%d" % nx)])
                            cur = nx
                            yield
                        B_["Tcur"] = cur

                    def gen_chain(b, B_):
                        q = B_["q"]
                        K_ = lambda n: "%s_%d" % (n, q)
                        bl = slice(b * 128, (b + 1) * 128)
                        MW = B_["MW"]
                        T, tk = B_["T"][B_["Tcur"]], K_("r_T%d" % B_["Tcur"])
                        v_tok = B_["v_tok"]
                        for cc in range(2):
                            crow = slice(cc * 64, cc * 64 + 64)
                            cl = slice(b * 128 + cc * 64, b * 128 + cc * 64 + 64)
                            ccol = slice(cc * 64, cc * 64 + 64)
                            PX, pxk = bank()
                            self.mm(PX[:, 0:128], KKt[:, bl], rwS[:, hp, :], True, False, ["r_KKt", "rwS"], [pxk], skip=True)
                            for hh in range(2):
                                self.mm(PX[:, hh * 64:(hh + 1) * 64], MW[:, hh, 0, :], v_tok[:, hh * 64:(hh + 1) * 64], False, True,
                                        [K_("r_MW"), K_("r_vtok")], [pxk], skip=True)
                            self.ts(negX[crow, :], PX[crow, 0:128], -1.0, None, ALU.mult, None, [pxk], ["r_negX"])
                            yield
                            PS_, psk = bank()
                            for hh in range(2):
                                self.mm(PS_[:, hh * 64:(hh + 1) * 64], T[:, hh, :], negX[:, hh * 64:(hh + 1) * 64], True, True,
                                        [tk, "r_negX"], [psk])
                            self.cp(SA[crow, :], PS_[crow, 0:128], [psk], ["r_SA"], eng="act_copy")
                            self.cp(SApad[crow, :].rearrange("p (a b) -> p a b", b=192)[:, :, 0:64],
                                    PS_[crow, 0:128].rearrange("p (a b) -> p a b", b=64), [psk], ["r_SApad"])
                            yield
                            PU, puk = bank()
                            self.mm(PU[:, 0:128], B_["bdec2"][:, cc, :], SA[:], True, False, [K_("r_bdec2"), "r_SA"], [puk])
                            self.mm(PU[:, 0:128], B_["kdec2"][:, cc, :], v_tok[:], False, True, [K_("r_kdec2"), K_("r_vtok")], [puk])
                            self.mm(PY[:, cl], rwS[:, hp, :], Rt[:, cl], True, False, ["rwS", "r_Rt"], [pyk], skip=True)
                            for hh in range(2):
                                self.mm(PY[:, cl], SApad[:, hh * 128:(hh + 1) * 128], MW[:, hh, 3, ccol], False, False,
                                        ["r_SApad", K_("r_MW")], [pyk], skip=True)
                            for hh in range(2):
                                self.mm(PY[:, cl], B_["vpad"][:, hh * 128:(hh + 1) * 128], MW[:, hh, 1, ccol], False, hh == 1,
                                        [K_("r_vpad"), K_("r_MW")], [pyk], skip=True)
                            for hh in range(2):
                                r2 = slice(hh * 64, hh * 64 + 64)
                                self.stt(rwS[r2, hp, hh * 64:(hh + 1) * 64], rwS[r2, hp, hh * 64:(hh + 1) * 64], egl[r2, b * 2 + cc:b * 2 + cc + 1],
                                         PU[r2, hh * 64:(hh + 1) * 64], ALU.mult, ALU.add, ["rwS", "r_egl", puk], ["rwS"])
                            yield

                    def drive(gens):
                        gens = list(gens)
                        while gens:
                            for g in list(gens):
                                try:
                                    next(g)
                                except StopIteration:
                                    gens.remove(g)

                    drive([gen_pre(0, BS[0])])
                    for b in range(4):
                        gl_ = [gen_chain(b, BS[b % 2])]
                        if b + 1 < 4:
                            gl_.append(gen_pre(b + 1, BS[(b + 1) % 2]))
                        drive(gl_)
                    if RS <= 6:
                        continue
                    self.cp(Ysb[:], PY[:], [pyk], ["r_kdec"], eng="act_copy")
                    ps, pk = bank()
                    self.mm(ps[:], self.blk64[:], Ysb[:], True, True, ["blk64", "r_kdec"], [pk])
                    self.tt(Ysb[:], Ysb[:], ps[:], ALU.subtract, ["r_kdec", pk], ["r_kdec"])
                    self.act(tmp[:], Ysb[:], AF.Square, ["r_kdec"], ["r_tmp"])
                    ps, pk = bank()
                    self.mm(ps[:], self.blk64[:], tmp[:], True, True, ["blk64", "r_tmp"], [pk])
                    self.act(tmp[:], ps[:], AF.Sqrt, [pk, "c_eps"], ["r_tmp"], bias=self.c_eps[:, 1:2])
                    self.recip(tmp[:], tmp[:], ["r_tmp"], ["r_tmp"])
                    self.tt(Ysb[:], Ysb[:], tmp[:], ALU.mult, ["r_kdec", "r_tmp"], ["r_kdec"])
                    self.ts(Ysb[:], Ysb[:], V("rwkv_ln_w", hp), V("rwkv_ln_b", hp), ALU.mult, ALU.add, ["r_kdec", "vec"], ["r_kdec"])
                    self.tt(Ysb[:], Ysb[:], bon[:], ALU.add, ["r_kdec", "r_bon"], ["r_kdec"])
                    self.tt(self.yT[:, hp, tsl], Ysb[:], g_[:], ALU.mult, ["r_kdec", "r_g"], ["yT"])

    def mix_merge(self, slot):
        TS, NT = self.TS, self.NT
        with contextlib.ExitStack() as st:
            mg = self.sb(st, "mg", [128, 8, TS], BF16)
            wbr = [self.sb(st, "mg_wbr%d" % i, [128, 3, 4, 128], BF16) for i in range(2)]
            wo = [self.sb(st, "mg_wo%d" % i, [128, 8, 128], BF16) for i in range(2)]
            gsb = self.sb(st, "mg_g", [128, 512], F32)
            acc = self.sb(st, "mg_acc", [128, 512], F32)
            tmp = self.sb(st, "mg_tmp", [128, 512], F32)
            it = 0
            for dc in range(8):
                wb_, wbk = wbr[dc % 2], "mg_wbr%d" % (dc % 2)
                self.dma(wb_[:].rearrange("p a b c -> p (a b c)"), self.w_br[slot, dc], [], [wbk], eng="pool")
                wg = []
                for j in range(3):
                    wg.append(self.win_tile(slot, 30 + j * 8 + dc, self.wib, self.wcnt))
                for nt in range(NT):
                    tsl = slice(nt * 512, (nt + 1) * 512)
                    for j in range(3):
                        k = it % 2
                        it += 1
                        PG, pgk = self.PB[k], "pb%d" % k
                        PP, ppk = self.PB[2 + k], "pb%d" % (2 + k)
                        self.proj(PG[:], pgk, wg[j][0], wg[j][1], tsl)
                        for c in range(4):
                            self.mm(PP[:], wb_[:, j, c, :], self.yT[:, j * 4 + c, tsl], c == 0, c == 3, [wbk, "yT"], [ppk])
                        self.act(gsb[:], PG[:], AF.Sigmoid, [pgk], ["mg_g"])
                        if j == 0:
                            self.tt(acc[:], PP[:], gsb[:], ALU.mult, [ppk, "mg_g"], ["mg_acc"])
                        elif j == 1:
                            self.tt(tmp[:], PP[:], gsb[:], ALU.mult, [ppk, "mg_g"], ["mg_tmp"])
                            self.tt(acc[:], acc[:], tmp[:], ALU.add, ["mg_acc", "mg_tmp"], ["mg_acc"], eng="pool")
                        else:
                            self.tt(tmp[:], PP[:], gsb[:], ALU.mult, [ppk, "mg_g"], ["mg_tmp"])
                            self.tt(mg[:, dc, tsl], acc[:], tmp[:], ALU.add, ["mg_acc", "mg_tmp"], ["mg"], eng="pool")
            for dc2 in range(8):
                w, wk = wo[dc2 % 2], "mg_wo%d" % (dc2 % 2)
                self.dma(w[:].rearrange("p a b -> p (a b)"), self.w_o[slot, dc2], [], [wk], eng="pool")
                for nt in range(NT):
                    tsl = slice(nt * 512, (nt + 1) * 512)
                    k = 4 + it % 2
                    it += 1
                    PO, pok = self.PB[k], "pb%d" % k
                    for dc in range(8):
                        self.mm(PO[:], w[:, dc, :], mg[:, dc, tsl], dc == 0, dc == 7, [wk, "mg"], [pok])
                    self.tt(self.xT[:, dc2, tsl], self.xT[:, dc2, tsl], PO[:], ALU.add, ["xT", pok], ["xT"])

def _fm(v, ncol):
    return np.ascontiguousarray(np.asarray(v, np.float32).reshape(ncol, 128).T)


def _wtile(w, cols):
    K, C = w.shape
    nk = K // 128
    nt = C // cols
    a = np.asarray(w, np.float32).reshape(nk, 128, nt, cols).transpose(2, 1, 0, 3)
    return np.ascontiguousarray(a).reshape(nt, 128, nk * cols)


def prep_layer(inp, l, final_norm):
    f32 = np.float32
    if l is None:
        z = lambda *s: np.zeros(s, f32)
        vec = z(128, NV)
        vec[:, VOFF["rwkv_v0"]:VOFF["rwkv_v0"] + 4] = -1e4
        vec[:, VOFF["final_norm"]:VOFF["final_norm"] + 8] = _fm(final_norm, 8)
        return dict(vecs=vec, w_f1i=z(NFF, 128, 2048), w_f1o=z(8, 128, NFF * 128), w_f2i=z(NFF, 128, 2048),
                    w_f2o=z(8, 128, NFF * 128), w_in=z(54, 128, 1024), w_al=z(128, 128), w_kv=z(128, 8192),
                    w_br=z(8, 128, 1536), w_o=z(8, 128, 1024), w_w2=z(64, 512), w_a2=z(64, 512), w_g2=z(128, 512),
                    w_v1=z(128, 128), w_v2=z(32, 512), w_aup=z(16, 256))
    vec = np.zeros((128, NV), f32)

    def put(name, v, n):
        vec[:, VOFF[name]:VOFF[name] + n] = _fm(v, n)
    put("ffn1_norm", inp["ffn1_norm"][l], 8)
    put("mix_norm", inp["mix_norm"][l], 8)
    put("mem_norm", inp["mem_norm"][l], 8)
    put("ffn2_norm", inp["ffn2_norm"][l], 8)
    put("final_norm", final_norm, 8)
    put("rwkv_mu", inp["rwkv_mu"][l], 14)
    for n in ("rwkv_w0", "rwkv_a0", "rwkv_k_k", "rwkv_k_a", "rwkv_ln_w", "rwkv_ln_b"):
        put(n, inp[n][l], 4)
    put("rwkv_r_k", np.asarray(inp["rwkv_r_k"][l]).reshape(-1), 4)
    if l == 0:
        vec[:, VOFF["rwkv_v0"]:VOFF["rwkv_v0"] + 4] = -1e4
        v1 = np.zeros((512, 32), f32)
        v2 = np.zeros((32, 512), f32)
    else:
        put("rwkv_v0", inp["rwkv_v0"][l - 1], 4)
        v1 = np.asarray(inp["rwkv_v1"][l - 1], f32)
        v2 = np.asarray(inp["rwkv_v2"][l - 1], f32)
    for j in range(4):
        put("conv%d" % j, inp["gla_conv"][l][j], 8)
    put("gla_a_bias", inp["gla_a_bias"][l], 2)
    put("gla_norm", inp["gla_norm"][l], 4)

    def ffn_in(w):
        w = np.asarray(w, f32)
        g = _wtile(w[:, :DFF], 128).reshape(NFF, 128, 8, 128)
        u = _wtile(w[:, DFF:], 128).reshape(NFF, 128, 8, 128)
        return np.ascontiguousarray(np.concatenate([g, u], axis=3)).reshape(NFF, 128, 2048)

    def ffn_out(w):
        return _wtile(np.asarray(w, f32), 128)

    win = np.asarray(inp["w_in"][l], f32)
    full = np.concatenate([win[:, 0:2816], win[:, 2832:]], axis=1)
    wbr = np.asarray(inp["w_branch"][l], f32)
    br = np.stack([_wtile(wbr[j], 128).reshape(8, 128, 4 * 128) for j in range(3)], axis=2)
    return dict(
        vecs=vec,
        w_f1i=ffn_in(inp["ffn1_w_in"][l]), w_f1o=ffn_out(inp["ffn1_w_out"][l]),
        w_f2i=ffn_in(inp["ffn2_w_in"][l]), w_f2o=ffn_out(inp["ffn2_w_out"][l]),
        w_in=_wtile(full, 128), w_al=_wtile(win[:, 2816:2832], 16)[0],
        w_kv=_wtile(np.asarray(inp["xa_w_kv"][l], f32), 1024)[0],
        w_br=np.ascontiguousarray(br).reshape(8, 128, 1536),
        w_o=_wtile(np.asarray(inp["w_out"][l], f32), 128),
        w_w2=np.asarray(inp["rwkv_w2"][l], f32), w_a2=np.asarray(inp["rwkv_a2"][l], f32),
        w_g2=np.asarray(inp["rwkv_g2"][l], f32), w_v1=_wtile(v1, 32)[0], w_v2=v2,
        w_aup=np.asarray(inp["gla_a_up"][l], f32))


def x_to_fm(x, n_sub, TS):
    a = np.asarray(x, np.float32).reshape(n_sub, TS, 8, 128).transpose(0, 3, 2, 1)
    return np.ascontiguousarray(a)


def fm_to_x(a):
    n_sub, _, _, TS = a.shape
    return np.ascontiguousarray(a.transpose(0, 3, 2, 1)).reshape(n_sub * TS, 1024)


_PROG = {}


def _get_prog(key, **kw):
    if key not in _PROG:
        b = B(**kw)
        b.build()
        _PROG[key] = b
    return _PROG[key]


def _in_map(L, x_fm, vf, st_in, fl, memT):
    m = {k: v[None] for k, v in L.items()}
    m.update(x_in=x_fm, vf_in=vf, st_in=st_in, flags=fl, memT=memT)
    return m


def kernel(**inp):
    inp = {k: np.asarray(v) for k, v in inp.items()}
    x, mem = inp["x"], inp["mem"]
    Bn, SEQ, _ = x.shape
    NSUB, TS, NSL = 2, 1024, 5
    HALF = NSUB * TS
    assert SEQ == 2 * HALF and Bn == 4
    return _run_fused(inp, x, mem, Bn, NSUB, TS, NSL)


def _run_fused(inp, x, mem, Bn, NSUB, TS, NSL, nlayers=4):
    HALF = NSUB * TS
    fn = inp["final_norm"]
    layers = [prep_layer(inp, l, fn) for l in range(nlayers)]
    dummy = prep_layer(inp, None, fn)
    ncores = 2 * Bn
    prog = _get_prog(("fused", ncores, NSUB, TS, NSL), n_slots=NSL, n_sub=NSUB, TS=TS, norm_after=[NSL - 2, NSL - 1], n_cores=ncores)
    maps = []
    for c in range(ncores):
        b, half = c // 2, c % 2
        ls = [(k if half == 0 else k - 1) for k in range(NSL)]
        Ls = [layers[l] if 0 <= l < nlayers else dummy for l in ls]
        m = {key: np.stack([L[key] for L in Ls]) for key in dummy}
        fl = np.zeros((128, 2 * NSL), np.float32)
        for k, l in enumerate(ls):
            fl[:, 2 * k] = 1.0 if l == 0 else 0.0
            fl[:, 2 * k + 1] = 1.0 if half == 1 else 0.0
        m.update(x_in=x_to_fm(x[b, half * HALF:(half + 1) * HALF], NSUB, TS),
                 vf_in=np.zeros((NSUB, 128, 4, TS), np.float32), st_in=np.zeros((128, NST), np.float32), flags=fl,
                 memT=np.ascontiguousarray(mem[b].reshape(NMEM, 8, 128).transpose(2, 1, 0)).astype(np.float32))
        maps.append(m)
    res = run_bass_kernel_spmd(prog.nc, maps, core_ids=list(range(ncores))).results
    out = np.zeros((Bn, 2 * HALF, D), np.float32)
    for c in range(ncores):
        b, half = c // 2, c % 2
        y = np.asarray(res[c]["y_out"])[half]
        out[b, half * HALF:(half + 1) * HALF] = fm_to_x(y)
    return out


def kernel_unfused(**inp):
    inp = {k: np.asarray(v) for k, v in inp.items()}
    x, mem = inp["x"], inp["mem"]
    Bn, SEQ, _ = x.shape
    NSUB, TS = 2, 1024
    HALF = NSUB * TS
    assert SEQ == 2 * HALF and Bn == 4
    fn = inp["final_norm"]
    layers = [prep_layer(inp, l, fn) for l in range(4)]
    dummy = prep_layer(inp, None, fn)
    prog = _get_prog("slot1", n_slots=1, n_sub=NSUB, TS=TS, norm_after=[0])
    ncores = 8
    memT = [np.ascontiguousarray(mem[b].reshape(NMEM, 8, 128).transpose(2, 1, 0)).astype(np.float32) for b in range(Bn)]
    xs = [x_to_fm(x[c // 2, (c % 2) * HALF:(c % 2 + 1) * HALF], NSUB, TS) for c in range(ncores)]
    vfs = [np.zeros((NSUB, 128, 4, TS), np.float32) for _ in range(ncores)]
    st_prev = [np.zeros((128, NST), np.float32) for _ in range(Bn)]
    ys = [None] * ncores
    for k in range(5):
        maps = []
        for c in range(ncores):
            b, half = c // 2, c % 2
            l = k if half == 0 else k - 1
            L = layers[l] if 0 <= l <= 3 else dummy
            fl = np.zeros((128, 2), np.float32)
            fl[:, 0] = 1.0 if l == 0 else 0.0
            fl[:, 1] = 1.0 if half == 1 else 0.0
            maps.append(_in_map(L, xs[c], vfs[c], st_prev[b], fl, memT[b]))
        res = run_bass_kernel_spmd(prog.nc, maps, core_ids=list(range(ncores))).results
        for c in range(ncores):
            b, half = c // 2, c % 2
            l = k if half == 0 else k - 1
            xs[c] = np.asarray(res[c]["x_out"])
            vfs[c] = np.asarray(res[c]["vf_out"])
            if l == 3:
                ys[c] = np.asarray(res[c]["y_out"])[0]
        for c in range(0, ncores, 2):
            st_prev[c // 2] = np.asarray(res[c]["st_out"])
    out = np.zeros((Bn, SEQ, D), np.float32)
    for c in range(ncores):
        out[c // 2, (c % 2) * HALF:(c % 2 + 1) * HALF] = fm_to_x(ys[c])
    return out
```

```python
import contextlib
import numpy as np
import concourse.bass as bass
import concourse.mybir as mybir
from concourse.bass_utils import run_bass_kernel_spmd

F32 = mybir.dt.float32
BF16 = mybir.dt.bfloat16
AF = mybir.ActivationFunctionType
ALU = mybir.AluOpType

D = 1024
DFF = 2816
NFF = DFF // 128
NMEM = 256
EPS = 1e-6
LN_EPS = 64e-5
WCOLS = 6928

ENGS = ("pe", "act", "dve", "pool", "sp")
EPOCH = 8000
NDMASEM = 44
DMASEM_RANGE = {"sp": (0, 16), "pool": (16, 16), "act": (32, 8), "cc": (40, 4)}


class Sched:
    def __init__(self, nc):
        self.nc = nc
        self.ops = {e: [] for e in ENGS}
        self.last_write = {}
        self.rd_e = {}
        self.rd_d = {}
        self.seen = {e: {} for e in ENGS}
        self.dma_seen = {e: {} for e in ENGS}
        self.dma_rr_e = {}
        self.dma_val = [0] * NDMASEM
        self.dma_last = [None] * NDMASEM
        self.phase_tok = None

    def add(self, eng, fn, reads=(), writes=(), dma=False, cc=False):
        ops = self.ops[eng]
        idx = len(ops)
        deps = set()
        pbr = [k for k in reads if k.startswith("pb") and k not in writes]
        if pbr:
            writes = list(writes) + pbr
        if self.phase_tok is not None:
            deps.add(self.phase_tok)
        for b in reads:
            lw = self.last_write.get(b)
            if lw is not None:
                deps.add(lw)
        for b in writes:
            lw = self.last_write.get(b)
            if lw is not None:
                deps.add(lw)
            for e2, i2 in self.rd_e.get(b, {}).items():
                deps.add(("e", e2, i2))
            for tok in self.rd_d.get(b, ()):
                deps.add(tok)
        op = dict(fn=fn, waits=[], inc=False, dma=None)
        if dma or cc:
            rk = "cc" if cc else eng
            base, cnt = DMASEM_RANGE[rk]
            s = base + self.dma_rr_e.get(rk, 0)
            self.dma_rr_e[rk] = (self.dma_rr_e.get(rk, 0) + 1) % cnt
            if self.dma_last[s] is not None:
                deps.add(self.dma_last[s])
            amt = 1 if cc else 16
            self.dma_val[s] += amt
            tok = ("d", s, self.dma_val[s])
            self.dma_last[s] = tok
            op["dma"] = (s, self.dma_val[s], amt)
        else:
            tok = ("e", eng, idx)
        best = {}
        for d in deps:
            if d[0] == "e":
                _, se, si = d
                if se == eng and (eng == "pe" or si >= idx):
                    continue
                if self.seen[eng].get(se, -1) >= si:
                    continue
                if best.get(("e", se), -1) < si:
                    best[("e", se)] = si
            else:
                _, s, v = d
                if self.dma_seen[eng].get(s, 0) >= v:
                    continue
                if best.get(("d", s), 0) < v:
                    best[("d", s)] = v
        for k, v in best.items():
            if k[0] == "e":
                self.seen[eng][k[1]] = v
                self.ops[k[1]][v]["inc"] = True
                op["waits"].append(("e", k[1], v))
            else:
                self.dma_seen[eng][k[1]] = v
                op["waits"].append(("d", k[1], v))
        ops.append(op)
        for b in reads:
            if tok[0] == "e":
                m = self.rd_e.setdefault(b, {})
                m[eng] = idx
            else:
                self.rd_d.setdefault(b, set()).add(tok)
        for b in writes:
            self.last_write[b] = tok
            self.rd_e[b] = {}
            self.rd_d[b] = set()
        return tok

    def barrier(self, scratch_ap):
        deps = []
        for e in ENGS:
            if self.ops[e]:
                deps.append(e)
        keys_w = ["__barrier__"]
        idx = len(self.ops["dve"])
        op = dict(fn=lambda e: e.memset(scratch_ap, 0.0), waits=[], inc=True, dma=None)
        for e in ENGS:
            if not self.ops[e]:
                continue
            li = len(self.ops[e]) - 1
            while li >= 0 and self.ops[e][li]["dma"] is not None:
                li -= 1
            if li >= 0 and self.seen["dve"].get(e, -1) < li:
                self.seen["dve"][e] = li
                self.ops[e][li]["inc"] = True
                op["waits"].append(("e", e, li))
        for s in range(NDMASEM):
            if self.dma_last[s] is not None and self.dma_seen["dve"].get(s, 0) < self.dma_val[s]:
                self.dma_seen["dve"][s] = self.dma_val[s]
                op["waits"].append(("d", s, self.dma_val[s]))
        self.ops["dve"].append(op)
        self.phase_tok = ("e", "dve", idx)
        self.last_write = {}
        self.rd_e = {}
        self.rd_d = {}

    def emit(self, final_waits=()):
        nc = self.nc
        for t in final_waits:
            if t[0] == "e":
                self.ops[t[1]][t[2]]["inc"] = True
        with contextlib.ExitStack() as st:
            esems = {}
            counts = {}
            for e in ENGS:
                c = 0
                cl = []
                for op in self.ops[e]:
                    if op["inc"] and op["dma"] is None:
                        c += 1
                    cl.append(c)
                counts[e] = cl
                nep = max(0, c - 1) // EPOCH + 1
                esems[e] = [st.enter_context(nc.semaphore(f"s_{e}_{k}")) for k in range(nep)]
            dsems = [st.enter_context(nc.semaphore(f"s_dma_{k}")) for k in range(NDMASEM)]
            block = st.enter_context(nc.Block())

            def split(c):
                ep = (c - 1) // EPOCH
                return ep, c - ep * EPOCH

            def do_wait(engine, w):
                if w[0] == "e":
                    ep, r = split(counts[w[1]][w[2]])
                    engine.wait_ge(esems[w[1]][ep], r)
                else:
                    engine.wait_ge(dsems[w[1]], w[2])

            def run(e):
                def body(engine):
                    for i, op in enumerate(self.ops[e]):
                        for w in op["waits"]:
                            do_wait(engine, w)
                        ins = op["fn"](engine)
                        if op["dma"] is not None:
                            ins.then_inc(dsems[op["dma"][0]], op["dma"][2])
                        elif op["inc"]:
                            ep, r = split(counts[e][i])
                            ins.then_inc(esems[e][ep], 1)
                    if e == "sp":
                        for w in final_waits:
                            do_wait(engine, w)
                return body

            block.tensor(run("pe"))
            block.scalar(run("act"))
            block.vector(run("dve"))
            block.gpsimd(run("pool"))
            block.sync(run("sp"))


VEC_SPEC = [("ffn1_norm", 8), ("mix_norm", 8), ("mem_norm", 8), ("ffn2_norm", 8), ("final_norm", 8),
            ("rwkv_mu", 14), ("rwkv_w0", 4), ("rwkv_a0", 4), ("rwkv_k_k", 4), ("rwkv_k_a", 4),
            ("rwkv_r_k", 4), ("rwkv_ln_w", 4), ("rwkv_ln_b", 4), ("rwkv_v0", 4),
            ("conv0", 8), ("conv1", 8), ("conv2", 8), ("conv3", 8), ("gla_a_bias", 2), ("gla_norm", 4)]
VOFF = {}
_o = 0
for _n, _c in VEC_SPEC:
    VOFF[_n] = _o
    _o += _c
NV = _o
NST = 512 + 256 + 14 + 24


class B:
    def __init__(self, n_slots, n_sub, TS, norm_after, stages=("ffn1", "mix", "ffn2"), dbg=(), n_cores=8):
        self.n_cores = n_cores
        self.nc = nc = bass.Bass("TRN2", target_bir_lowering=False)
        self.S = Sched(nc)
        self.NSL, self.NSUB, self.TS = n_slots, n_sub, TS
        self.NT = TS // 512
        self.stages = stages
        self.norm_after = list(norm_after)
        self.dbg = dbg
        self.dbg_out = {}
        self.mix_parts = ("xa", "gla", "rwkv", "merge")
        NSL = n_slots
        din = lambda n, sh: nc.dram_tensor(n, sh, F32, kind="ExternalInput").ap()
        dout = lambda n, sh: nc.dram_tensor(n, sh, F32, kind="ExternalOutput").ap()
        self.x_in = din("x_in", [n_sub, 128, 8, TS])
        self.x_out = dout("x_out", [n_sub, 128, 8, TS])
        self.y_out = dout("y_out", [max(1, len(self.norm_after)), n_sub, 128, 8, TS])
        self.memT = din("memT", [128, 8, NMEM])
        self.vecs = din("vecs", [NSL, 128, NV])
        self.flags = din("flags", [128, 2 * NSL])
        self.st_in = din("st_in", [128, NST])
        self.st_out = dout("st_out", [128, NST])
        self.vf_in = din("vf_in", [n_sub, 128, 4, TS])
        self.vf_out = dout("vf_out", [n_sub, 128, 4, TS])
        self.w_f1i = din("w_f1i", [NSL, NFF, 128, 8 * 256])
        self.w_f1o = din("w_f1o", [NSL, 8, 128, NFF * 128])
        self.w_f2i = din("w_f2i", [NSL, NFF, 128, 8 * 256])
        self.w_f2o = din("w_f2o", [NSL, 8, 128, NFF * 128])
        self.w_in = din("w_in", [NSL, 54, 128, 8 * 128])
        self.w_al = din("w_al", [NSL, 128, 8 * 16])
        self.w_kv = din("w_kv", [NSL, 128, 8 * 1024])
        self.w_br = din("w_br", [NSL, 8, 128, 3 * 4 * 128])
        self.w_o = din("w_o", [NSL, 8, 128, 8 * 128])
        self.w_w2 = din("w_w2", [NSL, 64, 512])
        self.w_a2 = din("w_a2", [NSL, 64, 512])
        self.w_g2 = din("w_g2", [NSL, 128, 512])
        self.w_v1 = din("w_v1", [NSL, 128, 4 * 32])
        self.w_v2 = din("w_v2", [NSL, 32, 512])
        self.w_aup = din("w_aup", [NSL, 16, 256])
        self.uid = 0
        self.st_snd = nc.dram_tensor("st_snd", [128, NST], F32).ap()
        self.st_pair = nc.dram_tensor("st_pair", [256, NST], F32).ap()

    def mm(self, out, lhsT, rhs, start, stop, r, w, skip=False):
        if skip:
            self.S.add("pe", lambda e: e.matmul(out, lhsT, rhs, start=start, stop=stop, skip_group_check=True), r, w)
        else:
            self.S.add("pe", lambda e: e.matmul(out, lhsT, rhs, start=start, stop=stop), r, w)

    def act(self, out, in_, func, r, w, bias=None, scale=None, eng="act"):
        kw = {}
        if bias is not None:
            kw["bias"] = bias
        if scale is not None:
            kw["scale"] = scale
        self.S.add("act", lambda e: e.activation(out=out, in_=in_, func=func, **kw), r, w)

    def tt(self, out, in0, in1, op, r, w, eng="dve"):
        self.S.add(eng, lambda e: e.tensor_tensor(out=out, in0=in0, in1=in1, op=op), r, w)

    def ts(self, out, in0, s1, s2, op0, op1, r, w, eng="dve"):
        if op1 is None:
            self.S.add(eng, lambda e: e.tensor_scalar(out=out, in0=in0, scalar1=s1, scalar2=None, op0=op0), r, w)
        else:
            self.S.add(eng, lambda e: e.tensor_scalar(out=out, in0=in0, scalar1=s1, scalar2=s2, op0=op0, op1=op1), r, w)

    def stt(self, out, in0, scalar, in1, op0, op1, r, w):
        self.S.add("dve", lambda e: e.scalar_tensor_tensor(out=out, in0=in0, scalar=scalar, in1=in1, op0=op0, op1=op1), r, w)

    def cp(self, out, in_, r, w, eng="dve"):
        if eng == "act_copy":
            self.S.add("act", lambda e: e.activation(out=out, in_=in_, func=AF.Copy), r, w)
        else:
            self.S.add(eng, lambda e: e.tensor_copy(out=out, in_=in_), r, w)

    def recip(self, out, in_, r, w):
        self.S.add("dve", lambda e: e.reciprocal(out=out, in_=in_), r, w)

    def dma(self, out, in_, r, w, eng="sp"):
        return self.S.add(eng, lambda e: e.dma_start(out=out, in_=in_), r, w, dma=True)

    def memset(self, ap, val, w, eng="pool"):
        self.S.add(eng, lambda e: e.memset(ap, val), (), w)

    def dump(self, name, ap, shape, key):
        if name not in self.dbg:
            return
        t = self.nc.dram_tensor("dbg_" + name, list(shape), ap.dtype if hasattr(ap, "dtype") else F32, kind="ExternalOutput").ap()
        self.dbg_out[name] = self.dma(t, ap, [key], ["dbg_" + name])

    def sb(self, st, name, shape, dt):
        self.uid += 1
        return st.enter_context(self.nc.sbuf_tensor("%s_%d" % (name, self.uid), list(shape), dt))

    def build(self):
        nc, S, TS, NT = self.nc, self.S, self.TS, self.NT
        with contextlib.ExitStack() as st:
            self.PB = [st.enter_context(nc.psum_tensor(f"pb{i}", [128, 512], F32)) for i in range(8)]
            self.c_scr = self.sb(st, "c_scr", [128, 8], F32)
            self.alloc_consts(st)
            self.vec = self.sb(st, "vec", [128, NV], F32)
            self.flg = self.sb(st, "flg", [128, 2 * self.NSL], F32)
            self.rwS = self.sb(st, "rwS", [128, 4, 128], F32)
            self.glS = self.sb(st, "glS", [128, 2, 128], F32)
            self.glSb = self.sb(st, "glSb", [128, 2, 128], BF16)
            self.uprev = self.sb(st, "uprev", [128, 14], F32)
            self.convh = self.sb(st, "convh", [128, 8, 3], F32)
            self.xT = self.sb(st, "xT", [128, 8, TS], F32)
            self.hT = self.sb(st, "hT", [128, 8, TS], BF16)
            self.init_consts()
            self.dma(self.flg[:], self.flags, [], ["flg"])
            finals = []
            for slot in range(self.NSL):
                self.dma(self.vec[:], self.vecs[slot], [], ["vec"])
                self.load_state(slot)
                for sub in range(self.NSUB):
                    src = self.x_in if slot == 0 else self.x_out
                    self.dma(self.xT[:], src[sub], ["xdram%d" % sub], ["xT"])
                    if "ffn1" in self.stages:
                        self.ffn(st, slot, self.w_f1i, self.w_f1o, VOFF["ffn1_norm"])
                    if "mix" in self.stages:
                        self.mixer(st, slot, sub)
                    if sub == self.NSUB - 1:
                        finals.append(self.store_state(slot))
                    if "ffn2" in self.stages:
                        self.ffn(st, slot, self.w_f2i, self.w_f2o, VOFF["ffn2_norm"])
                    t = self.dma(self.x_out[sub], self.xT[:], ["xT"], ["xdram%d" % sub])
                    if slot == self.NSL - 1:
                        finals.append(t)
                    if slot in self.norm_after:
                        self.rmsnorm(VOFF["final_norm"], out_f32=True)
                        t = self.dma(self.y_out[self.norm_after.index(slot), sub], self.xT[:], ["xT"], ["ydram"])
                        finals.append(t)
            for k, t in self.dbg_out.items():
                finals.append(t)
            S.emit(final_waits=finals)
        return nc

    def alloc_consts(self, st):
        self.ident = self.sb(st, "ident", [128, 128], F32)
        self.on_d = self.sb(st, "on_d", [128, 128], BF16)
        self.on_1 = self.sb(st, "on_1", [128, 128], BF16)
        self.on_128 = self.sb(st, "on_128", [128, 128], BF16)
        self.blk = self.sb(st, "blk", [128, 128], F32)
        self.blk64 = self.sb(st, "blk64", [128, 128], F32)
        self.triu = self.sb(st, "triu", [128, 128], F32)
        self.mpast = self.sb(st, "mpast", [128, 4, 128], F32)
        self.mfut = self.sb(st, "mfut", [128, 4, 128], F32)
        self.mstr = self.sb(st, "mstr", [128, 4, 128], F32)
        self.smask = self.sb(st, "smask", [128, 512], F32)
        self.c_eps = self.sb(st, "c_eps", [128, 4], F32)

    def init_consts(self):
        S = self.S
        ms = self.memset
        ms(self.ident[:], 1.0, ["ident"])
        S.add("pool", lambda e: e.affine_select(out=self.ident[:], in_=self.ident[:], pattern=[[-1, 128]],
                                                compare_op=ALU.is_equal, fill=0.0, base=0, channel_multiplier=1),
              ["ident"], ["ident"])
        ms(self.on_d[:], 1.0 / D, ["on_d"])
        ms(self.on_1[:], 1.0, ["on_1"])
        ms(self.on_128[:], 1.0 / 128, ["on_128"])
        ms(self.blk[:], 0.0, ["blk"])
        ms(self.blk[0:64, 0:64], 1.0, ["blk"])
        ms(self.blk[64:128, 64:128], 1.0, ["blk"])
        self.ts(self.blk64[:], self.blk[:], 1.0 / 64, None, ALU.mult, None, ["blk"], ["blk64"], eng="pool")
        ms(self.triu[:], 1.0, ["triu"])
        S.add("pool", lambda e: e.affine_select(out=self.triu[:], in_=self.triu[:], pattern=[[1, 128]],
                                                compare_op=ALU.is_ge, fill=0.0, base=0, channel_multiplier=-1),
              ["triu"], ["triu"])
        for h in range(4):
            self.tt(self.mpast[:, h, :], self.triu[:], self.blk[:], ALU.mult, ["triu", "blk"], ["mpast"], eng="pool")
        for h in range(4):
            self.tt(self.mfut[:, h, :], self.blk[:], self.mpast[:, h, :], ALU.subtract, ["blk", "mpast"], ["mfut"], eng="pool")
            self.tt(self.mstr[:, h, :], self.mpast[:, h, :], self.ident[:], ALU.subtract, ["ident", "mpast"], ["mstr"], eng="pool")
        ms(self.c_eps[:, 0:1], EPS, ["c_eps"])
        ms(self.c_eps[:, 1:2], LN_EPS, ["c_eps"])
        ms(self.c_eps[:, 2:3], 1.0, ["c_eps"])
        ms(self.c_eps[:, 3:4], 0.0, ["c_eps"])
        ms(self.smask[:], 1.0, ["smask"])
        ms(self.smask[:].rearrange("p (c s) -> p c s", s=64)[:, :, 0:1], 0.0, ["smask"])

    def load_state(self, slot):
        f = self.flg[:, 2 * slot + 1:2 * slot + 2]
        tmp = [(self.rwS[:].rearrange("p a b -> p (a b)"), 0, 512, "rwS"),
               (self.glS[:].rearrange("p a b -> p (a b)"), 512, 256, "glS"),
               (self.uprev[:], 768, 14, "uprev"),
               (self.convh[:].rearrange("p a b -> p (a b)"), 782, 24, "convh")]
        for ap, o, n, key in tmp:
            if slot == 0:
                self.dma(ap, self.st_in[:, o:o + n], [], [key])
            else:
                self.dma(ap, self.st_pair[0:128, o:o + n], ["st_pair"], [key])
            self.ts(ap, ap, f, None, ALU.mult, None, [key, "flg"], [key])

    def store_state(self, slot):
        t = None
        tmp = [(self.rwS[:].rearrange("p a b -> p (a b)"), 0, 512, "rwS"),
               (self.glS[:].rearrange("p a b -> p (a b)"), 512, 256, "glS"),
               (self.uprev[:], 768, 14, "uprev"),
               (self.convh[:].rearrange("p a b -> p (a b)"), 782, 24, "convh")]
        last = slot == self.NSL - 1
        for ap, o, n, key in tmp:
            if last:
                t = self.dma(self.st_out[:, o:o + n], ap, [key], ["st_out%d" % o])
            else:
                t = self.dma(self.st_snd[:, o:o + n], ap, [key], ["st_snd"])
        if not last:
            groups = [[2 * i, 2 * i + 1] for i in range(self.n_cores // 2)]
            snd, pair = self.st_snd, self.st_pair
            t = self.S.add("pool", lambda e: e.collective_compute("AllGather", ALU.bypass, replica_groups=groups,
                                                                  ins=[snd.opt()], outs=[pair.opt()]),
                           ["st_snd"], ["st_pair"], cc=True)
        return t

    def rmsnorm(self, goff, out_f32=False):
        with contextlib.ExitStack() as st:
            sq = self.sb(st, "n_sq", [128, 8, 512], BF16)
            rs = self.sb(st, "n_rs", [128, 512], F32)
            for nt in range(self.NT):
                tsl = slice(nt * 512, (nt + 1) * 512)
                ps = self.PB[7]
                for c in range(8):
                    self.act(sq[:, c, :], self.xT[:, c, tsl], AF.Square, ["xT"], ["n_sq%d" % c])
                    self.mm(ps[:], self.on_d[:], sq[:, c, :], c == 0, c == 7, ["on_d", "n_sq%d" % c], ["pb7"])
                self.act(rs[:], ps[:], AF.Ln, ["pb7", "c_eps"], ["n_rs"], bias=self.c_eps[:, 0:1])
                self.act(rs[:], rs[:], AF.Exp, ["n_rs"], ["n_rs"], scale=-0.5)
                for c in range(8):
                    g = self.vec[:, goff + c:goff + c + 1]
                    dst = self.xT[:, c, tsl] if out_f32 else self.hT[:, c, tsl]
                    self.stt(dst, self.xT[:, c, tsl], g, rs[:], ALU.mult, ALU.mult, ["xT", "vec", "n_rs"],
                             ["xT" if out_f32 else "hT"])
            self.S.barrier(self.c_scr[:, 0:1])

    def ffn(self, st0, slot, w_i, w_o, goff):
        self.rmsnorm(goff)
        NT = self.NT
        with contextlib.ExitStack() as st:
            actT = self.sb(st, "f_act", [128, NFF, self.TS], BF16)
            NWB, NWO = 6, 3
            wb = [self.sb(st, "f_wi%d" % i, [128, 8, 256], BF16) for i in range(NWB)]
            wo = [self.sb(st, "f_wo%d" % i, [128, NFF, 128], BF16) for i in range(NWO)]
            sg = [self.sb(st, "f_sg%d" % i, [128, 512], F32) for i in range(2)]
            it = 0
            for j in range(NFF):
                w = wb[j % NWB]
                wk = "f_wi%d" % (j % NWB)
                self.dma(w[:].rearrange("p a b -> p (a b)"), w_i[slot, j], [], [wk], eng="pool")
                for nt in range(NT):
                    tsl = slice(nt * 512, (nt + 1) * 512)
                    k = it % 2
                    it += 1
                    pg, pu = self.PB[k], self.PB[2 + k]
                    for kc in range(8):
                        self.mm(pg[:], w[:, kc, 0:128], self.hT[:, kc, tsl], kc == 0, kc == 7, [wk, "hT"], ["pb%d" % k])
                    for kc in range(8):
                        self.mm(pu[:], w[:, kc, 128:256], self.hT[:, kc, tsl], kc == 0, kc == 7, [wk, "hT"], ["pb%d" % (2 + k)])
                    self.act(sg[k][:], pg[:], AF.Silu, ["pb%d" % k], ["f_sg%d" % k])
                    self.tt(actT[:, j, tsl], sg[k][:], pu[:], ALU.mult, ["f_sg%d" % k, "pb%d" % (2 + k)], ["f_act%d" % j])
            it = 0
            for dc in range(8):
                w = wo[dc % NWO]
                wk = "f_wo%d" % (dc % NWO)
                self.dma(w[:].rearrange("p a b -> p (a b)"), w_o[slot, dc], [], [wk], eng="pool")
                for nt in range(NT):
                    tsl = slice(nt * 512, (nt + 1) * 512)
                    k = 4 + it % 2
                    it += 1
                    po = self.PB[k]
                    for j in range(NFF):
                        self.mm(po[:], w[:, j, :], actT[:, j, tsl], j == 0, j == NFF - 1, [wk, "f_act%d" % j], ["pb%d" % k])
                    self.stt(self.xT[:, dc, tsl], po[:], 0.5, self.xT[:, dc, tsl], ALU.mult, ALU.add,
                             ["pb%d" % k, "xT"], ["xT"])
            self.S.barrier(self.c_scr[:, 0:1])

    def win_tile(self, slot, ci, bufs, cnt):
        k = cnt[0] % len(bufs)
        cnt[0] += 1
        w = bufs[k]
        key = "m_wi%d" % k
        self.dma(w[:].rearrange("p a b -> p (a b)"), self.w_in[slot, ci], [], [key], eng="pool")
        return w, key

    def proj(self, ps, pkey, w, wkey, tsl, M=128):
        for kc in range(8):
            self.mm(ps, w[:, kc, 0:M], self.hT[:, kc, tsl], kc == 0, kc == 7, [wkey, "hT"], [pkey])

    def mixer(self, st0, slot, sub):
        self.rmsnorm(VOFF["mix_norm"])
        TS, NT = self.TS, self.NT
        self.dma(self.x_out[sub], self.xT[:], ["xT"], ["xdram%d" % sub])
        self.S.barrier(self.c_scr[:, 0:1])
        self.xslots = [self.xT[:, c, h * 512:(h + 1) * 512] for c in range(8) for h in range(TS // 512)]
        with contextlib.ExitStack() as st:
            self.yT = self.sb(st, "yT", [128, 12, TS], BF16)
            self.wib = [self.sb(st, "m_wi%d" % i, [128, 8, 128], BF16) for i in range(4)]
            self.wcnt = [0]
            if len(self.mix_parts) < 4:
                self.memset(self.yT[:].rearrange("p a b -> p (a b)"), 0.0, ["yT"])
            if "xa" in self.mix_parts:
                self.mix_xa(slot)
                self.S.barrier(self.c_scr[:, 0:1])
            if "gla" in self.mix_parts:
                self.mix_gla(slot, sub)
                self.S.barrier(self.c_scr[:, 0:1])
            if "rwkv" in self.mix_parts:
                self.mix_rwkv(slot, sub)
                self.S.barrier(self.c_scr[:, 0:1])
            self.dma(self.xT[:], self.x_out[sub], ["xdram%d" % sub], ["xT"])
            if "merge" in self.mix_parts:
                self.mix_merge(slot)
                self.S.barrier(self.c_scr[:, 0:1])
            self.dump("yT", self.yT[:], [128, 12, TS], "yT")
            self.S.barrier(self.c_scr[:, 0:1])

    def mix_xa(self, slot):
        TS, NT = self.TS, self.NT
        with contextlib.ExitStack() as st:
            mem = self.sb(st, "xa_mem", [128, 8, NMEM], F32)
            msq = self.sb(st, "xa_msq", [128, 8, NMEM], BF16)
            mrs = self.sb(st, "xa_mrs", [128, NMEM], F32)
            memn = self.sb(st, "xa_memn", [128, 8, NMEM], BF16)
            wkv = self.sb(st, "xa_wkv", [128, 8, 1024], BF16)
            KT = self.sb(st, "xa_KT", [128, 4, NMEM], BF16)
            V = self.sb(st, "xa_V", [128, 2, 512], BF16)
            qT = self.sb(st, "xa_q", [128, 512], BF16)
            E = self.sb(st, "xa_E", [128, 2, 512], BF16)
            rden = self.sb(st, "xa_rden", [128, 512], F32)
            self.dma(mem[:], self.memT, [], ["xa_mem"])
            self.dma(wkv[:].rearrange("p a b -> p (a b)"), self.w_kv[slot], [], ["xa_wkv"], eng="pool")
            ps = self.PB[0]
            for c in range(8):
                self.act(msq[:, c, :], mem[:, c, :], AF.Square, ["xa_mem"], ["xa_msq"])
                self.mm(ps[:, 0:NMEM], self.on_d[:], msq[:, c, :], c == 0, c == 7, ["on_d", "xa_msq"], ["pb0"])
            self.act(mrs[:], ps[:, 0:NMEM], AF.Ln, ["pb0", "c_eps"], ["xa_mrs"], bias=self.c_eps[:, 0:1])
            self.act(mrs[:], mrs[:], AF.Exp, ["xa_mrs"], ["xa_mrs"], scale=-0.5)
            go = VOFF["mem_norm"]
            for c in range(8):
                self.stt(memn[:, c, :], mem[:, c, :], self.vec[:, go + c:go + c + 1], mrs[:], ALU.mult, ALU.mult,
                         ["xa_mem", "vec", "xa_mrs"], ["xa_memn"])
            for h in range(4):
                ps = self.PB[h % 2]
                pk = "pb%d" % (h % 2)
                for kc in range(8):
                    self.mm(ps[:, 0:NMEM], wkv[:, kc, h * 128:(h + 1) * 128], memn[:, kc, :], kc == 0, kc == 7,
                            ["xa_wkv", "xa_memn"], [pk])
                self.act(KT[:, h, :], ps[:, 0:NMEM], AF.Copy, [pk], ["xa_KT"])
            for mb in range(2):
                ps = self.PB[2 + mb]
                pk = "pb%d" % (2 + mb)
                for kc in range(8):
                    self.mm(ps[:], memn[:, kc, mb * 128:(mb + 1) * 128], wkv[:, kc, 512:1024], kc == 0, kc == 7,
                            ["xa_wkv", "xa_memn"], [pk])
                self.act(V[:, mb, :], ps[:], AF.Copy, [pk], ["xa_V"])
            for h in range(4):
                w, wk = self.win_tile(slot, 26 + h, self.wib, self.wcnt)
                for nt in range(NT):
                    tsl = slice(nt * 512, (nt + 1) * 512)
                    self.proj(self.PB[4][:], "pb4", w, wk, tsl)
                    self.act(qT[:], self.PB[4][:], AF.Copy, ["pb4"], ["xa_q"], scale=float(128 ** -0.5))
                    for mb in range(2):
                        self.mm(self.PB[5][:], KT[:, h, mb * 128:(mb + 1) * 128], qT[:], True, True, ["xa_KT", "xa_q"], ["pb5"])
                        self.act(E[:, mb, :], self.PB[5][:], AF.Exp, ["pb5"], ["xa_E%d" % mb])
                    for mb in range(2):
                        self.mm(self.PB[6][:], V[:, mb, h * 128:(h + 1) * 128], E[:, mb, :], mb == 0, mb == 1,
                                ["xa_V", "xa_E%d" % mb], ["pb6"])
                    for mb in range(2):
                        self.mm(self.PB[7][:], self.on_1[:], E[:, mb, :], mb == 0, mb == 1, ["on_1", "xa_E%d" % mb], ["pb7"])
                    self.act(rden[:], self.PB[7][:], AF.Ln, ["pb7"], ["xa_rden"])
                    self.act(rden[:], rden[:], AF.Exp, ["xa_rden"], ["xa_rden"], scale=-1.0)
                    self.tt(self.yT[:, 8 + h, tsl], self.PB[6][:], rden[:], ALU.mult, ["pb6", "xa_rden"], ["yT"])

    def mix_gla(self, slot, sub):
        TS, NT = self.TS, self.NT
        NBLK = TS // 128
        S = self.S
        import os
        GS = int(os.environ.get("GLA_STOP", "99"))
        with contextlib.ExitStack() as st:
            sgo = self.sb(st, "g_sgo", [128, 4, TS], BF16)
            v = self.sb(st, "g_v", [128, 4, TS], F32)
            Eg = self.sb(st, "g_Eg", [128, 2, TS], F32)
            qg = self.sb(st, "g_qg", [128, 2, TS], BF16)
            qr = self.sb(st, "g_qr", [128, 2, TS], BF16)
            kg = self.sb(st, "g_kg", [128, 2, TS], BF16)
            kr = self.sb(st, "g_kr", [128, 2, TS], BF16)
            kd = self.sb(st, "g_kd", [128, 2, TS], F32)
            it = 0
            with contextlib.ExitStack() as stB:
                qk = self.sb(stB, "g_qk", [128, 4, TS], F32)
                Eng = self.sb(stB, "g_Eng", [128, 2, TS], F32)
                alT = self.sb(stB, "g_alT", [16, TS], F32)
                aup = self.sb(stB, "g_aup", [16, 256], F32)
                wal = self.sb(stB, "g_wal", [128, 8, 16], BF16)
                self.dma(aup[:], self.w_aup[slot], [], ["g_aup"])
                self.dma(wal[:].rearrange("p a b -> p (a b)"), self.w_al[slot], [], ["g_wal"], eng="pool")
                for c in range(4):
                    w, wk = self.win_tile(slot, 22 + c, self.wib, self.wcnt)
                    for nt in range(NT):
                        tsl = slice(nt * 512, (nt + 1) * 512)
                        k = it % 2
                        it += 1
                        self.proj(self.PB[k][:], "pb%d" % k, w, wk, tsl)
                        self.act(sgo[:, c, tsl], self.PB[k][:], AF.Silu, ["pb%d" % k], ["g_sgo"])
                for nt in range(NT):
                    tsl = slice(nt * 512, (nt + 1) * 512)
                    k = it % 2
                    it += 1
                    self.proj(self.PB[k][0:16, :], "pb%d" % k, wal, "g_wal", tsl, M=16)
                    self.act(alT[:, tsl], self.PB[k][0:16, :], AF.Copy, ["pb%d" % k], ["g_alT"])
                for half in range(2):
                    with contextlib.ExitStack() as stC:
                        ug = self.sb(stC, "g_ug", [128, 4, 3 + TS], F32)
                        tmp = self.sb(stC, "g_ctmp", [128, TS], F32)
                        self.cp(ug[:, :, 0:3], self.convh[:, half * 4:half * 4 + 4, :], ["convh"], ["g_ug"])
                        for c4 in range(4):
                            c = half * 4 + c4
                            w, wk = self.win_tile(slot, 14 + c, self.wib, self.wcnt)
                            for nt in range(NT):
                                tsl = slice(nt * 512, (nt + 1) * 512)
                                k = it % 2
                                it += 1
                                self.proj(self.PB[k][:], "pb%d" % k, w, wk, tsl)
                                self.act(ug[:, c4, 3 + nt * 512:3 + (nt + 1) * 512], self.PB[k][:], AF.Copy, ["pb%d" % k], ["g_ug"])
                        self.cp(self.convh[:, half * 4:half * 4 + 4, :], ug[:, :, TS:TS + 3], ["g_ug"], ["convh"])
                        for c4 in range(4):
                            c = half * 4 + c4
                            cw = [self.vec[:, VOFF["conv%d" % jj] + c:VOFF["conv%d" % jj] + c + 1] for jj in range(4)]
                            self.ts(tmp[:], ug[:, c4, 3:3 + TS], cw[3], None, ALU.mult, None, ["g_ug", "vec"], ["g_ctmp"])
                            for jj in (2, 1, 0):
                                self.stt(tmp[:], ug[:, c4, jj:jj + TS], cw[jj], tmp[:], ALU.mult, ALU.add, ["g_ug", "vec", "g_ctmp"], ["g_ctmp"])
                            dst = qk[:, c, :] if c < 4 else v[:, c - 4, :]
                            self.act(dst, tmp[:], AF.Silu, ["g_ctmp"], ["g_qk" if c < 4 else "g_v"])
                        self.S.barrier(self.c_scr[:, 0:1])
                if GS <= 2:
                    return
                with contextlib.ExitStack() as stC:
                    la = self.sb(stC, "g_la", [128, 2, TS], F32)
                    sgm = self.sb(stC, "g_sgm", [128, 512], F32)
                    ab = VOFF["gla_a_bias"]
                    for c2 in range(2):
                        for nt in range(NT):
                            tsl = slice(nt * 512, (nt + 1) * 512)
                            k = it % 2
                            it += 1
                            self.mm(self.PB[k][:], aup[0:16, c2 * 128:(c2 + 1) * 128], alT[0:16, tsl], True, True, ["g_aup", "g_alT"], ["pb%d" % k])
                            self.act(sgm[:], self.PB[k][:], AF.Sigmoid, ["pb%d" % k, "vec"], ["g_sgm"], bias=self.vec[:, ab + c2:ab + c2 + 1])
                            self.act(sgm[:], sgm[:], AF.Ln, ["g_sgm"], ["g_sgm"])
                            S.add("dve", (lambda o, d1: (lambda e: e.tensor_tensor_scan(out=o, data0=self.smask[:], data1=d1, initial=0.0,
                                                                                       op0=ALU.mult, op1=ALU.add)))(la[:, c2, tsl], sgm[:]),
                                  ["smask", "g_sgm"], ["g_la"])
                    self.act(Eg[:].rearrange("p a b -> p (a b)"), la[:].rearrange("p a b -> p (a b)"), AF.Exp, ["g_la"], ["g_Eg"], scale=1.0 / 16)
                    self.act(Eng[:].rearrange("p a b -> p (a b)"), la[:].rearrange("p a b -> p (a b)"), AF.Exp, ["g_la"], ["g_Eng"], scale=-1.0 / 16)
                    self.S.barrier(self.c_scr[:, 0:1])
                for c2 in range(2):
                    self.stt(qg[:, c2, :], qk[:, c2, :], 0.125, Eg[:, c2, :], ALU.mult, ALU.mult, ["g_qk", "g_Eg"], ["g_qg"])
                    self.stt(qr[:, c2, :], qk[:, c2, :], 0.125, Eng[:, c2, :], ALU.mult, ALU.mult, ["g_qk", "g_Eng"], ["g_qr"])
                    self.tt(kg[:, c2, :], qk[:, 2 + c2, :], Eng[:, c2, :], ALU.mult, ["g_qk", "g_Eng"], ["g_kg"])
                    self.tt(kr[:, c2, :], qk[:, 2 + c2, :], Eg[:, c2, :], ALU.mult, ["g_qk", "g_Eg"], ["g_kr"])
                    for ch in range(TS // 64):
                        cs_ = slice(ch * 64, ch * 64 + 64)
                        self.stt(kd[:, c2, cs_], qk[:, 2 + c2, cs_], Eg[:, c2, ch * 64 + 63:ch * 64 + 64], Eng[:, c2, cs_],
                                 ALU.mult, ALU.mult, ["g_qk", "g_Eg", "g_Eng"], ["g_kd"])
                self.S.barrier(self.c_scr[:, 0:1])
            if GS <= 4:
                return
            v_tok = self.sb(st, "g_vtok", [128, 512], F32)
            kd2 = self.sb(st, "g_kd2", [128, 2, 256], F32)
            kgp = self.sb(st, "g_kgp", [128, 2, 2, 128], BF16)
            krp = self.sb(st, "g_krp", [128, 2, 2, 128], BF16)
            qgp = self.sb(st, "g_qgp", [128, 2, 2, 128], F32)
            at = self.sb(st, "g_at", [128, 512], F32)
            at2 = self.sb(st, "g_at2", [128, 512], F32)
            atb = self.sb(st, "g_atb", [128, 512], F32)
            sq = self.sb(st, "g_sq", [128, 512], BF16)
            rs = self.sb(st, "g_rs", [128, 512], F32)
            ot = self.sb(st, "g_ot", [128, 512], F32)
            glS = self.glS
            for t_, k_ in ((kd2, "g_kd2"), (kgp, "g_kgp"), (krp, "g_krp"), (qgp, "g_qgp")):
                self.memset(t_[:].rearrange("p a b -> p (a b)") if len(t_.shape) == 3 else t_[:].rearrange("p a b c -> p (a b c)"), 0.0, [k_])
            mp = self.mpast[:].rearrange("p a b -> p (a b)")
            mf = self.mfut[:].rearrange("p a b -> p (a b)")
            gn = VOFF["gla_norm"]
            for b in range(NBLK):
                bsl = slice(b * 128, (b + 1) * 128)
                for c in range(4):
                    S.add("pe", (lambda o, i_: (lambda e: e.transpose(o, i_, self.ident[:])))(self.PB[0][:, c * 128:(c + 1) * 128], v[:, c, bsl]),
                          ["g_v", "ident"], ["pb0"])
                self.cp(v_tok[:], self.PB[0][:], ["pb0"], ["g_vtok"], eng="act_copy")
                for c2 in range(2):
                    S.add("pe", (lambda o, i_: (lambda e: e.transpose(o, i_, self.ident[:])))(self.PB[1][:, c2 * 128:(c2 + 1) * 128], kd[:, c2, bsl]),
                          ["g_kd", "ident"], ["pb1"])
                for cc in range(2):
                    crow = slice(cc * 64, cc * 64 + 64)
                    self.cp(kd2[crow, cc, :], self.PB[1][crow, 0:256], ["pb1"], ["g_kd2"], eng="act_copy")
                for hh in range(2):
                    r2 = slice(hh * 64, hh * 64 + 64)
                    self.cp(kgp[r2, hh, :, :], kg[r2, :, bsl], ["g_kg"], ["g_kgp"], eng="pool")
                    self.cp(krp[r2, hh, :, :], kr[r2, :, bsl], ["g_kr"], ["g_krp"], eng="pool")
                    self.cp(qgp[r2, hh, :, :], qg[r2, :, bsl], ["g_qg"], ["g_qgp"], eng="pool")
                if GS <= 5:
                    continue
                for h in range(4):
                    hs = slice(h * 128, (h + 1) * 128)
                    self.mm(self.PB[2][:, hs], kgp[:, h % 2, h // 2, :], qg[:, h // 2, bsl], True, True, ["g_kgp", "g_qg"], ["pb2"])
                    self.mm(self.PB[3][:, hs], krp[:, h % 2, h // 2, :], qr[:, h // 2, bsl], True, True, ["g_krp", "g_qr"], ["pb3"])
                self.tt(at[:], self.PB[2][:], mp, ALU.mult, ["pb2", "mpast"], ["g_at"])
                self.tt(at2[:], self.PB[3][:], mf, ALU.mult, ["pb3", "mfut"], ["g_at2"])
                self.tt(atb[:], at[:], at2[:], ALU.add, ["g_at", "g_at2"], ["g_atb"], eng="pool")
                if GS <= 6:
                    continue
                po = self.PB[4]
                for h in range(4):
                    hs = slice(h * 128, (h + 1) * 128)
                    self.mm(po[:, hs], v_tok[:, hs], atb[:, hs], h == 0, False, ["g_vtok", "g_atb"], ["pb4"], skip=True)
                for cc in range(2):
                    for h in range(4):
                        self.mm(po[:, h * 128 + cc * 64:h * 128 + cc * 64 + 64], glS[:, h // 2, :], qgp[:, h % 2, h // 2, cc * 64:cc * 64 + 64],
                                False, True, ["glS", "g_qgp"], ["pb4"], skip=True)
                    for hp in range(2):
                        self.mm(self.PB[5][:, hp * 256:(hp + 1) * 256], kd2[:, cc, hp * 128:(hp + 1) * 128],
                                v_tok[:, hp * 256:(hp + 1) * 256], True, True, ["g_kd2", "g_vtok"], ["pb5"])
                    for hp in range(2):
                        for hh in range(2):
                            r2 = slice(hh * 64, hh * 64 + 64)
                            self.stt(glS[r2, hp, :], glS[r2, hp, :], Eg[r2, hp, b * 128 + cc * 64 + 63:b * 128 + cc * 64 + 64],
                                     self.PB[5][r2, hp * 256 + hh * 128:hp * 256 + (hh + 1) * 128], ALU.mult, ALU.add,
                                     ["glS", "g_Eg", "pb5"], ["glS"])
                if GS <= 7:
                    continue
                self.act(sq[:], po[:], AF.Square, ["pb4"], ["g_sq"])
                self.mm(self.PB[6][:], self.on_128[:], sq[:], True, True, ["on_128", "g_sq"], ["pb6"])
                self.act(rs[:], self.PB[6][:], AF.Ln, ["pb6", "c_eps"], ["g_rs"], bias=self.c_eps[:, 0:1])
                self.act(rs[:], rs[:], AF.Exp, ["g_rs"], ["g_rs"], scale=-0.5)
                self.tt(ot[:], po[:], rs[:], ALU.mult, ["pb4", "g_rs"], ["g_ot"])
                for h in range(4):
                    self.stt(self.yT[:, 4 + h, bsl], ot[:, h * 128:(h + 1) * 128], self.vec[:, gn + h:gn + h + 1], sgo[:, h, bsl],
                             ALU.mult, ALU.mult, ["g_ot", "vec", "g_sgo"], ["yT"])

    def mix_rwkv(self, slot, sub):
        TS, NT = self.TS, self.NT
        S = self.S
        C0 = float(np.exp(-0.5))
        V = lambda n, c: self.vec[:, VOFF[n] + c:VOFF[n] + c + 1]
        with contextlib.ExitStack() as st:
            wr = [self.sb(st, "r_w%d" % i, [128, 8, 128], BF16) for i in range(14)]
            for i in range(14):
                self.dma(wr[i][:].rearrange("p a b -> p (a b)"), self.w_in[slot, i], [], ["r_w%d" % i], eng="pool")
            w2p = self.sb(st, "r_w2p", [128, 512], F32)
            a2p = self.sb(st, "r_a2p", [128, 512], F32)
            g2 = self.sb(st, "r_g2", [128, 512], F32)
            v1 = self.sb(st, "r_v1", [128, 4, 32], F32)
            v2 = self.sb(st, "r_v2", [32, 512], F32)
            omk = self.sb(st, "r_omk", [128, 4], F32)
            self.memset(w2p[:], 0.0, ["r_w2p"])
            self.memset(a2p[:], 0.0, ["r_a2p"])
            self.dma(w2p[0:64, :], self.w_w2[slot], [], ["r_w2p"])
            self.dma(a2p[64:128, :], self.w_a2[slot], [], ["r_a2p"])
            self.dma(g2[:], self.w_g2[slot], [], ["r_g2"])
            self.dma(v1[:].rearrange("p a b -> p (a b)"), self.w_v1[slot], [], ["r_v1"])
            self.dma(v2[:], self.w_v2[slot], [], ["r_v2"])
            ka = VOFF["rwkv_k_a"]
            self.ts(omk[:], self.vec[:, ka:ka + 4], -1.0, 1.0, ALU.mult, ALU.add, ["vec"], ["r_omk"])
            def nb(n, sh=(128, 512), dt=F32):
                if tuple(sh) == (128, 512) and dt == F32 and self.xslots:
                    return self.xslots.pop()
                return self.sb(st, n, list(sh), dt)
            ub = nb("r_ub", (128, 513))
            lor1, lor2 = nb("r_lor1"), nb("r_lor2")
            uvall = nb("r_uvall", (128, 4, 512))
            t1T = nb("r_t1T", (32, 512))
            ur, uk, a_, g_, sig = [nb("r_" + n) for n in ("ur", "uk", "a", "g", "sig")]
            Eg, Eng, kk, kh, KKt, Kt, Bt, Rt, kdec, bon, tmp, tmp2, vfb = [
                nb("r_" + n) for n in ("Eg", "Eng", "kk", "kh", "KKt", "Kt", "Bt", "Rt", "kdec", "bon", "tmp", "tmp2", "vfb")]
            cs = tmp2
            bdec = Eng
            Ysb = kdec
            egl = nb("r_egl", (128, 8))
            BS = []
            for q in range(2):
                d_ = dict(q=q)
                for n in ("Ktp", "Btp", "KKp", "kdec2", "bdec2"):
                    d_[n] = nb("r_%s_%d" % (n, q), (128, 2, 128), BF16)
                d_["IAT"] = nb("r_IAT_%d" % q, (128, 2, 128), BF16)
                d_["MW"] = nb("r_MW_%d" % q, (128, 2, 4, 128), BF16)
                d_["A"] = [nb("r_A%d_%d" % (i, q), (128, 2, 128), BF16) for i in range(2)]
                d_["AT"] = [nb("r_AT%d_%d" % (i, q), (128, 2, 128), BF16) for i in range(2)]
                d_["T"] = [nb("r_T%d_%d" % (i, q), (128, 2, 128), BF16) for i in range(2)]
                d_["v_tok"] = nb("r_vtok_%d" % q, (128, 128), BF16)
                d_["vpad"] = nb("r_vpad_%d" % q, (128, 384), BF16)
                BS.append(d_)
            SApad = nb("r_SApad", (128, 384), BF16)
            negX, SA = nb("r_negX", (128, 128), BF16), nb("r_SA", (128, 128), BF16)
            KKb, Rb, Bb = [nb("r_" + n, (128, 512), BF16) for n in ("KKb", "Rb", "Bb")]
            KtP, BtP, KKP = [nb("r_" + n, (128, 2, 512), BF16) for n in ("KtP", "BtP", "KKP")]
            for t_, k_ in ((KtP, "r_KtP"), (BtP, "r_BtP"), (KKP, "r_KKP")):
                self.memset(t_[:].rearrange("p a b -> p (a b)"), 0.0, [k_])
            mask4 = nb("r_mask4", (128, 4, 128))
            id2 = nb("r_id2", (128, 2, 128))
            for d_ in BS:
                for n in ("Ktp", "Btp", "KKp", "kdec2", "bdec2"):
                    self.memset(d_[n][:].rearrange("p a b -> p (a b)"), 0.0, ["r_%s_%d" % (n, d_["q"])])
                self.memset(d_["vpad"][:], 0.0, ["r_vpad_%d" % d_["q"]])
            for t_, k_ in ((SApad, "r_SApad"), (negX, "r_negX"), (SA, "r_SA")):
                self.memset(t_[:], 0.0, [k_])
            for q_, m_ in ((0, self.mstr), (1, self.mpast), (2, self.mstr), (3, self.mpast)):
                self.cp(mask4[:, q_, :], m_[:, 0, :], ["mstr", "mpast"], ["r_mask4"], eng="pool")
            for hh in range(2):
                self.cp(id2[:, hh, :], self.ident[:], ["ident"], ["r_id2"], eng="pool")
            rwS = self.rwS
            pbi = [0]

            def bank():
                k = pbi[0] % 4
                pbi[0] += 1
                return self.PB[k], "pb%d" % k

            def shift_proj(ci, tsl, out, okey):
                ps, pk = bank()
                self.proj(ps[:], pk, wr[ci], "r_w%d" % ci, tsl)
                self.cp(ub[:, 1:513], ps[:], [pk], ["r_ub"], eng="act_copy")
                self.cp(ub[:, 0:1], self.uprev[:, ci:ci + 1], ["uprev"], ["r_ub"], eng="pool")
                self.tt(out, ub[:, 0:512], ub[:, 1:513], ALU.subtract, ["r_ub"], [okey])
                self.stt(out, out, V("rwkv_mu", ci), ub[:, 1:513], ALU.mult, ALU.add, [okey, "vec", "r_ub"], [okey])
                self.cp(self.uprev[:, ci:ci + 1], ub[:, 512:513], ["r_ub"], ["uprev"], eng="pool")

            vf_src = self.vf_in if slot == 0 else self.vf_out
            fl0 = self.flg[:, 2 * slot:2 * slot + 1]
            import os
            RS = float(os.environ.get("R_STOP", "99"))
            for nt in range(NT):
                tsl = slice(nt * 512, (nt + 1) * 512)
                shift_proj(12, tsl, tmp[:], "r_tmp")
                self.act(lor1[0:64, :], tmp[0:64, :], AF.Tanh, ["r_tmp"], ["r_lor1"])
                self.cp(lor1[64:128, :], tmp[64:128, :], ["r_tmp"], ["r_lor1"])
                shift_proj(13, tsl, tmp[:], "r_tmp")
                self.act(lor2[:], tmp[:], AF.Sigmoid, ["r_tmp"], ["r_lor2"])
                for hp in range(4):
                    shift_proj(8 + hp, tsl, uvall[:, hp, :], "r_uvall")
                ps, pk = bank()
                for c in range(4):
                    self.mm(ps[0:32, :], v1[:, c, :], uvall[:, c, :], c == 0, c == 3, ["r_v1", "r_uvall"], [pk])
                self.cp(t1T[:], ps[0:32, :], [pk], ["r_t1T"])
                if RS <= 1:
                    continue
                for hp in range(4):
                    hs = slice(hp * 128, (hp + 1) * 128)
                    uv = uvall[:, hp, :]
                    shift_proj(hp, tsl, ur[:], "r_ur")
                    shift_proj(4 + hp, tsl, uk[:], "r_uk")
                    ps, pk = bank()
                    self.mm(ps[:], v2[0:32, hs], t1T[0:32, :], True, True, ["r_v2", "r_t1T"], [pk])
                    self.act(tmp[:], ps[:], AF.Sigmoid, [pk, "vec"], ["r_tmp"], bias=V("rwkv_v0", hp))
                    self.dma(vfb[:], vf_src[sub, :, hp, tsl], ["vfdram"], ["r_vfb"])
                    self.tt(tmp2[:], vfb[:], uv, ALU.subtract, ["r_vfb", "r_uvall"], ["r_tmp2"])
                    self.tt(tmp2[:], tmp2[:], tmp[:], ALU.mult, ["r_tmp2", "r_tmp"], ["r_tmp2"])
                    self.tt(uv, uv, tmp2[:], ALU.add, ["r_uvall", "r_tmp2"], ["r_uvall"])
                    self.tt(tmp2[:], uv, vfb[:], ALU.subtract, ["r_vfb", "r_uvall"], ["r_tmp2"])
                    self.stt(vfb[:], tmp2[:], fl0, vfb[:], ALU.mult, ALU.add, ["r_tmp2", "flg", "r_vfb"], ["r_vfb"])
                    self.dma(self.vf_out[sub, :, hp, tsl], vfb[:], ["r_vfb"], ["vfdram"])
                    if RS <= 2:
                        continue
                    ps, pk = bank()
                    self.mm(ps[:], w2p[:, hs], lor1[:], True, True, ["r_w2p", "r_lor1"], [pk])
                    self.act(sig[:], ps[:], AF.Sigmoid, [pk, "vec"], ["r_sig"], bias=V("rwkv_w0", hp))
                    ps, pk = bank()
                    self.mm(ps[:], a2p[:, hs], lor1[:], True, True, ["r_a2p", "r_lor1"], [pk])
                    self.act(a_[:], ps[:], AF.Sigmoid, [pk, "vec"], ["r_a"], bias=V("rwkv_a0", hp))
                    ps, pk = bank()
                    self.mm(ps[:], g2[:, hs], lor2[:], True, True, ["r_g2", "r_lor2"], [pk])
                    self.cp(g_[:], ps[:], [pk], ["r_g"], eng="act_copy")
                    S.add("dve", (lambda: (lambda e: e.tensor_tensor_scan(out=cs[:], data0=self.smask[:], data1=sig[:], initial=0.0,
                                                                         op0=ALU.mult, op1=ALU.add)))(), ["smask", "r_sig"], ["r_tmp2"])
                    self.tt(KKt[:], cs[:], sig[:], ALU.subtract, ["r_tmp2", "r_sig"], ["r_KKt"], eng="pool")
                    self.act(Eg[:], cs[:], AF.Exp, ["r_tmp2"], ["r_Eg"], scale=-C0)
                    self.act(Eng[:], cs[:], AF.Exp, ["r_tmp2"], ["r_Eng"], scale=C0)
                    self.act(KKt[:], KKt[:], AF.Exp, ["r_KKt"], ["r_KKt"], scale=-C0)
                    self.cp(egl[:], Eg[:].rearrange("p (c s) -> p c s", s=64)[:, :, 63], ["r_Eg"], ["r_egl"], eng="pool")
                    self.ts(kk[:], uk[:], V("rwkv_k_k", hp), None, ALU.mult, None, ["r_uk", "vec"], ["r_kk"])
                    self.act(tmp[:], kk[:], AF.Square, ["r_kk"], ["r_tmp"])
                    ps, pk = bank()
                    self.mm(ps[:], self.blk[:], tmp[:], True, True, ["blk", "r_tmp"], [pk])
                    self.ts(tmp[:], ps[:], 1e-18, None, ALU.max, None, [pk], ["r_tmp"])
                    self.act(tmp[:], tmp[:], AF.Ln, ["r_tmp"], ["r_tmp"])
                    self.act(tmp[:], tmp[:], AF.Exp, ["r_tmp"], ["r_tmp"], scale=-0.5)
                    self.tt(kk[:], kk[:], tmp[:], ALU.mult, ["r_kk", "r_tmp"], ["r_kk"])
                    self.ts(tmp[:], a_[:], V("rwkv_k_a", hp), omk[:, hp:hp + 1], ALU.mult, ALU.add, ["r_a", "vec", "r_omk"], ["r_tmp"])
                    self.tt(kh[:], uk[:], tmp[:], ALU.mult, ["r_uk", "r_tmp"], ["r_kh"])
                    self.tt(tmp[:], ur[:], kh[:], ALU.mult, ["r_ur", "r_kh"], ["r_tmp"])
                    self.ts(tmp[:], tmp[:], V("rwkv_r_k", hp), None, ALU.mult, None, ["r_tmp", "vec"], ["r_tmp"])
                    ps, pk = bank()
                    self.mm(ps[:], self.blk[:], tmp[:], True, True, ["blk", "r_tmp"], [pk])
                    self.tt(bon[:], ps[:], uv, ALU.mult, [pk, "r_uvall"], ["r_bon"])
                    self.tt(KKt[:], KKt[:], kk[:], ALU.mult, ["r_KKt", "r_kk"], ["r_KKt"])
                    self.tt(Kt[:], kh[:], Eng[:], ALU.mult, ["r_kh", "r_Eng"], ["r_Kt"])
                    self.tt(tmp[:], kk[:], a_[:], ALU.mult, ["r_kk", "r_a"], ["r_tmp"], eng="pool")
                    self.tt(Bt[:], tmp[:], Eng[:], ALU.mult, ["r_tmp", "r_Eng"], ["r_Bt"])
                    self.tt(Rt[:], ur[:], Eg[:], ALU.mult, ["r_ur", "r_Eg"], ["r_Rt"])
                    for hh in range(2):
                        r2 = slice(hh * 64, hh * 64 + 64)
                        self.cp(KtP[r2, hh, :], Kt[r2, :], ["r_Kt"], ["r_KtP"], eng="pool" if hh else "dve")
                        self.cp(BtP[r2, hh, :], Bt[r2, :], ["r_Bt"], ["r_BtP"], eng="act_copy" if hh else "pool")
                        self.cp(KKP[r2, hh, :], KKt[r2, :], ["r_KKt"], ["r_KKP"], eng="dve" if hh else "act_copy")
                    self.cp(KKb[:], KKt[:], ["r_KKt"], ["r_KKb"], eng="act_copy")
                    self.cp(Rb[:], Rt[:], ["r_Rt"], ["r_Rb"], eng="pool")
                    self.cp(Bb[:], Bt[:], ["r_Bt"], ["r_Bb"], eng="act_copy")
                    for ch in range(8):
                        cs_ = slice(ch * 64, ch * 64 + 64)
                        self.ts(kdec[:, cs_], Kt[:, cs_], egl[:, ch:ch + 1], None, ALU.mult, None, ["r_Kt", "r_egl"], ["r_kdec"])
                        self.act(bdec[:, cs_], Bt[:, cs_], AF.Copy, ["r_Bt", "r_egl"], ["r_Eng"], scale=egl[:, ch:ch + 1])
                    if RS <= 3:
                        continue
                    PY, pyk = self.PB[7], "pb7"

                    def gen_pre(b, B_):
                        bl = slice(b * 128, (b + 1) * 128)
                        q = B_["q"]
                        K_ = lambda n: "%s_%d" % (n, q)
                        PT, ptk = self.PB[4], "pb4"
                        for q_, src, sk in ((0, uv, "r_uvall"), (1, kdec[:], "r_kdec"), (2, bdec[:], "r_Eng")):
                            S.add("pe", (lambda o, i_: (lambda e: e.transpose(o, i_, self.ident[:])))(PT[:, q_ * 128:(q_ + 1) * 128], src[:, bl]),
                                  [sk, "ident"], [ptk])
                        self.cp(B_["v_tok"][:], PT[:, 0:128], [ptk], [K_("r_vtok")], eng="act_copy")
                        self.cp(B_["vpad"][:].rearrange("p (a b) -> p a b", b=192)[:, :, 0:64], PT[:, 0:128].rearrange("p (a b) -> p a b", b=64),
                                [ptk], [K_("r_vpad")])
                        for cc in range(2):
                            crow = slice(cc * 64, cc * 64 + 64)
                            self.cp(B_["kdec2"][crow, cc, :], PT[crow, 128:256], [ptk], [K_("r_kdec2")], eng="act_copy")
                            self.cp(B_["bdec2"][crow, cc, :], PT[crow, 256:384], [ptk], [K_("r_bdec2")])
                        yield
                        MW = B_["MW"]
                        for hh in range(2):
                            PM, pmk = self.PB[5 + hh], "pb%d" % (5 + hh)
                            self.mm(PM[:, 0:128], KtP[:, hh, bl], KKb[:, bl], True, True, ["r_KtP", "r_KKb"], [pmk])
                            self.mm(PM[:, 128:256], KtP[:, hh, bl], Rb[:, bl], True, True, ["r_KtP", "r_Rb"], [pmk])
                            self.mm(PM[:, 256:384], BtP[:, hh, bl], KKb[:, bl], True, True, ["r_BtP", "r_KKb"], [pmk])
                            self.mm(PM[:, 384:512], BtP[:, hh, bl], Rb[:, bl], True, True, ["r_BtP", "r_Rb"], [pmk])
                            self.tt(MW[:, hh, :, :].rearrange("p a b -> p (a b)"), PM[:], mask4[:].rearrange("p a b -> p (a b)"), ALU.mult,
                                    [pmk, "r_mask4"], [K_("r_MW")])
                        PA, pak = bank()
                        for hh in range(2):
                            self.mm(PA[:, hh * 128:(hh + 1) * 128], KKP[:, hh, bl], Bb[:, bl], True, True, ["r_KKP", "r_Bb"], [pak])
                        Ab_, ATb_, Tb_, IAT_ = B_["A"], B_["AT"], B_["T"], B_["IAT"]
                        self.tt(ATb_[0][:].rearrange("p a b -> p (a b)"), PA[:, 0:256], self.mfut[:, 0:2, :].rearrange("p a b -> p (a b)"), ALU.mult,
                                [pak, "mfut"], [K_("r_AT0")])
                        self.tt(Tb_[0][:], id2[:], MW[:, :, 2, :], ALU.subtract, ["r_id2", K_("r_MW")], [K_("r_T0")])
                        self.cp(Ab_[0][:], MW[:, :, 2, :], [K_("r_MW")], [K_("r_A0")], eng="pool")
                        yield

                        def sq(cur, need_a):
                            A, AT = Ab_[cur], ATb_[cur]
                            ak, atk = K_("r_A%d" % cur), K_("r_AT%d" % cur)
                            P1, p1k = bank()
                            P2, p2k = bank()
                            for hh in range(2):
                                hs2 = slice(hh * 128, (hh + 1) * 128)
                                if need_a:
                                    self.mm(P1[:, hs2], AT[:, hh, :], A[:, hh, :], True, True, [ak, atk], [p1k])
                                self.mm(P2[:, hs2], A[:, hh, :], AT[:, hh, :], True, True, [ak, atk], [p2k])
                            return P1, p1k, P2, p2k
                        cur = 0
                        pend = sq(0, True)
                        yield
                        for it in range(5):
                            nx = 1 - cur
                            P1, p1k, P2, p2k = pend
                            if it < 4:
                                self.cp(Ab_[nx][:].rearrange("p a b -> p (a b)"), P1[:, 0:256], [p1k], [K_("r_A%d" % nx)], eng="act_copy")
                                self.cp(ATb_[nx][:].rearrange("p a b -> p (a b)"), P2[:, 0:256], [p2k], [K_("r_AT%d" % nx)], eng="act_copy")
                            self.tt(IAT_[:].rearrange("p a b -> p (a b)"), P2[:, 0:256], id2[:].rearrange("p a b -> p (a b)"), ALU.add,
                                    [p2k, "r_id2"], [K_("r_IAT")])
                            if it < 4:
                                pend = sq(nx, it < 3)
                            P3, p3k = bank()
                            for hh in range(2):
                                self.mm(P3[:, hh * 128:(hh + 1) * 128], IAT_[:, hh, :], Tb_[cur][:, hh, :], True, True,
                                        [K_("r_IAT"), K_("r_T%d" % cur)], [p3k])
                            self.cp(Tb_[nx][:].rearrange("p a b -> p (a b)"), P3[:, 0:256], [p3k], [K_("r_T%d" % nx)])
                            cur = nx
                            yield
                        B_["Tcur"] = cur

                    def gen_chain(b, B_):
                        q = B_["q"]
                        K_ = lambda n: "%s_%d" % (n, q)
                        bl = slice(b * 128, (b + 1) * 128)
                        MW = B_["MW"]
                        T, tk = B_["T"][B_["Tcur"]], K_("r_T%d" % B_["Tcur"])
                        v_tok = B_["v_tok"]
                        for cc in range(2):
                            crow = slice(cc * 64, cc * 64 + 64)
                            cl = slice(b * 128 + cc * 64, b * 128 + cc * 64 + 64)
                            ccol = slice(cc * 64, cc * 64 + 64)
                            PX, pxk = bank()
                            self.mm(PX[:, 0:128], KKt[:, bl], rwS[:, hp, :], True, False, ["r_KKt", "rwS"], [pxk], skip=True)
                            for hh in range(2):
                                self.mm(PX[:, hh * 64:(hh + 1) * 64], MW[:, hh, 0, :], v_tok[:, hh * 64:(hh + 1) * 64], False, True,
                                        [K_("r_MW"), K_("r_vtok")], [pxk], skip=True)
                            self.ts(negX[crow, :], PX[crow, 0:128], -1.0, None, ALU.mult, None, [pxk], ["r_negX"])
                            yield
                            PS_, psk = bank()
                            for hh in range(2):
                                self.mm(PS_[:, hh * 64:(hh + 1) * 64], T[:, hh, :], negX[:, hh * 64:(hh + 1) * 64], True, True,
                                        [tk, "r_negX"], [psk])
                            self.cp(SA[crow, :], PS_[crow, 0:128], [psk], ["r_SA"], eng="act_copy")
                            self.cp(SApad[crow, :].rearrange("p (a b) -> p a b", b=192)[:, :, 0:64],
                                    PS_[crow, 0:128].rearrange("p (a b) -> p a b", b=64), [psk], ["r_SApad"])
                            yield
                            PU, puk = bank()
                            self.mm(PU[:, 0:128], B_["bdec2"][:, cc, :], SA[:], True, False, [K_("r_bdec2"), "r_SA"], [puk])
                            self.mm(PU[:, 0:128], B_["kdec2"][:, cc, :], v_tok[:], False, True, [K_("r_kdec2"), K_("r_vtok")], [puk])
                            self.mm(PY[:, cl], rwS[:, hp, :], Rt[:, cl], True, False, ["rwS", "r_Rt"], [pyk], skip=True)
                            for hh in range(2):
                                self.mm(PY[:, cl], SApad[:, hh * 128:(hh + 1) * 128], MW[:, hh, 3, ccol], False, False,
                                        ["r_SApad", K_("r_MW")], [pyk], skip=True)
                            for hh in range(2):
                                self.mm(PY[:, cl], B_["vpad"][:, hh * 128:(hh + 1) * 128], MW[:, hh, 1, ccol], False, hh == 1,
                                        [K_("r_vpad"), K_("r_MW")], [pyk], skip=True)
                            for hh in range(2):
                                r2 = slice(hh * 64, hh * 64 + 64)
                                self.stt(rwS[r2, hp, hh * 64:(hh + 1) * 64], rwS[r2, hp, hh * 64:(hh + 1) * 64], egl[r2, b * 2 + cc:b * 2 + cc + 1],
                                         PU[r2, hh * 64:(hh + 1) * 64], ALU.mult, ALU.add, ["rwS", "r_egl", puk], ["rwS"])
                            yield

                    def drive(gens):
                        gens = list(gens)
                        while gens:
                            for g in list(gens):
                                try:
                                    next(g)
                                except StopIteration:
                                    gens.remove(g)

                    drive([gen_pre(0, BS[0])])
                    for b in range(4):
                        gl_ = [gen_chain(b, BS[b % 2])]
                        if b + 1 < 4:
                            gl_.append(gen_pre(b + 1, BS[(b + 1) % 2]))
                        drive(gl_)
                    if RS <= 6:
                        continue
                    self.cp(Ysb[:], PY[:], [pyk], ["r_kdec"], eng="act_copy")
                    ps, pk = bank()
                    self.mm(ps[:], self.blk64[:], Ysb[:], True, True, ["blk64", "r_kdec"], [pk])
                    self.tt(Ysb[:], Ysb[:], ps[:], ALU.subtract, ["r_kdec", pk], ["r_kdec"])
                    self.act(tmp[:], Ysb[:], AF.Square, ["r_kdec"], ["r_tmp"])
                    ps, pk = bank()
                    self.mm(ps[:], self.blk64[:], tmp[:], True, True, ["blk64", "r_tmp"], [pk])
                    self.act(tmp[:], ps[:], AF.Ln, [pk, "c_eps"], ["r_tmp"], bias=self.c_eps[:, 1:2])
                    self.act(tmp[:], tmp[:], AF.Exp, ["r_tmp"], ["r_tmp"], scale=-0.5)
                    self.tt(Ysb[:], Ysb[:], tmp[:], ALU.mult, ["r_kdec", "r_tmp"], ["r_kdec"])
                    self.ts(Ysb[:], Ysb[:], V("rwkv_ln_w", hp), V("rwkv_ln_b", hp), ALU.mult, ALU.add, ["r_kdec", "vec"], ["r_kdec"])
                    self.tt(Ysb[:], Ysb[:], bon[:], ALU.add, ["r_kdec", "r_bon"], ["r_kdec"])
                    self.tt(self.yT[:, hp, tsl], Ysb[:], g_[:], ALU.mult, ["r_kdec", "r_g"], ["yT"])

    def mix_merge(self, slot):
        TS, NT = self.TS, self.NT
        with contextlib.ExitStack() as st:
            mg = self.sb(st, "mg", [128, 8, TS], BF16)
            wbr = [self.sb(st, "mg_wbr%d" % i, [128, 3, 4, 128], BF16) for i in range(2)]
            wo = [self.sb(st, "mg_wo%d" % i, [128, 8, 128], BF16) for i in range(2)]
            wib_save, wcnt_save = self.wib, self.wcnt
            self.wib = list(self.wib) + [self.sb(st, "m_wi%d" % (4 + i), [128, 8, 128], BF16) for i in range(5)]
            self.wcnt = [0]
            gsb = self.sb(st, "mg_g", [128, 512], F32)
            acc = self.sb(st, "mg_acc", [128, 512], F32)
            tmp = self.sb(st, "mg_tmp", [128, 512], F32)
            it = 0
            for dc in range(8):
                wb_, wbk = wbr[dc % 2], "mg_wbr%d" % (dc % 2)
                self.dma(wb_[:].rearrange("p a b c -> p (a b c)"), self.w_br[slot, dc], [], [wbk], eng="pool")
                wg = []
                for j in range(3):
                    wg.append(self.win_tile(slot, 30 + j * 8 + dc, self.wib, self.wcnt))
                for nt in range(NT):
                    tsl = slice(nt * 512, (nt + 1) * 512)
                    for j in range(3):
                        k = it % 2
                        it += 1
                        PG, pgk = self.PB[k], "pb%d" % k
                        PP, ppk = self.PB[2 + k], "pb%d" % (2 + k)
                        self.proj(PG[:], pgk, wg[j][0], wg[j][1], tsl)
                        for c in range(4):
                            self.mm(PP[:], wb_[:, j, c, :], self.yT[:, j * 4 + c, tsl], c == 0, c == 3, [wbk, "yT"], [ppk])
                        self.act(gsb[:], PG[:], AF.Sigmoid, [pgk], ["mg_g"])
                        if j == 0:
                            self.tt(acc[:], PP[:], gsb[:], ALU.mult, [ppk, "mg_g"], ["mg_acc"])
                        elif j == 1:
                            self.tt(tmp[:], PP[:], gsb[:], ALU.mult, [ppk, "mg_g"], ["mg_tmp"])
                            self.tt(acc[:], acc[:], tmp[:], ALU.add, ["mg_acc", "mg_tmp"], ["mg_acc"])
                        else:
                            self.tt(tmp[:], PP[:], gsb[:], ALU.mult, [ppk, "mg_g"], ["mg_tmp"])
                            self.tt(mg[:, dc, tsl], acc[:], tmp[:], ALU.add, ["mg_acc", "mg_tmp"], ["mg"])
            for dc2 in range(8):
                w, wk = wo[dc2 % 2], "mg_wo%d" % (dc2 % 2)
                self.dma(w[:].rearrange("p a b -> p (a b)"), self.w_o[slot, dc2], [], [wk], eng="pool")
                for nt in range(NT):
                    tsl = slice(nt * 512, (nt + 1) * 512)
                    k = 4 + it % 2
                    it += 1
                    PO, pok = self.PB[k], "pb%d" % k
                    for dc in range(8):
                        self.mm(PO[:], w[:, dc, :], mg[:, dc, tsl], dc == 0, dc == 7, [wk, "mg"], [pok])
                    self.tt(self.xT[:, dc2, tsl], self.xT[:, dc2, tsl], PO[:], ALU.add, ["xT", pok], ["xT"])
            self.wib, self.wcnt = wib_save, wcnt_save

def _fm(v, ncol):
    return np.ascontiguousarray(np.asarray(v, np.float32).reshape(ncol, 128).T)


def _wtile(w, cols):
    K, C = w.shape
    nk = K // 128
    nt = C // cols
    a = np.asarray(w, np.float32).reshape(nk, 128, nt, cols).transpose(2, 1, 0, 3)
    return np.ascontiguousarray(a).reshape(nt, 128, nk * cols)


def prep_layer(inp, l, final_norm):
    f32 = np.float32
    if l is None:
        z = lambda *s: np.zeros(s, f32)
        vec = z(128, NV)
        vec[:, VOFF["rwkv_v0"]:VOFF["rwkv_v0"] + 4] = -1e4
        vec[:, VOFF["final_norm"]:VOFF["final_norm"] + 8] = _fm(final_norm, 8)
        return dict(vecs=vec, w_f1i=z(NFF, 128, 2048), w_f1o=z(8, 128, NFF * 128), w_f2i=z(NFF, 128, 2048),
                    w_f2o=z(8, 128, NFF * 128), w_in=z(54, 128, 1024), w_al=z(128, 128), w_kv=z(128, 8192),
                    w_br=z(8, 128, 1536), w_o=z(8, 128, 1024), w_w2=z(64, 512), w_a2=z(64, 512), w_g2=z(128, 512),
                    w_v1=z(128, 128), w_v2=z(32, 512), w_aup=z(16, 256))
    vec = np.zeros((128, NV), f32)

    def put(name, v, n):
        vec[:, VOFF[name]:VOFF[name] + n] = _fm(v, n)
    put("ffn1_norm", inp["ffn1_norm"][l], 8)
    put("mix_norm", inp["mix_norm"][l], 8)
    put("mem_norm", inp["mem_norm"][l], 8)
    put("ffn2_norm", inp["ffn2_norm"][l], 8)
    put("final_norm", final_norm, 8)
    put("rwkv_mu", inp["rwkv_mu"][l], 14)
    for n in ("rwkv_w0", "rwkv_a0", "rwkv_k_k", "rwkv_k_a", "rwkv_ln_w", "rwkv_ln_b"):
        put(n, inp[n][l], 4)
    put("rwkv_r_k", np.asarray(inp["rwkv_r_k"][l]).reshape(-1), 4)
    if l == 0:
        vec[:, VOFF["rwkv_v0"]:VOFF["rwkv_v0"] + 4] = -1e4
        v1 = np.zeros((512, 32), f32)
        v2 = np.zeros((32, 512), f32)
    else:
        put("rwkv_v0", inp["rwkv_v0"][l - 1], 4)
        v1 = np.asarray(inp["rwkv_v1"][l - 1], f32)
        v2 = np.asarray(inp["rwkv_v2"][l - 1], f32)
    for j in range(4):
        put("conv%d" % j, inp["gla_conv"][l][j], 8)
    put("gla_a_bias", inp["gla_a_bias"][l], 2)
    put("gla_norm", inp["gla_norm"][l], 4)

    def ffn_in(w):
        w = np.asarray(w, f32)
        g = _wtile(w[:, :DFF], 128).reshape(NFF, 128, 8, 128)
        u = _wtile(w[:, DFF:], 128).reshape(NFF, 128, 8, 128)
        return np.ascontiguousarray(np.concatenate([g, u], axis=3)).reshape(NFF, 128, 2048)

    def ffn_out(w):
        return _wtile(np.asarray(w, f32), 128)

    win = np.asarray(inp["w_in"][l], f32)
    full = np.concatenate([win[:, 0:2816], win[:, 2832:]], axis=1)
    wbr = np.asarray(inp["w_branch"][l], f32)
    br = np.stack([_wtile(wbr[j], 128).reshape(8, 128, 4 * 128) for j in range(3)], axis=2)
    return dict(
        vecs=vec,
        w_f1i=ffn_in(inp["ffn1_w_in"][l]), w_f1o=ffn_out(inp["ffn1_w_out"][l]),
        w_f2i=ffn_in(inp["ffn2_w_in"][l]), w_f2o=ffn_out(inp["ffn2_w_out"][l]),
        w_in=_wtile(full, 128), w_al=_wtile(win[:, 2816:2832], 16)[0],
        w_kv=_wtile(np.asarray(inp["xa_w_kv"][l], f32), 1024)[0],
        w_br=np.ascontiguousarray(br).reshape(8, 128, 1536),
        w_o=_wtile(np.asarray(inp["w_out"][l], f32), 128),
        w_w2=np.asarray(inp["rwkv_w2"][l], f32), w_a2=np.asarray(inp["rwkv_a2"][l], f32),
        w_g2=np.asarray(inp["rwkv_g2"][l], f32), w_v1=_wtile(v1, 32)[0], w_v2=v2,
        w_aup=np.asarray(inp["gla_a_up"][l], f32))


def x_to_fm(x, n_sub, TS):
    a = np.asarray(x, np.float32).reshape(n_sub, TS, 8, 128).transpose(0, 3, 2, 1)
    return np.ascontiguousarray(a)


def fm_to_x(a):
    n_sub, _, _, TS = a.shape
    return np.ascontiguousarray(a.transpose(0, 3, 2, 1)).reshape(n_sub * TS, 1024)


_PROG = {}


def _get_prog(key, **kw):
    if key not in _PROG:
        b = B(**kw)
        b.build()
        _PROG[key] = b
    return _PROG[key]


def _in_map(L, x_fm, vf, st_in, fl, memT):
    m = {k: v[None] for k, v in L.items()}
    m.update(x_in=x_fm, vf_in=vf, st_in=st_in, flags=fl, memT=memT)
    return m


def kernel(**inp):
    inp = {k: np.asarray(v) for k, v in inp.items()}
    x, mem = inp["x"], inp["mem"]
    Bn, SEQ, _ = x.shape
    NSUB, TS, NSL = 2, 1024, 5
    HALF = NSUB * TS
    assert SEQ == 2 * HALF and Bn == 4
    return _run_fused(inp, x, mem, Bn, NSUB, TS, NSL)


def _run_fused(inp, x, mem, Bn, NSUB, TS, NSL, nlayers=4):
    HALF = NSUB * TS
    fn = inp["final_norm"]
    layers = [prep_layer(inp, l, fn) for l in range(nlayers)]
    dummy = prep_layer(inp, None, fn)
    ncores = 2 * Bn
    prog = _get_prog(("fused", ncores, NSUB, TS, NSL), n_slots=NSL, n_sub=NSUB, TS=TS, norm_after=[NSL - 2, NSL - 1], n_cores=ncores)
    maps = []
    for c in range(ncores):
        b, half = c // 2, c % 2
        ls = [(k if half == 0 else k - 1) for k in range(NSL)]
        Ls = [layers[l] if 0 <= l < nlayers else dummy for l in ls]
        m = {key: np.stack([L[key] for L in Ls]) for key in dummy}
        fl = np.zeros((128, 2 * NSL), np.float32)
        for k, l in enumerate(ls):
            fl[:, 2 * k] = 1.0 if l == 0 else 0.0
            fl[:, 2 * k + 1] = 1.0 if half == 1 else 0.0
        m.update(x_in=x_to_fm(x[b, half * HALF:(half + 1) * HALF], NSUB, TS),
                 vf_in=np.zeros((NSUB, 128, 4, TS), np.float32), st_in=np.zeros((128, NST), np.float32), flags=fl,
                 memT=np.ascontiguousarray(mem[b].reshape(NMEM, 8, 128).transpose(2, 1, 0)).astype(np.float32))
        maps.append(m)
    res = run_bass_kernel_spmd(prog.nc, maps, core_ids=list(range(ncores))).results
    out = np.zeros((Bn, 2 * HALF, D), np.float32)
    for c in range(ncores):
        b, half = c // 2, c % 2
        y = np.asarray(res[c]["y_out"])[half]
        out[b, half * HALF:(half + 1) * HALF] = fm_to_x(y)
    return out


def kernel_unfused(**inp):
    inp = {k: np.asarray(v) for k, v in inp.items()}
    x, mem = inp["x"], inp["mem"]
    Bn, SEQ, _ = x.shape
    NSUB, TS = 2, 1024
    HALF = NSUB * TS
    assert SEQ == 2 * HALF and Bn == 4
    fn = inp["final_norm"]
    layers = [prep_layer(inp, l, fn) for l in range(4)]
    dummy = prep_layer(inp, None, fn)
    prog = _get_prog("slot1", n_slots=1, n_sub=NSUB, TS=TS, norm_after=[0])
    ncores = 8
    memT = [np.ascontiguousarray(mem[b].reshape(NMEM, 8, 128).transpose(2, 1, 0)).astype(np.float32) for b in range(Bn)]
    xs = [x_to_fm(x[c // 2, (c % 2) * HALF:(c % 2 + 1) * HALF], NSUB, TS) for c in range(ncores)]
    vfs = [np.zeros((NSUB, 128, 4, TS), np.float32) for _ in range(ncores)]
    st_prev = [np.zeros((128, NST), np.float32) for _ in range(Bn)]
    ys = [None] * ncores
    for k in range(5):
        maps = []
        for c in range(ncores):
            b, half = c // 2, c % 2
            l = k if half == 0 else k - 1
            L = layers[l] if 0 <= l <= 3 else dummy
            fl = np.zeros((128, 2), np.float32)
            fl[:, 0] = 1.0 if l == 0 else 0.0
            fl[:, 1] = 1.0 if half == 1 else 0.0
            maps.append(_in_map(L, xs[c], vfs[c], st_prev[b], fl, memT[b]))
        res = run_bass_kernel_spmd(prog.nc, maps, core_ids=list(range(ncores))).results
        for c in range(ncores):
            b, half = c // 2, c % 2
            l = k if half == 0 else k - 1
            xs[c] = np.asarray(res[c]["x_out"])
            vfs[c] = np.asarray(res[c]["vf_out"])
            if l == 3:
                ys[c] = np.asarray(res[c]["y_out"])[0]
        for c in range(0, ncores, 2):
            st_prev[c // 2] = np.asarray(res[c]["st_out"])
    out = np.zeros((Bn, SEQ, D), np.float32)
    for c in range(ncores):
        out[c // 2, (c % 2) * HALF:(c % 2 + 1) * HALF] = fm_to_x(ys[c])
    return out
```
